# Optimizing a Trainium2 kernel written in Bass

```python
import math
import jax, jax.numpy as jnp
from jax import lax
import numpy as np

D_MODEL = 1024
BATCH = 4
SEQ = 8192
DEPTH = 2

N_EVEN = (DEPTH + 1) // 2
N_ODD = DEPTH // 2
D_FF = 2816
FFN_RES = 0.5
NORM_EPS = 1e-6
CHUNK = 64
CONV_W = 4

GLA_HEADS = 4
GLA_DK = 64
GLA_DV = 128
GLA_LORA = 16
GLA_GATE_NORM = 16.0
GLA_QK = GLA_HEADS * GLA_DK
GLA_V = GLA_HEADS * GLA_DV
GDN_HEADS = 4
GDN_DK = 128
GDN_DV = 128
GDN_QK = GDN_HEADS * GDN_DK
GDN_V = GDN_HEADS * GDN_DV
GDN_CONV_CH = 2 * GDN_QK + GDN_V
RWKV_HEADS = 8
RWKV_N = 64
RWKV_W = RWKV_HEADS * RWKV_N
RWKV_W_LORA = 64
RWKV_A_LORA = 64
RWKV_G_LORA = 128
RWKV_GN_EPS = 64e-5
RWKV_SHIFT = 3 * RWKV_W + RWKV_W_LORA + RWKV_A_LORA + RWKV_G_LORA
LRU_WIDTH = 512
LRU_BLOCKS = 8
LRU_BW = LRU_WIDTH // LRU_BLOCKS
LRU_C = 8.0

EVEN_SPLITS = (GLA_QK, GLA_QK, GLA_V, GLA_V, GLA_LORA, GDN_CONV_CH, GDN_V, GDN_HEADS, GDN_HEADS)
EVEN_IN = 2 * GLA_QK + 2 * GLA_V + GLA_LORA + GDN_CONV_CH + GDN_V + 2 * GDN_HEADS
EVEN_OUT = GLA_V + GDN_V
RWKV_SPLITS = (RWKV_W, RWKV_W, RWKV_W, RWKV_W_LORA, RWKV_A_LORA, RWKV_G_LORA)
ODD_IN = RWKV_SHIFT + 2 * LRU_WIDTH
ODD_OUT = RWKV_W + LRU_WIDTH

kernel_name = 'hybrid_gla_gdn_rwkv7_rglru_macaron'


def _split(x, sizes):
    out, o = [], 0
    for s in sizes:
        out.append(x[..., o:o + s])
        o += s
    return out


def _rmsnorm(x, w):
    xf = x.astype(jnp.float32)
    y = xf * lax.rsqrt(jnp.mean(xf * xf, -1, keepdims=True) + NORM_EPS) * w
    return y.astype(x.dtype)


def _heads(x, n_heads):
    return x.reshape(x.shape[:-1] + (n_heads, x.shape[-1] // n_heads))


def _l2norm(x):
    xf = x.astype(jnp.float32)
    return xf * lax.rsqrt(jnp.sum(xf * xf, -1, keepdims=True) + NORM_EPS)


def _head_rmsnorm(x, n_heads, w):
    xh = _heads(x.astype(jnp.float32), n_heads)
    xh = xh * lax.rsqrt(jnp.mean(xh * xh, -1, keepdims=True) + NORM_EPS) * w
    return xh.reshape(x.shape).astype(x.dtype)


def _swiglu(x, w_gate, w_up, w_down):
    return (jax.nn.silu(x @ w_gate) * (x @ w_up)) @ w_down


def _causal_dwconv(x, w):
    K, T = w.shape[0], x.shape[1]
    xp = jnp.pad(x, ((0, 0), (K - 1, 0), (0, 0)))
    y = xp[:, 0:T] * w[0]
    for j in range(1, K):
        y = y + xp[:, j:j + T] * w[j]
    return y


def _token_shift(x):
    return jnp.pad(x, ((0, 0), (1, 0), (0, 0)))[:, :-1]


def _to_chunks(x):
    B, T, H, d = x.shape
    return x.reshape(B, T // CHUNK, CHUNK, H, d).transpose(0, 3, 1, 2, 4)


def _from_chunks(o):
    B, H, N, C, d = o.shape
    return o.transpose(0, 2, 3, 1, 4).reshape(B, N * C, H * d)


def _gla_chunked(q, k, v, log_a):
    C = q.shape[-2]
    causal = jnp.tril(jnp.ones((C, C), bool))
    b = jnp.cumsum(log_a, axis=-2)
    q_e = q * jnp.exp(b)
    att = jnp.where(causal, jnp.einsum('bhncd,bhnsd->bhncs', q_e, k * jnp.exp(-b)), 0.0)
    o_intra = jnp.einsum('bhncs,bhnsv->bhncv', att, v)
    b_last = b[..., -1:, :]
    k_dec = k * jnp.exp(b_last - b)
    a_last = jnp.exp(b_last[..., 0, :])

    def step(S, xs):
        qe, kd, vv, al = xs
        o = jnp.einsum('bhcd,bhdv->bhcv', qe, S)
        S = S * al[..., None] + jnp.einsum('bhcd,bhcv->bhdv', kd, vv)
        return S, o

    B, H, _, _, dk = q.shape
    S0 = jnp.zeros((B, H, dk, v.shape[-1]), q.dtype)
    _, o_inter = lax.scan(step, S0, tuple(jnp.moveaxis(t, 2, 0) for t in (q_e, k_dec, v, a_last)))
    return o_intra + jnp.moveaxis(o_inter, 0, 2)


def _gated_delta_chunked(q, k, v, g, beta):
    C, dv = q.shape[-2], v.shape[-1]
    tril = jnp.tril(jnp.ones((C, C), bool))
    strict = jnp.tril(jnp.ones((C, C), bool), -1)
    gc = jnp.cumsum(g, axis=-1)
    diff = gc[..., :, None] - gc[..., None, :]
    decay = jnp.where(tril, jnp.exp(jnp.where(tril, diff, 0.0)), 0.0)
    kb = k * beta[..., None]
    L = jnp.where(strict, jnp.einsum('bhncd,bhnsd->bhncs', kb, k) * decay, 0.0)
    rhs = jnp.concatenate([v * beta[..., None], kb * jnp.exp(gc)[..., None]], -1)
    sol = lax.linalg.triangular_solve(L + jnp.eye(C, dtype=L.dtype), rhs,
                                      left_side=True, lower=True, unit_diagonal=True)
    u, w = sol[..., :dv], sol[..., dv:]
    att = jnp.where(tril, jnp.einsum('bhncd,bhnsd->bhncs', q, k) * decay, 0.0)
    q_e = q * jnp.exp(gc)[..., None]
    k_dec = k * jnp.exp(gc[..., -1:] - gc)[..., None]
    g_last = jnp.exp(gc[..., -1])

    def step(S, xs):
        qe, kd, uu, ww, aa, gl = xs
        v_new = uu - jnp.einsum('bhcd,bhdv->bhcv', ww, S)
        o = jnp.einsum('bhcd,bhdv->bhcv', qe, S) + jnp.einsum('bhcs,bhsv->bhcv', aa, v_new)
        S = S * gl[..., None, None] + jnp.einsum('bhcd,bhcv->bhdv', kd, v_new)
        return S, o

    B, H, _, _, dk = q.shape
    S0 = jnp.zeros((B, H, dk, dv), q.dtype)
    _, o = lax.scan(step, S0, tuple(jnp.moveaxis(t, 2, 0) for t in (q_e, k_dec, u, w, att, g_last)))
    return jnp.moveaxis(o, 0, 2)


def _rwkv7_scan(r, w, k, v, kk, a):
    B, T, H, N = r.shape

    def step(S, xs):
        rt, wt, kt, vt, kkt, at = xs
        sa = jnp.einsum('bhij,bhj->bhi', S, -kkt)
        S = S * wt[:, :, None, :] + sa[..., None] * (kkt * at)[:, :, None, :] + vt[..., None] * kt[:, :, None, :]
        return S, jnp.einsum('bhij,bhj->bhi', S, rt)

    S0 = jnp.zeros((B, H, N, N), r.dtype)
    _, y = lax.scan(step, S0, tuple(jnp.moveaxis(t, 1, 0) for t in (r, w, k, v, kk, a)))
    return jnp.moveaxis(y, 0, 1)


def _linear_scan(a, b):
    def combine(c1, c2):
        a1, b1 = c1
        a2, b2 = c2
        return a1 * a2, a2 * b1 + b2
    _, h = lax.associative_scan(combine, (a, b), axis=1)
    return h


def _even_mixer(h, w_in, w_out, gla_lora_w2, gla_lora_b, gla_norm, gdn_conv, gdn_a_log, gdn_dt_bias, gdn_norm):
    f32 = jnp.float32
    p = h @ w_in
    gq, gk, gv, gg, glr, dqkv, dz, da, db = _split(p, EVEN_SPLITS)
    log_a = jax.nn.log_sigmoid((glr @ gla_lora_w2 + gla_lora_b).astype(f32)) / GLA_GATE_NORM
    o_gla = _gla_chunked(_to_chunks(_heads(gq.astype(f32) * GLA_DK ** -0.5, GLA_HEADS)),
                         _to_chunks(_heads(gk.astype(f32), GLA_HEADS)),
                         _to_chunks(_heads(gv.astype(f32), GLA_HEADS)),
                         _to_chunks(_heads(log_a, GLA_HEADS)))
    y_gla = _head_rmsnorm(_from_chunks(o_gla).astype(h.dtype), GLA_HEADS, gla_norm) * jax.nn.silu(gg)
    c = jax.nn.silu(_causal_dwconv(dqkv, gdn_conv))
    cq, ck, cv = _split(c, (GDN_QK, GDN_QK, GDN_V))
    q = _l2norm(_heads(cq, GDN_HEADS)) * GDN_DK ** -0.5
    k = _l2norm(_heads(ck, GDN_HEADS))
    v = _heads(cv.astype(f32), GDN_HEADS)
    beta = jax.nn.sigmoid(db.astype(f32))
    g = -jnp.exp(gdn_a_log.astype(f32)) * jax.nn.softplus((da + gdn_dt_bias).astype(f32))
    o_gdn = _gated_delta_chunked(_to_chunks(q), _to_chunks(k), _to_chunks(v),
                                 _to_chunks(g[..., None])[..., 0], _to_chunks(beta[..., None])[..., 0])
    y_gdn = _head_rmsnorm(_from_chunks(o_gdn).astype(h.dtype), GDN_HEADS, gdn_norm) * jax.nn.silu(dz)
    return jnp.concatenate([y_gla, y_gdn], -1) @ w_out


def _odd_mixer(h, w_in, w_out, rwkv_mu, rwkv_w0, rwkv_w2, rwkv_a0, rwkv_a2, rwkv_g2, rwkv_k_k, rwkv_k_a,
               rwkv_r_k, rwkv_ln_w, rwkv_ln_b, lru_conv_w, lru_conv_b, lru_wa, lru_ba, lru_wx, lru_bx, lru_lambda):
    f32 = jnp.float32
    B, T, _ = h.shape
    p = h @ w_in
    ps, lx, ly = _split(p, (RWKV_SHIFT, LRU_WIDTH, LRU_WIDTH))
    ps = ps + (_token_shift(ps) - ps) * rwkv_mu
    r, k, v, wl, al, gl = _split(ps, RWKV_SPLITS)
    w = -jax.nn.softplus(-(rwkv_w0 + jnp.tanh(wl) @ rwkv_w2).astype(f32)) - 0.5
    decay = jnp.exp(-jnp.exp(w))
    a = jax.nn.sigmoid(rwkv_a0 + al @ rwkv_a2)
    g = jax.nn.sigmoid(gl) @ rwkv_g2
    kk = _l2norm(_heads(k * rwkv_k_k, RWKV_HEADS))
    k = k * (1.0 + (a - 1.0) * rwkv_k_a)
    rh, kh, vh = _heads(r, RWKV_HEADS), _heads(k, RWKV_HEADS), _heads(v, RWKV_HEADS)
    y = _rwkv7_scan(rh.astype(f32), _heads(decay, RWKV_HEADS), kh.astype(f32), vh.astype(f32),
                    kk, _heads(a, RWKV_HEADS).astype(f32))
    mu = jnp.mean(y, -1, keepdims=True)
    var = jnp.mean(jnp.square(y - mu), -1, keepdims=True)
    y = ((y - mu) * lax.rsqrt(var + RWKV_GN_EPS)).reshape(B, T, RWKV_W) * rwkv_ln_w + rwkv_ln_b
    bonus = (jnp.sum(rh * kh * rwkv_r_k, -1, keepdims=True) * vh).reshape(B, T, RWKV_W)
    y_rwkv = ((y.astype(h.dtype) + bonus) * g)
    xb = _causal_dwconv(lx, lru_conv_w) + lru_conv_b
    xblk = _heads(xb, LRU_BLOCKS)
    gate_r = jax.nn.sigmoid(jnp.einsum('btki,kij->btkj', xblk, lru_wa).reshape(B, T, LRU_WIDTH) + lru_ba).astype(f32)
    gate_i = jax.nn.sigmoid(jnp.einsum('btki,kij->btkj', xblk, lru_wx).reshape(B, T, LRU_WIDTH) + lru_bx).astype(f32)
    log_a = -LRU_C * gate_r * jax.nn.softplus(-lru_lambda.astype(f32))
    mult = jnp.sqrt(jnp.maximum(-jnp.expm1(2.0 * log_a), 0.0))
    hl = _linear_scan(jnp.exp(log_a), mult * gate_i * xb.astype(f32))
    y_lru = hl.astype(h.dtype) * jax.nn.gelu(ly)
    return jnp.concatenate([y_rwkv, y_lru], -1) @ w_out


def setup_inputs(seed: int = 0) -> dict:
    key = jax.random.key(seed)
    ks = jax.random.split(key, 34)
    f32 = jnp.float32

    def nrm(i, shape, scale):
        return jax.random.normal(ks[i], shape, f32) * scale

    def uni(i, shape, lo, hi):
        return jax.random.uniform(ks[i], shape, f32, lo, hi)

    E, O = N_EVEN, N_ODD
    dt = jnp.exp(uni(12, (E, GDN_HEADS), math.log(1e-3), math.log(1e-1)))
    s = uni(33, (O, LRU_WIDTH), 0.9, 0.999) ** (1.0 / LRU_C)
    return {
        'x': nrm(0, (BATCH, SEQ, D_MODEL), 1.0),
        'norm_w': 1.0 + nrm(1, (DEPTH, 6, D_MODEL), 0.05),
        'ffn_w_gate': nrm(2, (DEPTH, 2, D_MODEL, D_FF), D_MODEL ** -0.5),
        'ffn_w_up': nrm(3, (DEPTH, 2, D_MODEL, D_FF), D_MODEL ** -0.5),
        'ffn_w_down': nrm(4, (DEPTH, 2, D_FF, D_MODEL), D_FF ** -0.5),
        'even_w_in': nrm(5, (E, D_MODEL, EVEN_IN), D_MODEL ** -0.5),
        'even_w_out': nrm(6, (E, EVEN_OUT, D_MODEL), EVEN_OUT ** -0.5),
        'gla_lora_w2': nrm(7, (E, GLA_LORA, GLA_QK), GLA_LORA ** -0.5),
        'gla_lora_b': nrm(8, (E, GLA_QK), 0.1),
        'gla_norm': 1.0 + nrm(9, (E, GLA_DV), 0.05),
        'gdn_conv': nrm(10, (E, CONV_W, GDN_CONV_CH), CONV_W ** -0.5),
        'gdn_a_log': jnp.log(uni(11, (E, GDN_HEADS), 1.0, 16.0)),
        'gdn_dt_bias': dt + jnp.log(-jnp.expm1(-dt)),
        'gdn_norm': 1.0 + nrm(13, (E, GDN_DV), 0.05),
        'odd_w_in': nrm(14, (O, D_MODEL, ODD_IN), D_MODEL ** -0.5),
        'odd_w_out': nrm(15, (O, ODD_OUT, D_MODEL), ODD_OUT ** -0.5),
        'rwkv_mu': uni(16, (O, RWKV_SHIFT), 0.0, 1.0),
        'rwkv_w0': -6.5 + 5.0 * jnp.linspace(0.0, 1.0, RWKV_W, dtype=f32) ** 0.85 + nrm(17, (O, RWKV_W), 0.05),
        'rwkv_w2': nrm(18, (O, RWKV_W_LORA, RWKV_W), RWKV_W_LORA ** -0.5),
        'rwkv_a0': nrm(19, (O, RWKV_W), 0.1),
        'rwkv_a2': nrm(20, (O, RWKV_A_LORA, RWKV_W), RWKV_A_LORA ** -0.5),
        'rwkv_g2': nrm(21, (O, RWKV_G_LORA, RWKV_W), RWKV_G_LORA ** -0.5),
        'rwkv_k_k': 0.85 + nrm(22, (O, RWKV_W), 0.05),
        'rwkv_k_a': 1.0 + nrm(23, (O, RWKV_W), 0.05),
        'rwkv_r_k': nrm(24, (O, RWKV_HEADS, RWKV_N), 0.1),
        'rwkv_ln_w': 1.0 + nrm(25, (O, RWKV_W), 0.05),
        'rwkv_ln_b': nrm(26, (O, RWKV_W), 0.02),
        'lru_conv_w': nrm(27, (O, CONV_W, LRU_WIDTH), CONV_W ** -0.5),
        'lru_conv_b': nrm(28, (O, LRU_WIDTH), 0.02),
        'lru_wa': nrm(29, (O, LRU_BLOCKS, LRU_BW, LRU_BW), LRU_BW ** -0.5),
        'lru_ba': nrm(30, (O, LRU_WIDTH), 0.02),
        'lru_wx': nrm(31, (O, LRU_BLOCKS, LRU_BW, LRU_BW), LRU_BW ** -0.5),
        'lru_bx': nrm(32, (O, LRU_WIDTH), 0.02),
        'lru_lambda': jnp.log(s) - jnp.log1p(-s),
    }


def reference(x, norm_w, ffn_w_gate, ffn_w_up, ffn_w_down, even_w_in, even_w_out, gla_lora_w2, gla_lora_b,
              gla_norm, gdn_conv, gdn_a_log, gdn_dt_bias, gdn_norm, odd_w_in, odd_w_out, rwkv_mu, rwkv_w0,
              rwkv_w2, rwkv_a0, rwkv_a2, rwkv_g2, rwkv_k_k, rwkv_k_a, rwkv_r_k, rwkv_ln_w, rwkv_ln_b,
              lru_conv_w, lru_conv_b, lru_wa, lru_ba, lru_wx, lru_bx, lru_lambda):
    for i in range(DEPTH):
        j = i // 2
        hh = _rmsnorm(x, norm_w[i, 0])
        x = x + FFN_RES * _rmsnorm(_swiglu(hh, ffn_w_gate[i, 0], ffn_w_up[i, 0], ffn_w_down[i, 0]), norm_w[i, 1])
        hh = _rmsnorm(x, norm_w[i, 2])
        if i % 2 == 0:
            m = _even_mixer(hh, even_w_in[j], even_w_out[j], gla_lora_w2[j], gla_lora_b[j], gla_norm[j],
                            gdn_conv[j], gdn_a_log[j], gdn_dt_bias[j], gdn_norm[j])
        else:
            m = _odd_mixer(hh, odd_w_in[j], odd_w_out[j], rwkv_mu[j], rwkv_w0[j], rwkv_w2[j], rwkv_a0[j],
                           rwkv_a2[j], rwkv_g2[j], rwkv_k_k[j], rwkv_k_a[j], rwkv_r_k[j], rwkv_ln_w[j],
                           rwkv_ln_b[j], lru_conv_w[j], lru_conv_b[j], lru_wa[j], lru_ba[j], lru_wx[j],
                           lru_bx[j], lru_lambda[j])
        x = x + _rmsnorm(m, norm_w[i, 3])
        hh = _rmsnorm(x, norm_w[i, 4])
        x = x + FFN_RES * _rmsnorm(_swiglu(hh, ffn_w_gate[i, 1], ffn_w_up[i, 1], ffn_w_down[i, 1]), norm_w[i, 5])
    return x
```

```python
import numpy as np
import concourse.bass as bass
import concourse.mybir as mybir
from concourse.bass_utils import run_bass_kernel_spmd

F32 = mybir.dt.float32
BF16 = mybir.dt.bfloat16
AF = mybir.ActivationFunctionType
ALU = mybir.AluOpType
AX = mybir.AxisListType

EPOCH = 20000
N_EPOCH = 12


class Buf:
    __slots__ = ("name", "w", "r", "excl")

    def __init__(self, name="", excl=False):
        self.name = name
        self.excl = excl
        self.w = None
        self.r = []


class Prog:
    ENGS = ("pe", "dve", "act", "pool", "sp")

    def __init__(self, nc, same_engine_sync=True):
        self.nc = nc
        self.ops = {e: [] for e in self.ENGS}
        self.cnt = {e: 0 for e in self.ENGS}
        self.waited = {}
        self.same_engine_sync = same_engine_sync
        self.dma_cnt = {}
        self.sems = {}
        self._ctx = []
        self.n_wait = 0
        self.barrier_streams = set(["c0"])
        self.slow_map = {}
        self.slow_pe = set()
        self.last_drain = None

    def _sem(self, key):
        if key not in self.sems:
            cm = self.nc.semaphore("s_%s_%s" % key if isinstance(key, tuple) else str(key))
            h = cm.__enter__()
            self._ctx.append(cm)
            self.sems[key] = h
        return self.sems[key]

    def _tok(self, eng):
        i = self.cnt[eng]
        self.cnt[eng] = i + 1
        return ((eng, i // EPOCH), (i % EPOCH) + 1)

    def _need(self, eng, tok):
        if tok is None:
            return None
        key, val = tok
        if key[0] == "pe" and eng != "pe":
            tv = (key[1], val)
            if tv in self.slow_map:
                key, val = self.slow_map[tv]
            elif tv in self.slow_pe:
                dtok = self.op("pe", lambda e: e.drain(), (), ())
                for t in list(self.slow_pe):
                    if t <= tv:
                        self.slow_map[t] = dtok
                        self.slow_pe.discard(t)
                key, val = dtok
        if key[0] == "dma":
            val = self.dma_cnt[key]
        if key[0] == eng and (eng == "pe" or not self.same_engine_sync):
            return None
        if self.waited.get((eng, key), 0) >= val:
            return None
        self.waited[(eng, key)] = val
        return (key, val)

    def op(self, eng, fn, reads=(), writes=()):
        waits = []
        for b in reads:
            w = self._need(eng, b.w)
            if w:
                waits.append(w)
            if b.excl:
                for t in b.r:
                    if t[0][0] != eng:
                        w = self._need(eng, t)
                        if w:
                            waits.append(w)
        for b in writes:
            w = self._need(eng, b.w)
            if w:
                waits.append(w)
            for t in b.r:
                w = self._need(eng, t)
                if w:
                    waits.append(w)
        mx = {}
        for k, v in waits:
            mx[k] = max(mx.get(k, 0), v)
        tok = self._tok(eng)
        for b in reads:
            b.r.append(tok)
        for b in writes:
            b.w = tok
            b.r = []
        self._sem(tok[0])
        for k in mx:
            self._sem(k)
        self.n_wait += len(mx)
        self.ops[eng].append((list(mx.items()), fn, tok[0], 1))
        return tok

    def dma(self, eng, out_ap, in_ap, stream, reads=(), writes=(), **kw):
        waits = []
        for b in reads:
            w = self._need(eng, b.w)
            if w:
                waits.append(w)
        for b in writes:
            w = self._need(eng, b.w)
            if w:
                waits.append(w)
            for t in b.r:
                w = self._need(eng, t)
                if w:
                    waits.append(w)
        mx = {}
        for k, v in waits:
            if k == ("dma", stream) and stream in self.barrier_streams:
                continue
            mx[k] = max(mx.get(k, 0), v)
        key = ("dma", stream)
        c = self.dma_cnt.get(key, 0) + 16
        self.dma_cnt[key] = c
        tok = (key, c)
        for b in reads:
            b.r.append(tok)
        for b in writes:
            b.w = tok
            b.r = []
        self._sem(key)
        for k in mx:
            self._sem(k)

        def fn(e, out_ap=out_ap, in_ap=in_ap, kw=kw):
            return e.dma_start(out=out_ap, in_=in_ap, **kw)
        self.ops[eng].append((list(mx.items()), fn, key, 16))
        return tok

    def final_wait(self, eng, toks):
        for tok in toks:
            w = self._need(eng, tok)
            if w:
                self._sem(w[0])
                self.ops[eng].append(([w], None, None, 0))

    def mm(self, out, lhsT, rhs, start=True, stop=True, reads=(), writes=()):
        tok = self.op("pe", lambda e: e.matmul(out, lhsT, rhs, start=start, stop=stop), reads, writes)
        if lhsT.dtype == F32:
            self.slow_pe.add((tok[0][1], tok[1]))
        return tok

    def transpose(self, out, in_, ident, reads=(), writes=()):
        return self.op("pe", lambda e: e.transpose(out, in_, ident), reads, writes)

    def act(self, out, in_, func, reads=(), writes=(), eng="act", **kw):
        return self.op(eng, lambda e: e.activation(out=out, in_=in_, func=func, **kw), reads, writes)

    def tt(self, eng, out, in0, in1, op, reads=(), writes=()):
        return self.op(eng, lambda e: e.tensor_tensor(out=out, in0=in0, in1=in1, op=op), reads, writes)

    def ts(self, eng, out, in0, s1, s2, op0, op1=None, reads=(), writes=(), **kw):
        if op1 is None:
            return self.op(eng, lambda e: e.tensor_scalar(out=out, in0=in0, scalar1=s1, scalar2=s2, op0=op0, **kw), reads, writes)
        return self.op(eng, lambda e: e.tensor_scalar(out=out, in0=in0, scalar1=s1, scalar2=s2, op0=op0, op1=op1, **kw), reads, writes)

    def stt(self, eng, out, in0, scalar, in1, op0, op1, reads=(), writes=()):
        return self.op(eng, lambda e: e.scalar_tensor_tensor(out=out, in0=in0, scalar=scalar, in1=in1, op0=op0, op1=op1), reads, writes)

    def copy(self, eng, out, in_, reads=(), writes=()):
        if eng == "act":
            return self.op(eng, lambda e: e.copy(out=out, in_=in_), reads, writes)
        return self.op(eng, lambda e: e.tensor_copy(out=out, in_=in_), reads, writes)

    def memset(self, eng, ap, val, writes=()):
        return self.op(eng, lambda e: e.memset(ap, val), (), writes)

    def emit(self):
        nc = self.nc
        sems = self.sems
        ops = self.ops
        with nc.Block() as block:
            def run(e, lst):
                for waits, fn, inckey, incv in lst:
                    for k, v in waits:
                        if k[0] == "dma" and k[1] in self.barrier_streams:
                            v = self.dma_cnt[k]
                        e.wait_ge(sems[k], v)
                    if fn is not None:
                        ins = fn(e)
                        ins.then_inc(sems[inckey], incv)

            @block.tensor
            def _(e):
                run(e, ops["pe"])

            @block.vector
            def _(e):
                run(e, ops["dve"])

            @block.scalar
            def _(e):
                run(e, ops["act"])

            @block.gpsimd
            def _(e):
                run(e, ops["pool"])

            @block.sync
            def _(e):
                run(e, ops["sp"])

    def close(self):
        for cm in reversed(self._ctx):
            cm.__exit__(None, None, None)
        self._ctx = []


class Alloc:
    def __init__(self, nc):
        self.nc = nc
        self._ctx = []

    def sb(self, name, shape, dt):
        cm = self.nc.sbuf_tensor(name, list(shape), dt)
        t = cm.__enter__()
        self._ctx.append(cm)
        return t

    def ps(self, name, shape, dt=F32):
        cm = self.nc.psum_tensor(name, list(shape), dt)
        t = cm.__enter__()
        self._ctx.append(cm)
        return t

    def close(self):
        for cm in reversed(self._ctx):
            cm.__exit__(None, None, None)
        self._ctx = []


D = 1024
DFF = 2816
NKC = 8
NM = 22
TT = 512
C = 64
NCH = TT // C
EPS = 1e-6
EVEN_IN = 3608
ODD_IN = 2816


class Tl:
    def __init__(self, A, name, shape, dt, nb=1):
        self.t = A.sb(name, shape, dt)
        self.b = [Buf(name + str(i)) for i in range(nb)]


CONST_COLS = {}


def make_consts():
    cols = []
    off = 0

    def add(name, arr):
        nonlocal off
        arr = np.asarray(arr, np.float32)
        assert arr.shape[0] == 128
        CONST_COLS[name] = (off, arr.shape[1])
        cols.append(arr)
        off += arr.shape[1]

    add("ident", np.eye(128))
    add("ones_d", np.full((128, 128), 1.0 / D))
    add("ones_128m", np.full((128, 128), 1.0 / 128))
    add("ones_1", np.ones((128, 128)))
    bd = np.zeros((128, 128)); bd[:64, :64] = 1; bd[64:, 64:] = 1
    add("bd_1", bd)
    add("bd_64m", bd / 64.0)
    s = np.arange(64)[:, None]; c = np.arange(64)[None, :]
    incl = (s <= c).astype(np.float32)
    strict = (s < c).astype(np.float32)
    add("m_incl", np.tile(np.concatenate([incl, incl], 0), (1, NCH)))
    add("m_strict", np.tile(np.concatenate([strict, strict], 0), (1, NCH)))
    add("m_incl_T", np.tile(np.concatenate([incl.T, incl.T], 0), (1, NCH)))
    add("m_strict_T", np.tile(np.concatenate([strict.T, strict.T], 0), (1, NCH)))
    rst = np.ones((128, TT)); rst[:, ::C] = 0.0
    add("reset", rst)
    add("ident64x", np.tile(np.concatenate([np.eye(64), np.eye(64)], 0), (1, NCH)))
    return np.concatenate(cols, 1)


class Ctx:
    pass


def build(T, L=2, flags=None, dbg=()):
    flags = flags or {}
    NT = T // TT
    nc = bass.Bass("TRN2", target_bir_lowering=False)
    A = Alloc(nc)
    P = Prog(nc, same_engine_sync=True)
    g = Ctx()
    g.nc, g.A, g.P, g.T, g.NT, g.L, g.flags = nc, A, P, T, NT, L, flags

    def din(name, shape):
        return nc.dram_tensor(name, list(shape), F32, kind="ExternalInput").ap()

    consts_np = make_consts()
    NCC = consts_np.shape[1]
    I = {}
    I["x"] = din("x", [T, D])
    I["consts"] = din("consts", [128, NCC])
    I["norm_w"] = din("norm_w", [2, 6, D])
    I["ffn_w_gate"] = din("ffn_w_gate", [2, 2, D, DFF])
    I["ffn_w_up"] = din("ffn_w_up", [2, 2, D, DFF])
    I["ffn_w_down"] = din("ffn_w_down", [2, 2, DFF, D])
    I["even_w_in"] = din("even_w_in", [1, D, EVEN_IN])
    I["even_w_out"] = din("even_w_out", [1, D, D])
    I["gla_lora_w2"] = din("gla_lora_w2", [1, 16, 256])
    I["gla_lora_b"] = din("gla_lora_b", [1, 256])
    I["gla_norm"] = din("gla_norm", [1, 128])
    I["gdn_conv"] = din("gdn_conv", [1, 4, 1536])
    I["gdn_a_log"] = din("gdn_a_log", [1, 4])
    I["gdn_dt_bias"] = din("gdn_dt_bias", [1, 4])
    I["gdn_norm"] = din("gdn_norm", [1, 128])
    I["odd_w_in"] = din("odd_w_in", [1, D, ODD_IN])
    I["odd_w_out"] = din("odd_w_out", [1, D, D])
    I["rwkv_mu"] = din("rwkv_mu", [1, 1792])
    for nm in ["rwkv_w0", "rwkv_a0", "rwkv_k_k", "rwkv_k_a", "rwkv_ln_w", "rwkv_ln_b",
               "lru_conv_b", "lru_ba", "lru_bx", "lru_lambda"]:
        I[nm] = din(nm, [1, 512])
    I["rwkv_w2"] = din("rwkv_w2", [1, 64, 512])
    I["rwkv_a2"] = din("rwkv_a2", [1, 64, 512])
    I["rwkv_g2"] = din("rwkv_g2", [1, 128, 512])
    I["rwkv_r_k"] = din("rwkv_r_k", [1, 8, 64])
    I["lru_conv_w"] = din("lru_conv_w", [1, 4, 512])
    I["lru_wa"] = din("lru_wa", [1, 8, 64, 64])
    I["lru_wx"] = din("lru_wx", [1, 8, 64, 64])
    g.I = I
    g.out = nc.dram_tensor("out", [T, D], F32, kind="ExternalOutput").ap()
    g.dbg = {}
    for nm, shp in dbg:
        g.dbg[nm] = nc.dram_tensor("dbg_" + nm, list(shp), F32, kind="ExternalOutput").ap()
    g.dbg_toks = []

    g.cf = Tl(A, "cf", [128, NCC], F32)
    P.dma("sp", g.cf.t[:], I["consts"], "c0", writes=g.cf.b)
    g.cb = Tl(A, "cb", [128, 768], BF16)
    P.copy("dve", g.cb.t[:], g.cf.t[:, 0:768], reads=g.cf.b, writes=g.cb.b)

    def cF(name, rows=128, c0=0, n=None):
        o, w = CONST_COLS[name]
        n = w if n is None else n
        return g.cf.t[0:rows, o + c0:o + c0 + n]

    def cB(name, rows=128, c0=0, n=None):
        o, w = CONST_COLS[name]
        n = w if n is None else n
        return g.cb.t[0:rows, o + c0:o + c0 + n]
    g.cF, g.cB = cF, cB

    g.ps = [A.ps("ps%d" % i, [128, 512]) for i in range(8)]
    g.psb = [Buf("ps%d" % i, excl=True) for i in range(8)]

    rows = []
    rows.append(("norm_w", I["norm_w"].rearrange("l i (k p) -> (l i k) p", p=128)))
    stage1 = rows
    rows2 = []
    rows2.append(("gla_lora_b", I["gla_lora_b"].rearrange("o (k p) -> (o k) p", p=128)))
    rows2.append(("gla_norm", I["gla_norm"]))
    rows2.append(("gdn_norm", I["gdn_norm"]))
    rows2.append(("gdn_conv", I["gdn_conv"].rearrange("o j (k p) -> (o j k) p", p=128)))
    rows2.append(("rwkv_mu", I["rwkv_mu"].rearrange("o (k p) -> (o k) p", p=128)))
    for nm in ["rwkv_w0", "rwkv_a0", "rwkv_k_k", "rwkv_k_a", "rwkv_ln_w", "rwkv_ln_b",
               "lru_conv_b", "lru_ba", "lru_bx", "lru_lambda"]:
        rows2.append((nm, I[nm].rearrange("o (k p) -> (o k) p", p=128)))
    rows2.append(("rwkv_r_k", I["rwkv_r_k"].rearrange("o (k a) n -> (o k) (a n)", a=2)))
    rows2.append(("lru_conv_w", I["lru_conv_w"].rearrange("o j (k p) -> (o j k) p", p=128)))
    g.col = {}
    for si, rws in enumerate([stage1, rows2]):
        st = Tl(A, "pst%d" % si, [128, 128], F32)
        P.memset("pool", st.t[:], 0.0, writes=st.b)
        r0 = 0
        for nm, ap in rws:
            r = ap.shape[0]
            P.dma("sp", st.t[r0:r0 + r, :], ap, "c0", writes=st.b)
            g.col[nm] = (si, r0, r)
            r0 += r
        assert r0 <= 128, r0
        ct = Tl(A, "pcol%d" % si, [128, 128], F32)
        P.mm(g.ps[7][:, 0:128], st.t[:], cF("ident"), reads=st.b + g.cf.b, writes=[g.psb[7]])
        P.copy("dve", ct.t[:], g.ps[7][:, 0:128], reads=[g.psb[7]], writes=ct.b)
        if si == 0:
            g.colt0 = ct
        else:
            g.colt1 = ct

    rows64 = [("gla_lora_b", I["gla_lora_b"].rearrange("o (k p) -> (o k) p", p=64))]
    for nm in ["rwkv_w0", "rwkv_a0", "rwkv_k_k", "rwkv_k_a", "rwkv_ln_w", "rwkv_ln_b"]:
        rows64.append((nm, I[nm].rearrange("o (k p) -> (o k) p", p=64)))
    rows64.append(("rwkv_r_k", I["rwkv_r_k"].rearrange("o k n -> (o k) n")))
    rows64.append(("rwkv_mu", I["rwkv_mu"].rearrange("o (k p) -> (o k) p", p=64)))
    st = Tl(A, "pst64", [128, 64], F32)
    P.memset("pool", st.t[:], 0.0, writes=st.b)
    g.col64 = {}
    r0 = 0
    for nm, ap in rows64:
        r = ap.shape[0]
        P.dma("sp", st.t[r0:r0 + r, :], ap, "c0", writes=st.b)
        g.col64[nm] = (r0, r)
        r0 += r
    assert r0 <= 128
    g.colt64 = Tl(A, "pcol64", [64, 128], F32)
    P.mm(g.ps[7][0:64, 0:128], st.t[:], cF("ident"), reads=st.b + g.cf.b, writes=[g.psb[7]])
    P.copy("dve", g.colt64.t[:], g.ps[7][0:64, 0:128], reads=[g.psb[7]], writes=g.colt64.b)

    def c64(name, idx=0):
        r0, r = g.col64[name]
        return g.colt64.t[:, r0 + idx:r0 + idx + 1], g.colt64.b
    g.c64 = c64

    def col(name, idx=0):
        si, r0, r = g.col[name]
        ct = g.colt0 if si == 0 else g.colt1
        return ct.t[:, r0 + idx:r0 + idx + 1], ct.b
    g.colf = col

    g.xT = Tl(A, "xT", [128, NKC, TT], F32, nb=NKC)
    g.hh = Tl(A, "hh", [128, NKC, TT], BF16, nb=NKC)
    ARENA = 62 * 1024
    g.arena = A.sb("arena", [128, ARENA // 2], BF16)
    g.ar_off = 0
    g.ar_bufs = []
    g.ar_tok = None
    g.fscr = Tl(A, "fscr", [128, 2], F32)

    def phase():
        old = g.ar_bufs
        g.ar_tok = P.op("dve", lambda e: e.memset(g.fscr.t[:, 0:1], 0.0), reads=(), writes=old + g.fscr.b)
        g.ar_bufs = []
        g.ar_off = 0

    def carve(name, shape, dt, nb=1):
        nbytes = int(np.prod(shape[1:])) * (4 if dt == F32 else 2)
        nbytes = (nbytes + 63) // 64 * 64
        assert g.ar_off + nbytes <= ARENA, (name, g.ar_off, nbytes)
        v = g.arena[0:shape[0], g.ar_off // 2:(g.ar_off + nbytes) // 2]
        g.ar_off += nbytes
        if dt == F32:
            v = v.bitcast(F32)
        n_el = int(np.prod(shape[1:]))
        v = v[:, 0:n_el]
        if len(shape) == 3:
            v = v.rearrange("p (a b) -> p a b", a=shape[1])
        t = Ctx()
        t.t = v
        t.b = [Buf(name + str(i)) for i in range(nb)]
        for b in t.b:
            b.w = g.ar_tok
        g.ar_bufs.extend(t.b)
        return t
    g.phase, g.carve = phase, carve
    g.f = Tl(A, "f", [128, NKC, TT], F32, nb=NKC)
    g.sq = Tl(A, "sq", [128, NKC, TT], BF16, nb=NKC)
    g.rstd = Tl(A, "rstd", [128, TT], F32)
    g.tmp = Tl(A, "tmp", [128, 2, TT], F32, nb=2)
    g.gsb = Tl(A, "gsb", [128, 2, TT], F32, nb=2)

    if flags.get("even", True) and L >= 1:
        setup_even(g)
    if flags.get("odd", True) and L >= 2:
        setup_odd(g)

    SLOT = 4096
    class _V:
        pass
    g.stg = []
    phase()
    _xs = carve("prep_stg", [128, 4, D], F32)
    for tl in (g.f, _xs):
        v = _V(); v.t = tl.t[:].rearrange("p a c -> p (a c)"); v.b = tl.b
        g.stg.append(v)
    g.cst = []
    for tl in (g.hh, g.sq):
        v = _V(); v.t = tl.t[:].rearrange("p a c -> p (a c)"); v.b = tl.b
        g.cst.append(v)
    g.prep_i = 0
    cast_eng = ["dve", "pool", "act"]

    def prep(src3, dst3, dbuf):
        np_, a, c = src3.shape[0], src3.shape[1], src3.shape[2]
        assert a * c <= SLOT
        i = g.prep_i
        g.prep_i += 1
        s = g.stg[i % 2]
        d = g.cst[i % 2]
        sv = s.t[0:np_, 0:a * c].rearrange("p (a c) -> p a c", a=a)
        dv = d.t[0:np_, 0:a * c].rearrange("p (a c) -> p a c", a=a)
        P.dma("sp", sv, src3, "stg%d" % (i % 2), writes=s.b)
        P.copy(cast_eng[i % 3], dv, sv, reads=s.b, writes=d.b)
        P.dma("act", dst3, dv, "cst%d" % (i % 2), reads=d.b, writes=[dbuf])

    def scratch(name, shape):
        return nc.dram_tensor(name, list(shape), BF16, kind="Internal").ap()

    g.W = {}

    def prep_matrix(name, src2d, pieces, p=128):
        K = src2d.shape[0]
        kc = K // p
        lst = []
        for pi, (c0, c1) in enumerate(pieces):
            w = c1 - c0
            sc = scratch("%s_%d" % (name, pi), [p, kc, w])
            bl = []
            src3 = src2d[:, c0:c1].rearrange("(k p) c -> p k c", p=p)
            kstep = max(1, SLOT // w)
            k0 = 0
            while k0 < kc:
                k1 = min(kc, k0 + kstep)
                b = Buf(name)
                bl.append(b)
                prep(src3[:, k0:k1, :], sc[:, k0:k1, :], b)
                k0 = k1
            lst.append((sc, bl))
        g.W[name] = lst

    g.prep_matrix = prep_matrix

    for l in range(L):
        for f in range(2):
            if flags.get("ffn", True):
                gu = [(i * 256, (i + 1) * 256) for i in range(11)]
                prep_matrix("g%d%d" % (l, f), I["ffn_w_gate"][l, f], gu)
                prep_matrix("u%d%d" % (l, f), I["ffn_w_up"][l, f], gu)
                prep_matrix("d%d%d" % (l, f), I["ffn_w_down"][l, f], [(i * 128, (i + 1) * 128) for i in range(8)])

    NR = 4
    RS = 4096
    g.ring = [Tl(A, "ring%d" % i, [128, RS], BF16) for i in range(NR)]
    g.ring_i = 0

    def wload(name, pi):
        sc, b = g.W[name][pi]
        np_, kc, w = sc.shape[0], sc.shape[1], sc.shape[2]
        assert kc * w <= RS
        i = g.ring_i
        g.ring_i += 1
        slot = g.ring[i % NR]
        v = slot.t[0:np_, 0:kc * w].rearrange("p (k c) -> p k c", k=kc)
        P.dma("sp", v, sc, "ring%d" % (i % NR), reads=b, writes=slot.b)
        return v, slot.b
    g.wload = wload

    def rmsnorm(src, l, i, mode, coef=1.0):
        for k in range(NKC):
            P.act(g.sq.t[:, k, :], src.t[:, k, :], AF.Square, reads=[src.b[k]], writes=[g.sq.b[k]])
        for k in range(NKC):
            P.mm(g.ps[6][:], g.cb.t[:, CONST_COLS["ones_d"][0]:CONST_COLS["ones_d"][0] + 128], g.sq.t[:, k, :],
                 start=(k == 0), stop=(k == NKC - 1), reads=[g.sq.b[k]] + g.cb.b, writes=[g.psb[6]])
        P.act(g.rstd.t[:], g.ps[6][:], AF.Sqrt, bias=EPS, reads=[g.psb[6]], writes=g.rstd.b)
        P.op("dve", lambda e: e.reciprocal(out=g.rstd.t[:], in_=g.rstd.t[:]), reads=g.rstd.b, writes=g.rstd.b)
        if mode == "post" and coef != 1.0:
            P.ts("dve", g.rstd.t[:], g.rstd.t[:], float(coef), None, ALU.mult, reads=g.rstd.b, writes=g.rstd.b)
        for k in range(NKC):
            wc, wb = col("norm_w", (l * 6 + i) * 8 + k)
            if mode == "pre":
                P.stt("dve", g.hh.t[:, k, :], src.t[:, k, :], wc, g.rstd.t[:], ALU.mult, ALU.mult,
                      reads=[src.b[k]] + wb + g.rstd.b, writes=[g.hh.b[k]])
            else:
                tb = k % 2
                P.stt("dve", g.tmp.t[:, tb, :], src.t[:, k, :], wc, g.rstd.t[:], ALU.mult, ALU.mult,
                      reads=[src.b[k]] + wb + g.rstd.b, writes=[g.tmp.b[tb]])
                P.tt("pool", g.xT.t[:, k, :], g.tmp.t[:, tb, :], g.xT.t[:, k, :], ALU.add,
                     reads=[g.tmp.b[tb], g.xT.b[k]], writes=[g.xT.b[k]])
    g.rmsnorm = rmsnorm

    def ffn(l, f):
        phase()
        g.hid = carve("hid", [128, NM, TT], BF16, nb=NM)
        rmsnorm(g.xT, l, 0 if f == 0 else 4, "pre")
        for grp in range(11):
            wg, wgb = wload("g%d%d" % (l, f), grp)
            wu, wub = wload("u%d%d" % (l, f), grp)
            for j in range(2):
                m = grp * 2 + j
                pg, pu = g.ps[m % 2], g.ps[2 + m % 2]
                pgb, pub = g.psb[m % 2], g.psb[2 + m % 2]
                for k in range(NKC):
                    P.mm(pg[:], wg[:, k, j * 128:(j + 1) * 128], g.hh.t[:, k, :], start=(k == 0), stop=(k == NKC - 1),
                         reads=wgb + [g.hh.b[k]], writes=[pgb])
                for k in range(NKC):
                    P.mm(pu[:], wu[:, k, j * 128:(j + 1) * 128], g.hh.t[:, k, :], start=(k == 0), stop=(k == NKC - 1),
                         reads=wub + [g.hh.b[k]], writes=[pub])
                P.act(g.gsb.t[:, m % 2, :], pg[:], AF.Silu, reads=[pgb], writes=[g.gsb.b[m % 2]])
                P.tt("dve", g.hid.t[:, m, :], g.gsb.t[:, m % 2, :], pu[:], ALU.mult,
                     reads=[g.gsb.b[m % 2], pub], writes=[g.hid.b[m]])
        for n in range(NKC):
            wd, wdb = wload("d%d%d" % (l, f), n)
            pd, pdb = g.ps[4 + n % 2], g.psb[4 + n % 2]
            for m in range(NM):
                P.mm(pd[:], wd[:, m, :], g.hid.t[:, m, :], start=(m == 0), stop=(m == NM - 1),
                     reads=wdb + [g.hid.b[m]], writes=[pdb])
            P.copy("act", g.f.t[:, n, :], pd[:], reads=[pdb], writes=[g.f.b[n]])
        rmsnorm(g.f, l, 1 if f == 0 else 5, "post", coef=0.5)
    g.ffn = ffn

    def load_x(tt):
        t0 = tt * TT
        phase()
        g.xin = carve("xin", [128, 4, D], F32)
        P.dma("pool", g.xin.t[:], I["x"][t0:t0 + TT, :].rearrange("(g p) d -> p g d", p=128), "xin", writes=g.xin.b)
        for k in range(NKC):
            pb = g.ps[k % 2]
            for gi in range(4):
                P.mm(pb[:, gi * 128:(gi + 1) * 128], g.xin.t[:, gi, k * 128:(k + 1) * 128], cF("ident"),
                     reads=g.xin.b + g.cf.b, writes=[g.psb[k % 2]])
            P.copy("act" if k % 2 else "dve", g.xT.t[:, k, :], pb[:], reads=[g.psb[k % 2]], writes=[g.xT.b[k]])

    def store_x(tt):
        t0 = tt * TT
        phase()
        g.xin = carve("xin", [128, 4, D], F32)
        for gi in range(4):
            for h in range(2):
                pb, pbb = g.ps[(gi * 2 + h) % 2], g.psb[(gi * 2 + h) % 2]
                for kk in range(4):
                    k = h * 4 + kk
                    P.mm(pb[:, kk * 128:(kk + 1) * 128], g.xT.t[:, k, gi * 128:(gi + 1) * 128], cF("ident"),
                         reads=[g.xT.b[k]] + g.cf.b, writes=[pbb])
                P.copy("act" if h else "dve", g.xin.t[:, gi, h * 512:(h + 1) * 512], pb[:], reads=[pbb], writes=g.xin.b)
        return P.dma("pool", g.out[t0:t0 + TT, :].rearrange("(g p) d -> p g d", p=128), g.xin.t[:], "xout", reads=g.xin.b)

    def dump(name, ap_sb, bufs, dram_view=None):
        dv = g.dbg[name] if dram_view is None else dram_view
        g.dbg_toks.append(P.dma("pool", dv, ap_sb, "dbg", reads=bufs))
    g.dump = dump

    if flags.get("even", True) and L >= 1:
        setup_even_w(g)
    if flags.get("odd", True) and L >= 2:
        setup_odd_w(g)

    last = None
    for tt in range(NT):
        load_x(tt)
        for l in range(L):
            if flags.get("ffn", True):
                ffn(l, 0)
            if l == 0 and flags.get("even", True):
                even_mixer(g, tt)
            if l == 1 and flags.get("odd", True):
                odd_mixer(g, tt)
            if flags.get("ffn", True):
                ffn(l, 1)
        last = store_x(tt)
    P.final_wait("pool", [last] + g.dbg_toks)
    P.emit()
    P.close()
    A.close()
    g.n_instr = dict(P.cnt)
    return nc, g


class M:
    pass


M.Tl = Tl
M.Ctx = Ctx
M.CONST_COLS = CONST_COLS
M.EPS = EPS


TT, C, NCH, NKC = 512, 64, 8, 8


def setup_even(g):
    A, P, I = g.A, g.P, g.I
    Tl = M.Tl
    e = M.Ctx()
    g.e = e
    e.w2 = Tl(A, "gla_w2", [16, 256], F32)
    P.dma("sp", e.w2.t[:], I["gla_lora_w2"][0], "c0", writes=e.w2.b)
    e.neglb = Tl(A, "neglb", [64, 4], F32)
    for h in range(4):
        c_, cb_ = g.c64("gla_lora_b", h)
        P.ts("dve", e.neglb.t[:, h:h + 1], c_, -1.0, None, ALU.mult, reads=cb_, writes=e.neglb.b)
    e.S = Tl(A, "gla_S", [64, 4, 128], F32)
    e.Sb = Tl(A, "gla_Sb", [64, 4, 128], BF16)
    P.memset("pool", e.S.t[:], 0.0, writes=e.S.b)
    P.memset("pool", e.Sb.t[:], 0.0, writes=e.Sb.b)
    e.ycat = Tl(A, "ycat", [128, 8, TT], BF16, nb=8)
    if not g.flags.get("gdn", True):
        for k in range(4, 8):
            P.memset("pool", e.ycat.t[:, k, :], 0.0, writes=[e.ycat.b[k]])
    if g.flags.get("gdn", True):
        setup_gdn(g)


def setup_even_w(g):
    I = g.I
    pieces = [(0, 512), (512, 1024), (1024, 1536), (1536, 1552), (1552, 2064), (2064, 2576), (2576, 3088),
              (3088, 3600), (3600, 3608)]
    g.prep_matrix("ein", I["even_w_in"][0], pieces)
    g.prep_matrix("eout", I["even_w_out"][0], [(0, 512), (512, 1024)])


def proj_fm(g, ps, psb, w, wb, c0, ncols):
    P = g.P
    for k in range(NKC):
        P.mm(ps[0:ncols, :], w[:, k, c0:c0 + ncols], g.hh.t[:, k, :], start=(k == 0), stop=(k == NKC - 1),
             reads=wb + [g.hh.b[k]], writes=[psb])


def head_norm_gate(g, ps_o, ps_ob, normcol, normb, gate_ap, gate_b, out_ap, out_b, stat_bank):
    P, e = g.P, g.e
    P.copy("act", e.osb.t[:], ps_o[:], reads=[ps_ob], writes=e.osb.b)
    P.act(g.sq.t[:, 0, :], ps_o[:], AF.Square, reads=[ps_ob], writes=[g.sq.b[0]])
    sp_, spb = g.ps[stat_bank], g.psb[stat_bank]
    o1 = M.CONST_COLS["ones_128m"][0]
    P.mm(sp_[:], g.cb.t[:, o1:o1 + 128], g.sq.t[:, 0, :], reads=g.cb.b + [g.sq.b[0]], writes=[spb])
    P.act(g.rstd.t[:], sp_[:], AF.Sqrt, bias=M.EPS, reads=[spb], writes=g.rstd.b)
    P.op("dve", lambda e_: e_.reciprocal(out=g.rstd.t[:], in_=g.rstd.t[:]), reads=g.rstd.b, writes=g.rstd.b)
    P.stt("dve", e.osb.t[:], e.osb.t[:], normcol, g.rstd.t[:], ALU.mult, ALU.mult,
          reads=e.osb.b + normb + g.rstd.b, writes=e.osb.b)
    P.tt("dve", out_ap, e.osb.t[:], gate_ap, ALU.mult, reads=e.osb.b + gate_b, writes=out_b)


def zero_y(g):
    for k in range(4):
        g.P.memset("pool", g.e.ycat.t[:, k, :], 0.0, writes=[g.e.ycat.b[k]])


def gla(g, tt):
    P, e = g.P, g.e
    ps, psb = g.ps, g.psb
    cv = g.carve
    e.qe = cv("qe", [64, 4, TT], BF16, nb=4)
    e.ke = cv("ke", [64, 4, TT], BF16, nb=4)
    e.kd = cv("kd", [64, 4, TT], BF16, nb=4)
    e.ksb = cv("ksb", [64, TT], F32)
    e.cs = cv("cs", [64, TT], F32)
    e.ex = cv("ex", [64, 2, TT], F32, nb=2)
    e.alast = cv("alast", [64, 4, NCH], F32, nb=4)
    e.glr = cv("glr", [16, TT], F32)
    e.vtok = cv("vtok", [64, NCH, 512], BF16, nb=NCH)
    e.kdtok = cv("kdtok", [64, NCH, 256], BF16, nb=NCH)
    e.sgg = cv("sgg", [128, 4, TT], BF16, nb=4)
    e.att = cv("att", [64, 4, 512], BF16, nb=4)
    e.osb = cv("osb", [128, TT], F32)
    wl, wlb = g.wload("ein", 3)
    proj_fm(g, ps[0], psb[0], wl, wlb, 0, 16)
    P.copy("act", e.glr.t[:], ps[0][0:16, :], reads=[psb[0]], writes=e.glr.b)
    wqk, wqkb = g.wload("ein", 0)
    for h in range(4):
        b0 = (h % 2) * 2
        P.mm(ps[b0][0:64, :], e.w2.t[0:16, h * 64:(h + 1) * 64], e.glr.t[0:16, :], reads=e.w2.b + e.glr.b, writes=[psb[b0]])
        P.act(e.ex.t[:, 0, :], ps[b0][0:64, :], AF.Exp, scale=-1.0, bias=e.neglb.t[:, h:h + 1], reads=[psb[b0]] + e.neglb.b, writes=[e.ex.b[0]])
        P.act(e.ex.t[:, 0, :], e.ex.t[:, 0, :], AF.Ln, bias=1.0, reads=[e.ex.b[0]], writes=[e.ex.b[0]])
        P.op("dve", lambda e_: e_.tensor_tensor_scan(out=e.cs.t[:], data0=g.cF("reset", 64), data1=e.ex.t[:, 0, :], initial=0.0,
                                                     op0=ALU.mult, op1=ALU.add), reads=[e.ex.b[0]] + g.cf.b, writes=e.cs.b)
        proj_fm(g, ps[b0 + 1], psb[b0 + 1], wqk, wqkb, h * 64, 64)
        P.act(e.ex.t[:, 1, :], e.cs.t[:], AF.Exp, scale=-1.0 / 16, reads=e.cs.b, writes=[e.ex.b[1]])
        P.stt("dve", e.qe.t[:, h, :], ps[b0 + 1][0:64, :], 0.125, e.ex.t[:, 1, :], ALU.mult, ALU.mult,
              reads=[psb[b0 + 1], e.ex.b[1]], writes=[e.qe.b[h]])
        proj_fm(g, ps[b0], psb[b0], wqk, wqkb, 256 + h * 64, 64)
        P.copy("act", e.ksb.t[:], ps[b0][0:64, :], reads=[psb[b0]], writes=e.ksb.b)
        P.act(e.ex.t[:, 1, :], e.cs.t[:], AF.Exp, scale=1.0 / 16, reads=e.cs.b, writes=[e.ex.b[1]])
        P.tt("dve", e.ke.t[:, h, :], e.ksb.t[:], e.ex.t[:, 1, :], ALU.mult, reads=e.ksb.b + [e.ex.b[1]], writes=[e.ke.b[h]])
        cs3 = e.cs.t[:].rearrange("p (n c) -> p n c", c=C)
        ex3 = e.ex.t[:, 0, :].rearrange("p (n c) -> p n c", c=C)
        P.tt("dve", ex3, cs3[:, :, C - 1:C].broadcast_to([64, NCH, C]), cs3, ALU.subtract, reads=e.cs.b, writes=[e.ex.b[0]])
        P.act(e.ex.t[:, 0, :], e.ex.t[:, 0, :], AF.Exp, scale=-1.0 / 16, reads=[e.ex.b[0]], writes=[e.ex.b[0]])
        P.tt("dve", e.kd.t[:, h, :], e.ksb.t[:], e.ex.t[:, 0, :], ALU.mult, reads=e.ksb.b + [e.ex.b[0]], writes=[e.kd.b[h]])
        P.act(e.alast.t[:, h, :], cs3[:, :, C - 1], AF.Exp, scale=-1.0 / 16, reads=e.cs.b, writes=[e.alast.b[h]])
    wv, wvb = g.wload("ein", 1)
    for n in range(NCH):
        pb, pbb = ps[n % 4], psb[n % 4]
        for k in range(NKC):
            P.mm(pb[0:64, :], g.hh.t[:, k, n * 64:(n + 1) * 64], wv[:, k, :], start=(k == 0), stop=(k == NKC - 1),
                 reads=[g.hh.b[k]] + wvb, writes=[pbb])
        P.copy("act" if n % 2 else "dve", e.vtok.t[:, n, :], pb[0:64, :], reads=[pbb], writes=[e.vtok.b[n]])
    wg, wgb = g.wload("ein", 2)
    for h in range(4):
        pb, pbb = ps[4 + h % 2], psb[4 + h % 2]
        proj_fm(g, pb, pbb, wg, wgb, h * 128, 128)
        P.act(e.sgg.t[:, h, :], pb[:], AF.Silu, reads=[pbb], writes=[e.sgg.b[h]])
    ib = M.CONST_COLS["ident"][0]
    for n in range(NCH):
        pb, pbb = ps[n % 4], psb[n % 4]
        for h in range(4):
            P.mm(pb[0:64, h * 64:(h + 1) * 64], e.kd.t[:, h, n * 64:(n + 1) * 64], g.cb.t[0:64, ib:ib + 64],
                 reads=[e.kd.b[h]] + g.cb.b, writes=[pbb])
        P.copy("act" if n % 2 else "dve", e.kdtok.t[:, n, :], pb[0:64, 0:256], reads=[pbb], writes=[e.kdtok.b[n]])
    for h in range(4):
        pb, pbb = ps[h], psb[h]
        for n in range(NCH):
            P.mm(pb[0:64, n * 64:(n + 1) * 64], e.ke.t[:, h, n * 64:(n + 1) * 64], e.qe.t[:, h, n * 64:(n + 1) * 64],
                 reads=[e.ke.b[h], e.qe.b[h]], writes=[pbb])
        P.tt("dve", e.att.t[:, h, :], pb[0:64, :], g.cF("m_incl", 64), ALU.mult, reads=[pbb] + g.cf.b, writes=[e.att.b[h]])
    for n in range(NCH):
        for h in range(4):
            po, pob = ps[4 + h], psb[4 + h]
            P.mm(po[:, n * 64:(n + 1) * 64], e.vtok.t[:, n, h * 128:(h + 1) * 128], e.att.t[:, h, n * 64:(n + 1) * 64],
                 start=True, stop=False, reads=[e.vtok.b[n], e.att.b[h]], writes=[pob])
            P.mm(po[:, n * 64:(n + 1) * 64], e.Sb.t[:, h, :], e.qe.t[:, h, n * 64:(n + 1) * 64],
                 start=False, stop=True, reads=e.Sb.b + [e.qe.b[h]], writes=[pob])
        for h in range(4):
            P.mm(ps[3][0:64, h * 128:(h + 1) * 128], e.kdtok.t[:, n, h * 64:(h + 1) * 64],
                 e.vtok.t[:, n, h * 128:(h + 1) * 128], reads=[e.kdtok.b[n], e.vtok.b[n]], writes=[psb[3]])
        for h in range(4):
            P.stt("dve", e.S.t[:, h, :], e.S.t[:, h, :], e.alast.t[:, h, n:n + 1], ps[3][0:64, h * 128:(h + 1) * 128], ALU.mult, ALU.add,
                  reads=e.S.b + [e.alast.b[h], psb[3]], writes=e.S.b)
        P.copy("act", e.Sb.t[:], e.S.t[:], reads=e.S.b, writes=e.Sb.b)
    nc_, nb_ = g.colf("gla_norm", 0)
    for h in range(4):
        head_norm_gate(g, ps[4 + h], psb[4 + h], nc_, nb_, e.sgg.t[:, h, :], [e.sgg.b[h]], e.ycat.t[:, h, :], [e.ycat.b[h]], 0)


def even_mixer(g, tt):
    P, e = g.P, g.e
    g.phase()
    g.rmsnorm(g.xT, 0, 2, "pre")
    gla(g, tt)
    if g.flags.get("gdn", True):
        gdn(g, tt)
    for half in range(2):
        wo, wob = g.wload("eout", half)
        for nn in range(4):
            n = half * 4 + nn
            pb, pbb = g.ps[n % 2], g.psb[n % 2]
            for k in range(NKC):
                P.mm(pb[:], wo[:, k, nn * 128:(nn + 1) * 128], e.ycat.t[:, k, :], start=(k == 0), stop=(k == NKC - 1),
                     reads=wob + [e.ycat.b[k]], writes=[pbb])
            P.copy("act", g.f.t[:, n, :], pb[:], reads=[pbb], writes=[g.f.b[n]])
    g.rmsnorm(g.f, 0, 3, "post", coef=1.0)


TT, C, NCH, NKC = 512, 64, 8, 8


def setup_gdn(g):
    A, P, I = g.A, g.P, g.I
    Tl = M.Tl
    e = g.e
    e.hist = Tl(A, "gdn_hist", [128, 12, 3], F32, nb=12)
    P.memset("pool", e.hist.t[:], 0.0, writes=e.hist.b)
    e.GS = Tl(A, "gdn_S", [128, 4, 128], F32, nb=4)
    e.GSb = Tl(A, "gdn_Sb", [128, 4, 128], BF16, nb=4)
    P.memset("pool", e.GS.t[:], 0.0, writes=e.GS.b)
    P.memset("pool", e.GSb.t[:], 0.0, writes=e.GSb.b)
    e.negA = Tl(A, "negA", [64, 4], F32)
    e.dtb = Tl(A, "dtb", [64, 4], F32)
    P.dma("sp", e.negA.t[:], I["gdn_a_log"][0].partition_broadcast(64), "c0", writes=e.negA.b)
    P.dma("sp", e.dtb.t[:], I["gdn_dt_bias"][0].partition_broadcast(64), "c0", writes=e.dtb.b)
    P.act(e.negA.t[:], e.negA.t[:], AF.Exp, reads=e.negA.b, writes=e.negA.b)
    P.ts("dve", e.negA.t[:], e.negA.t[:], -1.0, None, ALU.mult, reads=e.negA.b, writes=e.negA.b)


def inv_rsqrt(g, out_ap, out_b, in_ap, in_b, eps):
    P = g.P
    P.act(out_ap, in_ap, AF.Sqrt, bias=eps, reads=in_b, writes=out_b)
    P.op("dve", lambda e_: e_.reciprocal(out=out_ap, in_=out_ap), reads=out_b, writes=out_b)


def gdn(g, tt):
    P, e = g.P, g.e
    ps, psb = g.ps, g.psb
    cF, cB = g.cF, g.cB
    g.phase()
    cv = g.carve
    ib = M.CONST_COLS["ident"][0]
    identb = g.cb.t[:, ib:ib + 128]
    identf = g.cf.t[0:64, ib:ib + 64]
    o1 = M.CONST_COLS["ones_1"][0]

    ab = cv("ab", [64, NCH, 8], F32)
    wab, wabb = g.wload("ein", 8)
    for n in range(NCH):
        for k in range(NKC):
            P.mm(ps[0][0:64, n * 8:(n + 1) * 8], g.hh.t[:, k, n * 64:(n + 1) * 64], wab[:, k, 0:8], start=(k == 0), stop=(k == NKC - 1),
                 reads=[g.hh.b[k]] + wabb, writes=[psb[0]])
    P.copy("dve", ab.t[:].rearrange("p n c -> p (n c)"), ps[0][0:64, 0:64], reads=[psb[0]], writes=ab.b)
    beta = cv("beta", [64, NCH, 4], F32)
    gg = cv("gg", [64, NCH, 4], F32)
    P.act(beta.t[:], ab.t[:, :, 4:8], AF.Sigmoid, reads=ab.b, writes=beta.b)
    P.tt("dve", gg.t[:], ab.t[:, :, 0:4], e.dtb.t[:].unsqueeze(1).broadcast_to([64, NCH, 4]), ALU.add, reads=ab.b + e.dtb.b, writes=gg.b)
    P.act(gg.t[:], gg.t[:], AF.Exp, reads=gg.b, writes=gg.b)
    P.act(gg.t[:], gg.t[:], AF.Ln, bias=1.0, reads=gg.b, writes=gg.b)
    P.tt("dve", gg.t[:], gg.t[:], e.negA.t[:].unsqueeze(1).broadcast_to([64, NCH, 4]), ALU.mult, reads=gg.b + e.negA.b, writes=gg.b)
    gflat = gg.t[:].rearrange("p n h -> p (n h)")
    gc = cv("gc", [64, NCH, 4], F32)
    egl = cv("egl", [128, NCH, 4], F32)
    sc_kbe = cv("sc_kbe", [64, NCH, 4], F32)
    sc_kd = cv("sc_kd", [64, NCH, 4], F32)
    P.mm(ps[1][0:64, 0:32], cF("m_incl", 64, 0, 64), gflat, reads=g.cf.b + gg.b, writes=[psb[1]])
    P.mm(ps[1][:, 32:64], cF("ones_1", 64), gflat, reads=g.cf.b + gg.b, writes=[psb[1]])
    P.copy("dve", gc.t[:].rearrange("p n h -> p (n h)"), ps[1][0:64, 0:32], reads=[psb[1]], writes=gc.b)
    P.act(egl.t[:].rearrange("p n h -> p (n h)"), ps[1][:, 32:64], AF.Exp, reads=[psb[1]], writes=egl.b)
    P.tt("dve", sc_kd.t[:].rearrange("p n h -> p (n h)"), ps[1][0:64, 32:64], gc.t[:].rearrange("p n h -> p (n h)"), ALU.subtract,
         reads=[psb[1]] + gc.b, writes=sc_kd.b)
    P.act(sc_kd.t[:], sc_kd.t[:], AF.Exp, reads=sc_kd.b, writes=sc_kd.b)
    P.act(sc_kbe.t[:], gc.t[:], AF.Exp, reads=gc.b, writes=sc_kbe.b)
    P.tt("dve", sc_kbe.t[:], sc_kbe.t[:], beta.t[:], ALU.mult, reads=sc_kbe.b + beta.b, writes=sc_kbe.b)

    st = g.flags.get("gstage", 99)

    def zy(h):
        P.memset("pool", e.ycat.t[:, 4 + h, :], 0.0, writes=[e.ycat.b[4 + h]])
    if st <= 0:
        for h in range(4):
            zy(h)
        return
    raw = cv("raw", [128, TT + 3], F32)
    acc = cv("acc", [128, TT], F32)
    cc = cv("cc", [128, TT], F32)
    qT = cv("qT", [128, TT], BF16)
    kT = cv("kT", [128, TT], BF16)
    qeT = cv("qeT", [128, TT], BF16)
    G = cv("G", [64, NCH, C], F32)
    grep = cv("grep", [64, NCH, 128], F32)
    EB = cv("EB", [128, TT], F32)
    kbe = cv("kbe", [64, NCH, 128], BF16)
    kdc = cv("kdc", [64, NCH, 128], BF16)
    vb = cv("vb", [64, NCH, 128], BF16)
    E = cv("E", [64, TT], F32)
    Ya = [cv("Ya%d" % i, [64, TT], F32) for i in range(2)]
    YTa = [cv("YTa%d" % i, [64, TT], F32) for i in range(2)]
    R = cv("R", [64, TT], F32)
    att = cv("gatt", [64, TT], BF16)
    Tb = cv("Tb", [64, TT], BF16)
    u = cv("u", [64, NCH, 128], F32)
    wT = cv("wT", [128, TT], BF16)
    vnew = cv("vnew", [64, 128], BF16)
    sgz = cv("sgz", [128, TT], BF16)
    osb = cv("gosb", [128, TT], F32)
    e.osb = osb
    wdq = [None] * 3

    def conv_tile(k12, w3, w3b, cbase):
        proj_fm(g, ps[2], psb[2], w3, w3b, cbase, 128)
        P.copy("dve", raw.t[:, 0:3], e.hist.t[:, k12, :], reads=[e.hist.b[k12]], writes=raw.b)
        P.copy("act", raw.t[:, 3:TT + 3], ps[2][:], reads=[psb[2]], writes=raw.b)
        P.copy("dve", e.hist.t[:, k12, :], raw.t[:, TT:TT + 3], reads=raw.b, writes=[e.hist.b[k12]])
        c0, c0b = g.colf("gdn_conv", 0 * 12 + k12)
        P.ts("dve", acc.t[:], raw.t[:, 0:TT], c0, None, ALU.mult, reads=raw.b + c0b, writes=acc.b)
        for j in range(1, 4):
            cj, cjb = g.colf("gdn_conv", j * 12 + k12)
            P.stt("dve", acc.t[:], raw.t[:, j:j + TT], cj, acc.t[:], ALU.mult, ALU.add, reads=raw.b + cjb + acc.b, writes=acc.b)
        P.act(cc.t[:], acc.t[:], AF.Silu, reads=acc.b, writes=cc.b)

    def l2n(dst, scale):
        P.act(g.sq.t[:, 0, :], cc.t[:], AF.Square, reads=cc.b, writes=[g.sq.b[0]])
        P.mm(ps[3][:], g.cb.t[:, o1:o1 + 128], g.sq.t[:, 0, :], reads=g.cb.b + [g.sq.b[0]], writes=[psb[3]])
        inv_rsqrt(g, g.rstd.t[:], g.rstd.b, ps[3][:], [psb[3]], M.EPS)
        P.stt("dve", dst.t[:], cc.t[:], float(scale), g.rstd.t[:], ALU.mult, ALU.mult, reads=cc.b + g.rstd.b, writes=dst.b)

    for h in range(4):
        if wdq[0] is None or True:
            pass
        wq, wqb = g.wload("ein", 4)
        conv_tile(h, wq, wqb, h * 128)
        l2n(qT, 128 ** -0.5)
        wk, wkb = g.wload("ein", 5)
        conv_tile(4 + h, wk, wkb, h * 128)
        l2n(kT, 1.0)
        wv, wvb = g.wload("ein", 6)
        conv_tile(8 + h, wv, wvb, h * 128)
        P.copy("act", wT.t[:], cc.t[:], reads=cc.b, writes=wT.b)
        if st <= 1:
            zy(h)
            continue
        for half in range(2):
            pb, pbb = ps[half], psb[half]
            for n4 in range(4):
                n = half * 4 + n4
                P.mm(pb[0:64, n4 * 128:(n4 + 1) * 128], kT.t[:, n * 64:(n + 1) * 64], identb, reads=kT.b + g.cb.b, writes=[pbb])
            pv = pb[0:64, :].rearrange("p (n d) -> p n d", n=4)
            P.tt("dve", kbe.t[:, half * 4:half * 4 + 4, :], pv, sc_kbe.t[:, half * 4:half * 4 + 4, h:h + 1].broadcast_to([64, 4, 128]), ALU.mult,
                 reads=[pbb] + sc_kbe.b, writes=kbe.b)
            P.tt("dve", kdc.t[:, half * 4:half * 4 + 4, :], pv, sc_kd.t[:, half * 4:half * 4 + 4, h:h + 1].broadcast_to([64, 4, 128]), ALU.mult,
                 reads=[pbb] + sc_kd.b, writes=kdc.b)
        for half in range(2):
            pb, pbb = ps[2 + half], psb[2 + half]
            for n4 in range(4):
                n = half * 4 + n4
                P.mm(pb[0:64, n4 * 128:(n4 + 1) * 128], wT.t[:, n * 64:(n + 1) * 64], identb, reads=wT.b + g.cb.b, writes=[pbb])
            pv = pb[0:64, :].rearrange("p (n d) -> p n d", n=4)
            P.tt("dve", vb.t[:, half * 4:half * 4 + 4, :], pv, beta.t[:, half * 4:half * 4 + 4, h:h + 1].broadcast_to([64, 4, 128]), ALU.mult,
                 reads=[pbb] + beta.b, writes=vb.b)
        if st <= 2:
            zy(h)
            continue
        P.tt("dve", G.t[:], cF("m_strict_T", 64).rearrange("p (n c) -> p n c", c=C), gg.t[:, :, h:h + 1].broadcast_to([64, NCH, C]), ALU.mult,
             reads=g.cf.b + gg.b, writes=G.b)
        P.tt("dve", grep.t[:], cF("ones_1", 64).unsqueeze(1).broadcast_to([64, NCH, 128]), gg.t[:, :, h:h + 1].broadcast_to([64, NCH, 128]), ALU.mult,
             reads=g.cf.b + gg.b, writes=grep.b)
        for n in range(NCH):
            P.mm(ps[4][:, n * 64:(n + 1) * 64], grep.t[:, n, :], cF("m_incl", 64, 0, 64), reads=grep.b + g.cf.b, writes=[psb[4]])
        P.act(EB.t[:], ps[4][:], AF.Exp, reads=[psb[4]], writes=EB.b)
        P.tt("dve", qeT.t[:], qT.t[:], EB.t[:], ALU.mult, reads=qT.b + EB.b, writes=qeT.b)
        if st <= 3:
            zy(h)
            continue
        for n in range(NCH):
            sl = slice(n * 64, (n + 1) * 64)
            P.mm(ps[5][0:64, sl], cF("m_incl", 64, 0, 64), G.t[:, n, :], reads=g.cf.b + G.b, writes=[psb[5]])
            P.mm(ps[6][0:64, sl], G.t[:, n, :], cF("m_incl", 64, 0, 64), reads=g.cf.b + G.b, writes=[psb[6]])
            P.mm(ps[0][0:64, sl], kT.t[:, sl], kT.t[:, sl], reads=kT.b, writes=[psb[0]])
            P.mm(ps[1][0:64, sl], kT.t[:, sl], qT.t[:, sl], reads=kT.b + qT.b, writes=[psb[1]])
        P.act(E.t[:], ps[5][0:64, :], AF.Exp, reads=[psb[5]], writes=E.b)
        P.tt("pool", E.t[:], E.t[:], cF("m_strict_T", 64), ALU.mult, reads=E.b + g.cf.b, writes=E.b)
        P.stt("dve", YTa[0].t[:], ps[0][0:64, :], -1.0, E.t[:], ALU.mult, ALU.mult, reads=[psb[0]] + E.b, writes=YTa[0].b)
        yt3 = YTa[0].t[:].rearrange("p (n c) -> p n c", c=C)
        P.tt("dve", yt3, yt3, beta.t[:, :, h:h + 1].broadcast_to([64, NCH, C]), ALU.mult, reads=YTa[0].b + beta.b, writes=YTa[0].b)
        P.act(E.t[:], ps[6][0:64, :], AF.Exp, reads=[psb[6]], writes=E.b)
        P.tt("pool", E.t[:], E.t[:], cF("m_incl", 64), ALU.mult, reads=E.b + g.cf.b, writes=E.b)
        P.tt("dve", att.t[:], ps[1][0:64, :], E.t[:], ALU.mult, reads=[psb[1]] + E.b, writes=att.b)
        if st <= 4:
            zy(h)
            continue
        for n in range(NCH):
            sl = slice(n * 64, (n + 1) * 64)
            P.mm(ps[5][0:64, sl], YTa[0].t[:, sl], identf, reads=YTa[0].b + g.cf.b, writes=[psb[5]])
        nsub = g.flags.get("nsub", 9)
        extra = []
        if g.flags.get("dummy"):
            P.mm(ps[4][0:64, 0:64], YTa[0].t[:, 0:64], identf, reads=YTa[0].b + g.cf.b, writes=[psb[4]])
            extra = [psb[4]]
        if nsub >= 2:
            ydst = {"Ya": Ya[0], "R": R, "E": E}[g.flags.get("ydst", "Ya")]
            if g.flags.get("yeng", "dve") == "actf":
                P.act(ydst.t[:], ps[5][0:64, :], AF.Identity, reads=[psb[5]], writes=ydst.b)
            else:
                P.copy(g.flags.get("yeng", "dve"), ydst.t[:], ps[5][0:64, :], reads=[psb[5]] + extra, writes=ydst.b)
        if nsub >= 3:
            P.tt("dve", R.t[:], ps[5][0:64, :], cF("ident64x", 64), ALU.add, reads=[psb[5]] + g.cf.b, writes=R.b)
        cur = 0
        for lvl in range(1, 1 + g.flags.get("nlev", 5)):
            nxt = 1 - cur
            for n in range(NCH):
                sl = slice(n * 64, (n + 1) * 64)
                P.mm(ps[0][0:64, sl], YTa[cur].t[:, sl], Ya[cur].t[:, sl], reads=YTa[cur].b + Ya[cur].b, writes=[psb[0]])
                P.mm(ps[1][0:64, sl], Ya[cur].t[:, sl], YTa[cur].t[:, sl], reads=YTa[cur].b + Ya[cur].b, writes=[psb[1]])
            lsub = g.flags.get("lsub", 9)
            if lvl < 5 and lsub >= 2:
                P.copy(g.flags.get("yeng", "dve"), Ya[nxt].t[:], ps[0][0:64, :], reads=[psb[0]], writes=Ya[nxt].b)
            if lsub >= 2:
                P.copy("dve", YTa[nxt].t[:], ps[1][0:64, :], reads=[psb[1]], writes=YTa[nxt].b)
            if lsub >= 3:
                for n in range(NCH):
                    sl = slice(n * 64, (n + 1) * 64)
                    P.mm(ps[5][0:64, sl], YTa[nxt].t[:, sl], R.t[:, sl], reads=YTa[nxt].b + R.b, writes=[psb[5]])
            if lsub >= 4:
                P.tt("dve", R.t[:], R.t[:], ps[5][0:64, :], ALU.add, reads=R.b + [psb[5]], writes=R.b)
            cur = nxt
        if nsub >= 4:
            P.copy("act", Tb.t[:], R.t[:], reads=R.b, writes=Tb.b)
        if st <= 5:
            zy(h)
            continue
        for half in range(2):
            pb, pbb = ps[half], psb[half]
            for n4 in range(4):
                n = half * 4 + n4
                P.mm(pb[0:64, n4 * 128:(n4 + 1) * 128], Tb.t[:, n * 64:(n + 1) * 64], vb.t[:, n, :], reads=Tb.b + vb.b, writes=[pbb])
            P.copy("act" if half else "dve", u.t[:, half * 4:half * 4 + 4, :].rearrange("p n d -> p (n d)"), pb[0:64, :], reads=[pbb], writes=u.b)
        for n in range(NCH):
            sl = slice(n * 64, (n + 1) * 64)
            P.mm(ps[2][:, sl], kbe.t[:, n, :], Tb.t[:, sl], reads=kbe.b + Tb.b, writes=[psb[2]])
        P.copy("act", wT.t[:], ps[2][:], reads=[psb[2]], writes=wT.b)
        wz, wzb = g.wload("ein", 7)
        proj_fm(g, ps[3], psb[3], wz, wzb, h * 128, 128)
        P.act(sgz.t[:], ps[3][:], AF.Silu, reads=[psb[3]], writes=sgz.b)
        if st <= 6:
            zy(h)
            continue
        po, pob = ps[7], psb[7]
        for n in range(NCH):
            sl = slice(n * 64, (n + 1) * 64)
            P.mm(ps[4][0:64, 0:128], wT.t[:, sl], e.GSb.t[:, h, :], reads=wT.b + [e.GSb.b[h]], writes=[psb[4]])
            P.tt("dve", vnew.t[:], u.t[:, n, :], ps[4][0:64, 0:128], ALU.subtract, reads=u.b + [psb[4]], writes=vnew.b)
            P.mm(po[:, sl], e.GSb.t[:, h, :], qeT.t[:, sl], start=True, stop=False, reads=[e.GSb.b[h]] + qeT.b, writes=[pob])
            P.mm(po[:, sl], vnew.t[:], att.t[:, sl], start=False, stop=True, reads=vnew.b + att.b, writes=[pob])
            P.mm(ps[6][:, 0:128], kdc.t[:, n, :], vnew.t[:], reads=kdc.b + vnew.b, writes=[psb[6]])
            P.stt("dve", e.GS.t[:, h, :], e.GS.t[:, h, :], egl.t[:, n, h:h + 1], ps[6][:, 0:128], ALU.mult, ALU.add,
                  reads=[e.GS.b[h], psb[6]] + egl.b, writes=[e.GS.b[h]])
            P.copy("act", e.GSb.t[:, h, :], e.GS.t[:, h, :], reads=[e.GS.b[h]], writes=[e.GSb.b[h]])
        nc_, nb_ = g.colf("gdn_norm", 0)
        head_norm_gate(g, po, pob, nc_, nb_, sgz.t[:], sgz.b, e.ycat.t[:, 4 + h, :], [e.ycat.b[4 + h]], 3)


TT, C, NCH, NKC = 512, 64, 8, 8
RWKV_GN_EPS = 64e-5


def setup_odd(g):
    A, P, I = g.A, g.P, g.I
    Tl = M.Tl
    o = M.Ctx()
    g.o = o
    o.w2 = Tl(A, "rw_w2", [64, 512], F32)
    o.a2 = Tl(A, "rw_a2", [64, 512], F32)
    o.g2 = M.Ctx(); o.g2.t = g.f.t[:, 2, :]; o.g2.b = [g.f.b[2]]
    P.dma("sp", o.w2.t[:], I["rwkv_w2"][0], "c0", writes=o.w2.b)
    P.dma("sp", o.a2.t[:], I["rwkv_a2"][0], "c0", writes=o.a2.b)
    P.dma("sp", o.g2.t[:], I["rwkv_g2"][0], "c0", writes=o.g2.b)
    o.g2b = Tl(A, "rw_g2b", [128, 512], BF16)
    P.copy("dve", o.g2b.t[:], o.g2.t[:], reads=o.g2.b, writes=o.g2b.b)
    o.wa = M.Ctx(); o.wa.t = g.f.t[:, 0, :].rearrange("p (k c) -> p k c", k=4); o.wa.b = [g.f.b[0]]
    o.wx = M.Ctx(); o.wx.t = g.f.t[:, 1, :].rearrange("p (k c) -> p k c", k=4); o.wx.b = [g.f.b[1]]
    P.memset("pool", o.wa.t[:], 0.0, writes=o.wa.b)
    P.memset("pool", o.wx.t[:], 0.0, writes=o.wx.b)
    for k in range(4):
        for a in range(2):
            P.dma("sp", o.wa.t[a * 64:(a + 1) * 64, k, a * 64:(a + 1) * 64], I["lru_wa"][0, 2 * k + a], "c0", writes=o.wa.b)
            P.dma("sp", o.wx.t[a * 64:(a + 1) * 64, k, a * 64:(a + 1) * 64], I["lru_wx"][0, 2 * k + a], "c0", writes=o.wx.b)
    o.wab = Tl(A, "lru_wab", [128, 4, 128], BF16)
    o.wxb = Tl(A, "lru_wxb", [128, 4, 128], BF16)
    P.copy("dve", o.wab.t[:], o.wa.t[:], reads=o.wa.b, writes=o.wab.b)
    P.copy("dve", o.wxb.t[:], o.wx.t[:], reads=o.wx.b, writes=o.wxb.b)
    o.dc = Tl(A, "odd_cols", [128, 16], F32)
    for k in range(4):
        lc, lb = g.colf("lru_lambda", k)
        P.act(o.dc.t[:, k:k + 1], lc, AF.Exp, scale=-1.0, reads=lb, writes=o.dc.b)
    P.act(o.dc.t[:, 0:4], o.dc.t[:, 0:4], AF.Ln, bias=1.0, reads=o.dc.b, writes=o.dc.b)
    P.ts("dve", o.dc.t[:, 0:4], o.dc.t[:, 0:4], -8.0, None, ALU.mult, reads=o.dc.b, writes=o.dc.b)
    for h in range(8):
        wc, wb = g.c64("rwkv_w0", h)
        P.ts("dve", o.dc.t[0:64, 4 + h:5 + h], wc, -1.0, None, ALU.mult, reads=wb, writes=o.dc.b)
    o.A = Tl(A, "rw_A", [64, 8, 64], F32, nb=8)
    o.Ab = Tl(A, "rw_Ab", [64, 8, 64], BF16, nb=8)
    P.memset("pool", o.A.t[:], 0.0, writes=o.A.b)
    P.memset("pool", o.Ab.t[:], 0.0, writes=o.Ab.b)
    o.sh = Tl(A, "rw_sh", [128, 28], F32)
    P.memset("pool", o.sh.t[:], 0.0, writes=o.sh.b)
    o.lh = Tl(A, "lru_hist", [128, 4, 3], F32)
    P.memset("pool", o.lh.t[:], 0.0, writes=o.lh.b)
    o.hs = Tl(A, "lru_state", [128, 4], F32)
    P.memset("pool", o.hs.t[:], 0.0, writes=o.hs.b)
    o.yr = Tl(A, "yr", [64, 8, TT], BF16, nb=8)
    o.yl = g.e.ycat if hasattr(g, "e") and hasattr(g.e, "ycat") else Tl(A, "ycat_o", [128, 8, TT], BF16, nb=8)


def setup_odd_w(g):
    I = g.I
    g.prep_matrix("oin", I["odd_w_in"][0], [(0, 512), (512, 1024), (1024, 1536), (1536, 1792), (1792, 2304), (2304, 2816)])
    g.prep_matrix("oout_r", I["odd_w_out"][0][0:512, :], [(0, 512), (512, 1024)], p=64)
    g.prep_matrix("oout_l", I["odd_w_out"][0][512:1024, :], [(0, 512), (512, 1024)])


def lru(g, tt):
    P, o = g.P, g.o
    ps, psb = g.ps, g.psb
    cv = g.carve
    raw = cv("lraw", [128, TT + 3], F32)
    xb = cv("lxb", [128, TT], F32)
    xbb = cv("lxbb", [128, TT], BF16)
    gr = cv("lgr", [128, TT], F32)
    gi = cv("lgi", [128, TT], F32)
    aa = cv("laa", [128, TT], F32)
    hh_ = cv("lhh", [128, TT], F32)
    gl = cv("lgl", [128, TT], F32)
    for k in range(4):
        wx_, wxb_ = g.wload("oin", 4)
        proj_fm(g, ps[0], psb[0], wx_, wxb_, k * 128, 128)
        P.copy("dve", raw.t[:, 0:3], o.lh.t[:, k, :], reads=o.lh.b, writes=raw.b)
        P.copy("act", raw.t[:, 3:TT + 3], ps[0][:], reads=[psb[0]], writes=raw.b)
        P.copy("dve", o.lh.t[:, k, :], raw.t[:, TT:TT + 3], reads=raw.b, writes=o.lh.b)
        c0, c0b = g.colf("lru_conv_w", 0 * 4 + k)
        bc, bcb = g.colf("lru_conv_b", k)
        P.ts("dve", xb.t[:], raw.t[:, 0:TT], c0, bc, ALU.mult, ALU.add, reads=raw.b + c0b + bcb, writes=xb.b)
        for j in range(1, 4):
            cj, cjb = g.colf("lru_conv_w", j * 4 + k)
            P.stt("dve", xb.t[:], raw.t[:, j:j + TT], cj, xb.t[:], ALU.mult, ALU.add, reads=raw.b + cjb + xb.b, writes=xb.b)
        P.copy("act", xbb.t[:], xb.t[:], reads=xb.b, writes=xbb.b)
        P.mm(ps[1][:], o.wab.t[:, k, :], xbb.t[:], reads=o.wab.b + xbb.b, writes=[psb[1]])
        P.mm(ps[2][:], o.wxb.t[:, k, :], xbb.t[:], reads=o.wxb.b + xbb.b, writes=[psb[2]])
        ba, bab = g.colf("lru_ba", k)
        bx, bxb = g.colf("lru_bx", k)
        P.act(gr.t[:], ps[1][:], AF.Sigmoid, bias=ba, reads=[psb[1]] + bab, writes=gr.b)
        P.act(gi.t[:], ps[2][:], AF.Sigmoid, bias=bx, reads=[psb[2]] + bxb, writes=gi.b)
        P.act(aa.t[:], gr.t[:], AF.Exp, scale=o.dc.t[:, k:k + 1], reads=gr.b + o.dc.b, writes=aa.b)
        P.tt("pool", gr.t[:], aa.t[:], aa.t[:], ALU.mult, reads=aa.b, writes=gr.b)
        P.act(gr.t[:], gr.t[:], AF.Sqrt, scale=-1.0, bias=1.0, reads=gr.b, writes=gr.b)
        P.tt("dve", gi.t[:], gi.t[:], gr.t[:], ALU.mult, reads=gi.b + gr.b, writes=gi.b)
        P.tt("dve", gi.t[:], gi.t[:], xb.t[:], ALU.mult, reads=gi.b + xb.b, writes=gi.b)
        P.op("dve", lambda e_, k=k: e_.tensor_tensor_scan(out=hh_.t[:], data0=aa.t[:], data1=gi.t[:], initial=o.hs.t[:, k:k + 1],
                                                          op0=ALU.mult, op1=ALU.add), reads=aa.b + gi.b + o.hs.b, writes=hh_.b)
        P.copy("dve", o.hs.t[:, k:k + 1], hh_.t[:, TT - 1:TT], reads=hh_.b, writes=o.hs.b)
        wy_, wyb_ = g.wload("oin", 5)
        proj_fm(g, ps[3], psb[3], wy_, wyb_, k * 128, 128)
        P.act(gl.t[:], ps[3][:], AF.Square, reads=[psb[3]], writes=gl.b)
        P.ts("dve", gl.t[:], gl.t[:], 0.044715, 1.0, ALU.mult, ALU.add, reads=gl.b, writes=gl.b)
        P.tt("dve", gl.t[:], gl.t[:], ps[3][:], ALU.mult, reads=gl.b + [psb[3]], writes=gl.b)
        P.act(gl.t[:], gl.t[:], AF.Tanh, scale=0.7978845608028654, reads=gl.b, writes=gl.b)
        P.ts("dve", gl.t[:], gl.t[:], 1.0, 0.5, ALU.add, ALU.mult, reads=gl.b, writes=gl.b)
        P.tt("dve", gl.t[:], gl.t[:], ps[3][:], ALU.mult, reads=gl.b + [psb[3]], writes=gl.b)
        P.tt("dve", o.yl.t[:, 4 + k, :], hh_.t[:], gl.t[:], ALU.mult, reads=hh_.b + gl.b, writes=[o.yl.b[4 + k]])


def rwkv(g, tt):
    P, o = g.P, g.o
    ps, psb = g.ps, g.psb
    cF = g.cF
    cv = g.carve
    ib = M.CONST_COLS["ident"][0]
    identb64 = g.cb.t[0:64, ib:ib + 64]
    identf = g.cf.t[0:64, ib:ib + 64]
    o1 = M.CONST_COLS["ones_1"][0]
    ones64b = g.cb.t[0:64, o1:o1 + 64]
    raw = cv("rraw", [128, TT + 1], F32)
    dd = cv("rdd", [128, TT], F32)
    twl = cv("twl", [64, TT], F32)
    al = cv("ral", [64, TT], F32)
    sgl = cv("sgl", [128, TT], BF16)
    rr = cv("rr", [64, TT], F32)
    kk_ = cv("rk", [64, TT], F32)
    vv = cv("rv", [64, TT], F32)
    vb16 = cv("rvb", [64, TT], BF16)
    ew = cv("rew", [64, TT], F32)
    cum = cv("rcum", [64, TT], F32)
    ex = cv("rex", [64, TT], F32)
    aa = cv("raa", [64, TT], F32)
    kn = cv("rkn", [64, TT], F32)
    km = cv("rkm", [64, TT], F32)
    t1 = cv("rt1", [64, TT], F32)
    bon = cv("rbon", [64, TT], F32)
    gg_ = cv("rgg", [64, TT], F32)
    rt = cv("rrt", [64, TT], BF16)
    bt = cv("rbt", [64, TT], BF16)
    at = cv("rat", [64, TT], BF16)
    kt = cv("rkt", [64, TT], BF16)
    adT = cv("radT", [64, TT], BF16)
    kdT = cv("rkdT", [64, TT], BF16)
    pc = cv("rpc", [64, NCH], F32)
    vtok = cv("rvtok", [64, NCH, 64], BF16)
    adtok = cv("radtok", [64, NCH, 64], BF16)
    kdtok = cv("rkdtok", [64, NCH, 64], BF16)
    Ya = [cv("rYa%d" % i, [64, TT], F32) for i in range(2)]
    YTa = [cv("rYTa%d" % i, [64, TT], F32) for i in range(2)]
    R = cv("rR", [64, TT], F32)
    Tb = cv("rTb", [64, TT], BF16)
    MKT = cv("rMKT", [64, TT], BF16)
    NAT = cv("rNAT", [64, TT], BF16)
    NKT = cv("rNKT", [64, TT], BF16)
    zsb = cv("rz", [64, 64], BF16)
    usb = cv("ru", [64, 64], BF16)
    ysb = cv("rysb", [64, TT], F32)

    def shift_mix(dst, dstb, src_ps, srcb, npart, tile_id, mucol, mub):
        P.copy("dve", raw.t[0:npart, 0:1], o.sh.t[0:npart, tile_id:tile_id + 1], reads=o.sh.b, writes=raw.b)
        P.copy("act", raw.t[0:npart, 1:TT + 1], src_ps, reads=srcb, writes=raw.b)
        P.copy("dve", o.sh.t[0:npart, tile_id:tile_id + 1], raw.t[0:npart, TT:TT + 1], reads=raw.b, writes=o.sh.b)
        P.tt("dve", dd.t[0:npart, :], raw.t[0:npart, 0:TT], raw.t[0:npart, 1:TT + 1], ALU.subtract, reads=raw.b, writes=dd.b)
        P.stt("dve", dst, dd.t[0:npart, :], mucol, raw.t[0:npart, 1:TT + 1], ALU.mult, ALU.add, reads=dd.b + raw.b + mub, writes=dstb)

    wm, wmb = g.wload("oin", 3)
    proj_fm(g, ps[0], psb[0], wm, wmb, 0, 64)
    mc, mb = g.c64("rwkv_mu", 24)
    shift_mix(twl.t[:], twl.b, ps[0][0:64, :], [psb[0]], 64, 24, mc, mb)
    P.act(twl.t[:], twl.t[:], AF.Tanh, reads=twl.b, writes=twl.b)
    proj_fm(g, ps[1], psb[1], wm, wmb, 64, 64)
    mc, mb = g.c64("rwkv_mu", 25)
    shift_mix(al.t[:], al.b, ps[1][0:64, :], [psb[1]], 64, 25, mc, mb)
    proj_fm(g, ps[2], psb[2], wm, wmb, 128, 128)
    mc, mb = g.colf("rwkv_mu", 13)
    shift_mix(dd.t[:], dd.b, ps[2][:], [psb[2]], 128, 26, mc, mb)
    P.act(sgl.t[:], dd.t[:], AF.Sigmoid, reads=dd.b, writes=sgl.b)

    rst = g.flags.get("rstage", 99)

    def zy(h):
        P.memset("pool", o.yr.t[:, h, :], 0.0, writes=[o.yr.b[h]])
    for h in range(8):
        wr, wrb = g.wload("oin", 0)
        proj_fm(g, ps[0], psb[0], wr, wrb, h * 64, 64)
        mc, mb = g.c64("rwkv_mu", h)
        shift_mix(rr.t[:], rr.b, ps[0][0:64, :], [psb[0]], 64, h, mc, mb)
        wk, wkb = g.wload("oin", 1)
        proj_fm(g, ps[1], psb[1], wk, wkb, h * 64, 64)
        mc, mb = g.c64("rwkv_mu", 8 + h)
        shift_mix(kk_.t[:], kk_.b, ps[1][0:64, :], [psb[1]], 64, 8 + h, mc, mb)
        wv, wvb = g.wload("oin", 2)
        proj_fm(g, ps[2], psb[2], wv, wvb, h * 64, 64)
        mc, mb = g.c64("rwkv_mu", 16 + h)
        shift_mix(vv.t[:], vv.b, ps[2][0:64, :], [psb[2]], 64, 16 + h, mc, mb)
        P.copy("act", vb16.t[:], vv.t[:], reads=vv.b, writes=vb16.b)
        P.mm(ps[3][0:64, :], o.w2.t[:, h * 64:(h + 1) * 64], twl.t[:], reads=o.w2.b + twl.b, writes=[psb[3]])
        P.act(ew.t[:], ps[3][0:64, :], AF.Exp, scale=-1.0, bias=o.dc.t[0:64, 4 + h:5 + h], reads=[psb[3]] + o.dc.b, writes=ew.b)
        P.act(ew.t[:], ew.t[:], AF.Ln, bias=1.0, reads=ew.b, writes=ew.b)
        P.act(ew.t[:], ew.t[:], AF.Exp, scale=-1.0, bias=-0.5, reads=ew.b, writes=ew.b)
        P.op("dve", lambda e_: e_.tensor_tensor_scan(out=cum.t[:], data0=cF("reset", 64), data1=ew.t[:], initial=0.0,
                                                     op0=ALU.mult, op1=ALU.add), reads=ew.b + g.cf.b, writes=cum.b)
        P.mm(ps[4][0:64, :], o.a2.t[:, h * 64:(h + 1) * 64], al.t[:], reads=o.a2.b + al.b, writes=[psb[4]])
        a0c, a0b = g.c64("rwkv_a0", h)
        P.act(aa.t[:], ps[4][0:64, :], AF.Sigmoid, bias=a0c, reads=[psb[4]] + a0b, writes=aa.b)
        P.mm(ps[5][0:64, :], o.g2b.t[:, h * 64:(h + 1) * 64], sgl.t[:], reads=o.g2b.b + sgl.b, writes=[psb[5]])
        P.copy("act", gg_.t[:], ps[5][0:64, :], reads=[psb[5]], writes=gg_.b)
        kkc, kkb = g.c64("rwkv_k_k", h)
        P.ts("dve", kn.t[:], kk_.t[:], kkc, None, ALU.mult, reads=kk_.b + kkb, writes=kn.b)
        P.act(g.sq.t[0:64, 0, :], kn.t[:], AF.Square, reads=kn.b, writes=[g.sq.b[0]])
        P.mm(ps[6][0:64, :], ones64b, g.sq.t[0:64, 0, :], reads=g.cb.b + [g.sq.b[0]], writes=[psb[6]])
        inv_rsqrt(g, t1.t[:], t1.b, ps[6][0:64, :], [psb[6]], M.EPS)
        P.tt("dve", kn.t[:], kn.t[:], t1.t[:], ALU.mult, reads=kn.b + t1.b, writes=kn.b)
        kac, kab = g.c64("rwkv_k_a", h)
        P.ts("dve", km.t[:], aa.t[:], -1.0, kac, ALU.add, ALU.mult, reads=aa.b + kab, writes=km.b)
        P.stt("dve", km.t[:], km.t[:], 1.0, kk_.t[:], ALU.add, ALU.mult, reads=km.b + kk_.b, writes=km.b)
        rkc, rkb = g.c64("rwkv_r_k", h)
        P.stt("dve", t1.t[:], rr.t[:], rkc, km.t[:], ALU.mult, ALU.mult, reads=rr.b + rkb + km.b, writes=t1.b)
        P.copy("act", g.sq.t[0:64, 1, :], t1.t[:], reads=t1.b, writes=[g.sq.b[1]])
        P.mm(ps[7][0:64, :], ones64b, g.sq.t[0:64, 1, :], reads=g.cb.b + [g.sq.b[1]], writes=[psb[7]])
        P.tt("dve", bon.t[:], ps[7][0:64, :], vv.t[:], ALU.mult, reads=[psb[7]] + vv.b, writes=bon.b)
        P.stt("dve", t1.t[:], kn.t[:], -1.0, aa.t[:], ALU.mult, ALU.mult, reads=kn.b + aa.b, writes=t1.b)
        P.act(ex.t[:], cum.t[:], AF.Exp, scale=-1.0, reads=cum.b, writes=ex.b)
        P.tt("dve", rt.t[:], rr.t[:], ex.t[:], ALU.mult, reads=rr.b + ex.b, writes=rt.b)
        cum3 = cum.t[:].rearrange("p (n c) -> p n c", c=C)
        P.act(pc.t[:], cum3[:, :, C - 1], AF.Exp, scale=-1.0, reads=cum.b, writes=pc.b)
        P.tt("dve", ex.t[:], cum.t[:], ew.t[:], ALU.subtract, reads=cum.b + ew.b, writes=ex.b)
        P.act(ex.t[:], ex.t[:], AF.Exp, scale=-1.0, reads=ex.b, writes=ex.b)
        P.tt("dve", bt.t[:], kn.t[:], ex.t[:], ALU.mult, reads=kn.b + ex.b, writes=bt.b)
        P.act(ex.t[:], cum.t[:], AF.Exp, reads=cum.b, writes=ex.b)
        P.tt("dve", at.t[:], t1.t[:], ex.t[:], ALU.mult, reads=t1.b + ex.b, writes=at.b)
        P.tt("dve", kt.t[:], km.t[:], ex.t[:], ALU.mult, reads=km.b + ex.b, writes=kt.b)
        ex3 = ex.t[:].rearrange("p (n c) -> p n c", c=C)
        P.tt("dve", ex3, cum3[:, :, C - 1:C].broadcast_to([64, NCH, C]), cum3, ALU.subtract, reads=cum.b, writes=ex.b)
        P.act(ex.t[:], ex.t[:], AF.Exp, scale=-1.0, reads=ex.b, writes=ex.b)
        P.tt("dve", adT.t[:], t1.t[:], ex.t[:], ALU.mult, reads=t1.b + ex.b, writes=adT.b)
        P.tt("dve", kdT.t[:], km.t[:], ex.t[:], ALU.mult, reads=km.b + ex.b, writes=kdT.b)
        if rst <= 1:
            zy(h)
            continue
        for src, dst, bank in ((vb16, vtok, 0), (adT, adtok, 1), (kdT, kdtok, 2)):
            for n in range(NCH):
                sl = slice(n * 64, (n + 1) * 64)
                P.mm(ps[bank][0:64, sl], src.t[:, sl], identb64, reads=src.b + g.cb.b, writes=[psb[bank]])
            P.copy("act" if bank == 1 else "dve", dst.t[:].rearrange("p n c -> p (n c)"), ps[bank][0:64, :], reads=[psb[bank]], writes=dst.b)
        if rst <= 2:
            zy(h)
            continue
        for n in range(NCH):
            sl = slice(n * 64, (n + 1) * 64)
            P.mm(ps[3][0:64, sl], at.t[:, sl], bt.t[:, sl], reads=at.b + bt.b, writes=[psb[3]])
            P.mm(ps[4][0:64, sl], bt.t[:, sl], at.t[:, sl], reads=at.b + bt.b, writes=[psb[4]])
            P.mm(ps[5][0:64, sl], kt.t[:, sl], bt.t[:, sl], reads=kt.b + bt.b, writes=[psb[5]])
            P.mm(ps[6][0:64, sl], at.t[:, sl], rt.t[:, sl], reads=at.b + rt.b, writes=[psb[6]])
            P.mm(ps[7][0:64, sl], kt.t[:, sl], rt.t[:, sl], reads=kt.b + rt.b, writes=[psb[7]])
        P.tt("dve", Ya[0].t[:], ps[3][0:64, :], cF("m_strict", 64), ALU.mult, reads=[psb[3]] + g.cf.b, writes=Ya[0].b)
        P.tt("dve", YTa[0].t[:], ps[4][0:64, :], cF("m_strict_T", 64), ALU.mult, reads=[psb[4]] + g.cf.b, writes=YTa[0].b)
        P.tt("dve", MKT.t[:], ps[5][0:64, :], cF("m_strict", 64), ALU.mult, reads=[psb[5]] + g.cf.b, writes=MKT.b)
        P.tt("dve", NAT.t[:], ps[6][0:64, :], cF("m_incl", 64), ALU.mult, reads=[psb[6]] + g.cf.b, writes=NAT.b)
        P.tt("dve", NKT.t[:], ps[7][0:64, :], cF("m_incl", 64), ALU.mult, reads=[psb[7]] + g.cf.b, writes=NKT.b)
        if rst <= 3:
            zy(h)
            continue
        P.tt("dve", R.t[:], Ya[0].t[:], cF("ident64x", 64), ALU.add, reads=Ya[0].b + g.cf.b, writes=R.b)
        cur = 0
        for lvl in range(1, 6):
            nxt = 1 - cur
            for n in range(NCH):
                sl = slice(n * 64, (n + 1) * 64)
                P.mm(ps[0][0:64, sl], YTa[cur].t[:, sl], Ya[cur].t[:, sl], reads=YTa[cur].b + Ya[cur].b, writes=[psb[0]])
                P.mm(ps[1][0:64, sl], Ya[cur].t[:, sl], YTa[cur].t[:, sl], reads=YTa[cur].b + Ya[cur].b, writes=[psb[1]])
            if lvl < 5:
                P.copy("act", Ya[nxt].t[:], ps[0][0:64, :], reads=[psb[0]], writes=Ya[nxt].b)
            P.copy("dve", YTa[nxt].t[:], ps[1][0:64, :], reads=[psb[1]], writes=YTa[nxt].b)
            for n in range(NCH):
                sl = slice(n * 64, (n + 1) * 64)
                P.mm(ps[2][0:64, sl], YTa[nxt].t[:, sl], R.t[:, sl], reads=YTa[nxt].b + R.b, writes=[psb[2]])
            P.tt("dve", R.t[:], R.t[:], ps[2][0:64, :], ALU.add, reads=R.b + [psb[2]], writes=R.b)
            cur = nxt
        P.copy("act", Tb.t[:], R.t[:], reads=R.b, writes=Tb.b)
        if rst <= 4:
            zy(h)
            continue
        po, pob = ps[7], psb[7]
        for n in range(NCH):
            sl = slice(n * 64, (n + 1) * 64)
            P.mm(ps[3][0:64, 0:64], bt.t[:, sl], o.Ab.t[:, h, :], start=True, stop=False, reads=bt.b + [o.Ab.b[h]], writes=[psb[3]])
            P.mm(ps[3][0:64, 0:64], MKT.t[:, sl], vtok.t[:, n, :], start=False, stop=True, reads=MKT.b + vtok.b, writes=[psb[3]])
            P.copy("act", zsb.t[:], ps[3][0:64, 0:64], reads=[psb[3]], writes=zsb.b)
            P.mm(ps[4][0:64, 0:64], Tb.t[:, sl], zsb.t[:], reads=Tb.b + zsb.b, writes=[psb[4]])
            P.copy("dve", usb.t[:], ps[4][0:64, 0:64], reads=[psb[4]], writes=usb.b)
            P.mm(po[0:64, sl], o.Ab.t[:, h, :], rt.t[:, sl], start=True, stop=False, reads=[o.Ab.b[h]] + rt.b, writes=[pob])
            P.mm(po[0:64, sl], usb.t[:], NAT.t[:, sl], start=False, stop=False, reads=usb.b + NAT.b, writes=[pob])
            P.mm(po[0:64, sl], vtok.t[:, n, :], NKT.t[:, sl], start=False, stop=True, reads=vtok.b + NKT.b, writes=[pob])
            P.mm(ps[5][0:64, 0:64], adtok.t[:, n, :], usb.t[:], start=True, stop=False, reads=adtok.b + usb.b, writes=[psb[5]])
            P.mm(ps[5][0:64, 0:64], kdtok.t[:, n, :], vtok.t[:, n, :], start=False, stop=True, reads=kdtok.b + vtok.b, writes=[psb[5]])
            P.stt("dve", o.A.t[:, h, :], o.A.t[:, h, :], pc.t[:, n:n + 1], ps[5][0:64, 0:64], ALU.mult, ALU.add,
                  reads=[o.A.b[h], psb[5]] + pc.b, writes=[o.A.b[h]])
            P.copy("act", o.Ab.t[:, h, :], o.A.t[:, h, :], reads=[o.A.b[h]], writes=[o.Ab.b[h]])
        if rst <= 5:
            zy(h)
            continue
        om = M.CONST_COLS["bd_64m"][0]
        ones64m = g.cb.t[0:64, om:om + 64]
        P.copy("act", ysb.t[:], po[0:64, :], reads=[pob], writes=ysb.b)
        P.copy("dve", g.sq.t[0:64, 0, :], po[0:64, :], reads=[pob], writes=[g.sq.b[0]])
        P.mm(ps[6][0:64, :], ones64m, g.sq.t[0:64, 0, :], reads=g.cb.b + [g.sq.b[0]], writes=[psb[6]])
        P.tt("dve", ysb.t[:], ysb.t[:], ps[6][0:64, :], ALU.subtract, reads=ysb.b + [psb[6]], writes=ysb.b)
        P.act(g.sq.t[0:64, 1, :], ysb.t[:], AF.Square, reads=ysb.b, writes=[g.sq.b[1]])
        P.mm(ps[6][0:64, :], ones64m, g.sq.t[0:64, 1, :], reads=g.cb.b + [g.sq.b[1]], writes=[psb[6]])
        inv_rsqrt(g, t1.t[:], t1.b, ps[6][0:64, :], [psb[6]], RWKV_GN_EPS)
        lwc, lwb = g.c64("rwkv_ln_w", h)
        lbc, lbb = g.c64("rwkv_ln_b", h)
        P.stt("dve", ysb.t[:], ysb.t[:], lwc, t1.t[:], ALU.mult, ALU.mult, reads=ysb.b + lwb + t1.b, writes=ysb.b)
        P.stt("dve", ysb.t[:], ysb.t[:], lbc, bon.t[:], ALU.add, ALU.add, reads=ysb.b + lbb + bon.b, writes=ysb.b)
        P.tt("dve", o.yr.t[:, h, :], ysb.t[:], gg_.t[:], ALU.mult, reads=ysb.b + gg_.b, writes=[o.yr.b[h]])


def odd_mixer(g, tt):
    P, o = g.P, g.o
    g.phase()
    g.rmsnorm(g.xT, 1, 2, "pre")
    if g.flags.get("lru", True):
        lru(g, tt)
    else:
        for k in range(4):
            P.memset("pool", o.yl.t[:, 4 + k, :], 0.0, writes=[o.yl.b[4 + k]])
    g.phase()
    if g.flags.get("rwkv", True):
        rwkv(g, tt)
    else:
        for h in range(8):
            P.memset("pool", o.yr.t[:, h, :], 0.0, writes=[o.yr.b[h]])
    for half in range(2):
        wr, wrb = g.wload("oout_r", half)
        wl, wlb = g.wload("oout_l", half)
        for nn in range(4):
            n = half * 4 + nn
            pb, pbb = g.ps[n % 2], g.psb[n % 2]
            for h in range(8):
                P.mm(pb[:], wr[:, h, nn * 128:(nn + 1) * 128], o.yr.t[:, h, :], start=(h == 0), stop=False,
                     reads=wrb + [o.yr.b[h]], writes=[pbb])
            for k in range(4):
                P.mm(pb[:], wl[:, k, nn * 128:(nn + 1) * 128], o.yl.t[:, 4 + k, :], start=False, stop=(k == 3),
                     reads=wlb + [o.yl.b[4 + k]], writes=[pbb])
            P.copy("act", g.f.t[:, n, :], pb[:], reads=[pbb], writes=[g.f.b[n]])
    g.rmsnorm(g.f, 1, 3, "post", coef=1.0)


_PARAM_NAMES = ["norm_w", "ffn_w_gate", "ffn_w_up", "ffn_w_down", "even_w_in", "even_w_out", "gla_lora_w2", "gla_lora_b",
                "gla_norm", "gdn_conv", "gdn_a_log", "gdn_dt_bias", "gdn_norm", "odd_w_in", "odd_w_out", "rwkv_mu", "rwkv_w0",
                "rwkv_w2", "rwkv_a0", "rwkv_a2", "rwkv_g2", "rwkv_k_k", "rwkv_k_a", "rwkv_r_k", "rwkv_ln_w", "rwkv_ln_b",
                "lru_conv_w", "lru_conv_b", "lru_wa", "lru_ba", "lru_wx", "lru_bx", "lru_lambda"]


def kernel(**inputs):
    x = np.ascontiguousarray(np.asarray(inputs["x"], dtype=np.float32))
    B, T, _ = x.shape
    nc, g = build(T, 2, {})
    consts = make_consts()
    params = {k: np.ascontiguousarray(np.asarray(inputs[k], dtype=np.float32)) for k in _PARAM_NAMES}
    n_cores = 8
    in_maps = []
    for c in range(n_cores):
        m = dict(params)
        m["x"] = np.ascontiguousarray(x[c % B])
        m["consts"] = consts
        in_maps.append(m)
    res = run_bass_kernel_spmd(nc, in_maps, core_ids=list(range(n_cores)))
    out = np.stack([np.asarray(res.results[b]["out"], dtype=np.float32) for b in range(B)], 0)
    return out
```

```python
import numpy as np
import concourse.bass as bass
import concourse.mybir as mybir
from concourse.bass_utils import run_bass_kernel_spmd

F32 = mybir.dt.float32
BF16 = mybir.dt.bfloat16
AF = mybir.ActivationFunctionType
ALU = mybir.AluOpType
AX = mybir.AxisListType

USE_DRAIN = False
EPOCH = 20000
N_EPOCH = 12


class Buf:
    __slots__ = ("name", "w", "r", "excl")

    def __init__(self, name="", excl=False):
        self.name = name
        self.excl = excl
        self.w = None
        self.r = []


class Prog:
    ENGS = ("pe", "dve", "act", "pool", "sp")

    def __init__(self, nc, same_engine_sync=True):
        self.nc = nc
        self.ops = {e: [] for e in self.ENGS}
        self.cnt = {e: 0 for e in self.ENGS}
        self.waited = {}
        self.same_engine_sync = same_engine_sync
        self.dma_cnt = {}
        self.sems = {}
        self._ctx = []
        self.n_wait = 0
        self.barrier_streams = set(["c0"])
        self.slow_map = {}
        self.slow_pe = set()
        self.last_drain = None

    def _sem(self, key):
        if key not in self.sems:
            cm = self.nc.semaphore("s_%s_%s" % key if isinstance(key, tuple) else str(key))
            h = cm.__enter__()
            self._ctx.append(cm)
            self.sems[key] = h
        return self.sems[key]

    def _tok(self, eng):
        i = self.cnt[eng]
        self.cnt[eng] = i + 1
        return ((eng, i // EPOCH), (i % EPOCH) + 1)

    def _need(self, eng, tok):
        if tok is None:
            return None
        key, val = tok
        if key[0] == "pe" and eng != "pe":
            tv = (key[1], val)
            if tv in self.slow_map:
                key, val = self.slow_map[tv]
            elif tv in self.slow_pe:
                dtok = self.op("pe", lambda e: e.drain(), (), ())
                for t in list(self.slow_pe):
                    if t <= tv:
                        self.slow_map[t] = dtok
                        self.slow_pe.discard(t)
                key, val = dtok
        if key[0] == "dma":
            val = self.dma_cnt[key]
        if key[0] == eng and (eng == "pe" or not self.same_engine_sync):
            return None
        if self.waited.get((eng, key), 0) >= val:
            return None
        self.waited[(eng, key)] = val
        return (key, val)

    def op(self, eng, fn, reads=(), writes=()):
        waits = []
        for b in reads:
            w = self._need(eng, b.w)
            if w:
                waits.append(w)
            if b.excl:
                for t in b.r:
                    if t[0][0] != eng:
                        w = self._need(eng, t)
                        if w:
                            waits.append(w)
        for b in writes:
            w = self._need(eng, b.w)
            if w:
                waits.append(w)
            for t in b.r:
                w = self._need(eng, t)
                if w:
                    waits.append(w)
        mx = {}
        for k, v in waits:
            mx[k] = max(mx.get(k, 0), v)
        tok = self._tok(eng)
        for b in reads:
            b.r.append(tok)
        for b in writes:
            b.w = tok
            b.r = []
        self._sem(tok[0])
        for k in mx:
            self._sem(k)
        self.n_wait += len(mx)
        self.ops[eng].append((list(mx.items()), fn, tok[0], 1))
        return tok

    def dma(self, eng, out_ap, in_ap, stream, reads=(), writes=(), **kw):
        waits = []
        for b in reads:
            w = self._need(eng, b.w)
            if w:
                waits.append(w)
        for b in writes:
            w = self._need(eng, b.w)
            if w:
                waits.append(w)
            for t in b.r:
                w = self._need(eng, t)
                if w:
                    waits.append(w)
        mx = {}
        for k, v in waits:
            if k == ("dma", stream) and stream in self.barrier_streams:
                continue
            mx[k] = max(mx.get(k, 0), v)
        key = ("dma", stream)
        c = self.dma_cnt.get(key, 0) + 16
        self.dma_cnt[key] = c
        tok = (key, c)
        for b in reads:
            b.r.append(tok)
        for b in writes:
            b.w = tok
            b.r = []
        self._sem(key)
        for k in mx:
            self._sem(k)

        def fn(e, out_ap=out_ap, in_ap=in_ap, kw=kw):
            return e.dma_start(out=out_ap, in_=in_ap, **kw)
        self.ops[eng].append((list(mx.items()), fn, key, 16))
        return tok

    def final_wait(self, eng, toks):
        for tok in toks:
            w = self._need(eng, tok)
            if w:
                self._sem(w[0])
                self.ops[eng].append(([w], None, None, 0))

    def mm(self, out, lhsT, rhs, start=True, stop=True, reads=(), writes=()):
        tok = self.op("pe", lambda e: e.matmul(out, lhsT, rhs, start=start, stop=stop), reads, writes)
        if USE_DRAIN and lhsT.dtype == F32:
            self.slow_pe.add((tok[0][1], tok[1]))
        return tok

    def transpose(self, out, in_, ident, reads=(), writes=()):
        return self.op("pe", lambda e: e.transpose(out, in_, ident), reads, writes)

    def act(self, out, in_, func, reads=(), writes=(), eng="act", **kw):
        return self.op(eng, lambda e: e.activation(out=out, in_=in_, func=func, **kw), reads, writes)

    def tt(self, eng, out, in0, in1, op, reads=(), writes=()):
        return self.op(eng, lambda e: e.tensor_tensor(out=out, in0=in0, in1=in1, op=op), reads, writes)

    def ts(self, eng, out, in0, s1, s2, op0, op1=None, reads=(), writes=(), **kw):
        if op1 is None:
            return self.op(eng, lambda e: e.tensor_scalar(out=out, in0=in0, scalar1=s1, scalar2=s2, op0=op0, **kw), reads, writes)
        return self.op(eng, lambda e: e.tensor_scalar(out=out, in0=in0, scalar1=s1, scalar2=s2, op0=op0, op1=op1, **kw), reads, writes)

    def stt(self, eng, out, in0, scalar, in1, op0, op1, reads=(), writes=()):
        return self.op(eng, lambda e: e.scalar_tensor_tensor(out=out, in0=in0, scalar=scalar, in1=in1, op0=op0, op1=op1), reads, writes)

    def copy(self, eng, out, in_, reads=(), writes=()):
        if eng == "act":
            return self.op(eng, lambda e: e.copy(out=out, in_=in_), reads, writes)
        return self.op(eng, lambda e: e.tensor_copy(out=out, in_=in_), reads, writes)

    def memset(self, eng, ap, val, writes=()):
        return self.op(eng, lambda e: e.memset(ap, val), (), writes)

    def emit(self):
        nc = self.nc
        sems = self.sems
        ops = self.ops
        with nc.Block() as block:
            def run(e, lst):
                for waits, fn, inckey, incv in lst:
                    for k, v in waits:
                        if k[0] == "dma" and k[1] in self.barrier_streams:
                            v = self.dma_cnt[k]
                        e.wait_ge(sems[k], v)
                    if fn is not None:
                        ins = fn(e)
                        ins.then_inc(sems[inckey], incv)

            @block.tensor
            def _(e):
                run(e, ops["pe"])

            @block.vector
            def _(e):
                run(e, ops["dve"])

            @block.scalar
            def _(e):
                run(e, ops["act"])

            @block.gpsimd
            def _(e):
                run(e, ops["pool"])

            @block.sync
            def _(e):
                run(e, ops["sp"])

    def close(self):
        for cm in reversed(self._ctx):
            cm.__exit__(None, None, None)
        self._ctx = []


class Alloc:
    def __init__(self, nc):
        self.nc = nc
        self._ctx = []

    def sb(self, name, shape, dt):
        cm = self.nc.sbuf_tensor(name, list(shape), dt)
        t = cm.__enter__()
        self._ctx.append(cm)
        return t

    def ps(self, name, shape, dt=F32):
        cm = self.nc.psum_tensor(name, list(shape), dt)
        t = cm.__enter__()
        self._ctx.append(cm)
        return t

    def close(self):
        for cm in reversed(self._ctx):
            cm.__exit__(None, None, None)
        self._ctx = []


D = 1024
DFF = 2816
NKC = 8
NM = 22
TT = 512
C = 64
NCH = TT // C
EPS = 1e-6
EVEN_IN = 3608
ODD_IN = 2816


class Tl:
    def __init__(self, A, name, shape, dt, nb=1):
        self.t = A.sb(name, shape, dt)
        self.b = [Buf(name + str(i)) for i in range(nb)]


CONST_COLS = {}


def make_consts():
    cols = []
    off = 0

    def add(name, arr):
        nonlocal off
        arr = np.asarray(arr, np.float32)
        assert arr.shape[0] == 128
        CONST_COLS[name] = (off, arr.shape[1])
        cols.append(arr)
        off += arr.shape[1]

    add("ident", np.eye(128))
    add("ones_d", np.full((128, 128), 1.0 / D))
    add("ones_128m", np.full((128, 128), 1.0 / 128))
    add("ones_1", np.ones((128, 128)))
    bd = np.zeros((128, 128)); bd[:64, :64] = 1; bd[64:, 64:] = 1
    add("bd_1", bd)
    add("bd_64m", bd / 64.0)
    s = np.arange(64)[:, None]; c = np.arange(64)[None, :]
    incl = (s <= c).astype(np.float32)
    strict = (s < c).astype(np.float32)
    add("m_incl", np.tile(np.concatenate([incl, incl], 0), (1, NCH)))
    add("m_strict", np.tile(np.concatenate([strict, strict], 0), (1, NCH)))
    add("m_incl_T", np.tile(np.concatenate([incl.T, incl.T], 0), (1, NCH)))
    add("m_strict_T", np.tile(np.concatenate([strict.T, strict.T], 0), (1, NCH)))
    rst = np.ones((128, TT)); rst[:, ::C] = 0.0
    add("reset", rst)
    add("ident64x", np.tile(np.concatenate([np.eye(64), np.eye(64)], 0), (1, NCH)))
    return np.concatenate(cols, 1)


class Ctx:
    pass


def build(T, L=2, flags=None, dbg=()):
    flags = flags or {}
    NT = T // TT
    nc = bass.Bass("TRN2", target_bir_lowering=False)
    A = Alloc(nc)
    P = Prog(nc, same_engine_sync=True)
    g = Ctx()
    g.nc, g.A, g.P, g.T, g.NT, g.L, g.flags = nc, A, P, T, NT, L, flags

    def din(name, shape):
        return nc.dram_tensor(name, list(shape), F32, kind="ExternalInput").ap()

    consts_np = make_consts()
    NCC = consts_np.shape[1]
    I = {}
    I["x"] = din("x", [T, D])
    I["consts"] = din("consts", [128, NCC])
    I["norm_w"] = din("norm_w", [2, 6, D])
    I["ffn_w_gate"] = din("ffn_w_gate", [2, 2, D, DFF])
    I["ffn_w_up"] = din("ffn_w_up", [2, 2, D, DFF])
    I["ffn_w_down"] = din("ffn_w_down", [2, 2, DFF, D])
    I["even_w_in"] = din("even_w_in", [1, D, EVEN_IN])
    I["even_w_out"] = din("even_w_out", [1, D, D])
    I["gla_lora_w2"] = din("gla_lora_w2", [1, 16, 256])
    I["gla_lora_b"] = din("gla_lora_b", [1, 256])
    I["gla_norm"] = din("gla_norm", [1, 128])
    I["gdn_conv"] = din("gdn_conv", [1, 4, 1536])
    I["gdn_a_log"] = din("gdn_a_log", [1, 4])
    I["gdn_dt_bias"] = din("gdn_dt_bias", [1, 4])
    I["gdn_norm"] = din("gdn_norm", [1, 128])
    I["odd_w_in"] = din("odd_w_in", [1, D, ODD_IN])
    I["odd_w_out"] = din("odd_w_out", [1, D, D])
    I["rwkv_mu"] = din("rwkv_mu", [1, 1792])
    for nm in ["rwkv_w0", "rwkv_a0", "rwkv_k_k", "rwkv_k_a", "rwkv_ln_w", "rwkv_ln_b",
               "lru_conv_b", "lru_ba", "lru_bx", "lru_lambda"]:
        I[nm] = din(nm, [1, 512])
    I["rwkv_w2"] = din("rwkv_w2", [1, 64, 512])
    I["rwkv_a2"] = din("rwkv_a2", [1, 64, 512])
    I["rwkv_g2"] = din("rwkv_g2", [1, 128, 512])
    I["rwkv_r_k"] = din("rwkv_r_k", [1, 8, 64])
    I["lru_conv_w"] = din("lru_conv_w", [1, 4, 512])
    I["lru_wa"] = din("lru_wa", [1, 8, 64, 64])
    I["lru_wx"] = din("lru_wx", [1, 8, 64, 64])
    g.I = I
    g.out = nc.dram_tensor("out", [T, D], F32, kind="ExternalOutput").ap()
    g.dbg = {}
    for nm, shp in dbg:
        g.dbg[nm] = nc.dram_tensor("dbg_" + nm, list(shp), F32, kind="ExternalOutput").ap()
    g.dbg_toks = []

    g.cf = Tl(A, "cf", [128, NCC], F32)
    P.dma("sp", g.cf.t[:], I["consts"], "c0", writes=g.cf.b)
    g.cb = Tl(A, "cb", [128, 768], BF16)
    P.copy("dve", g.cb.t[:], g.cf.t[:, 0:768], reads=g.cf.b, writes=g.cb.b)

    def cF(name, rows=128, c0=0, n=None):
        o, w = CONST_COLS[name]
        n = w if n is None else n
        return g.cf.t[0:rows, o + c0:o + c0 + n]

    def cB(name, rows=128, c0=0, n=None):
        o, w = CONST_COLS[name]
        n = w if n is None else n
        return g.cb.t[0:rows, o + c0:o + c0 + n]
    g.cF, g.cB = cF, cB

    g.ps = [A.ps("ps%d" % i, [128, 512]) for i in range(8)]
    g.psb = [Buf("ps%d" % i, excl=True) for i in range(8)]

    rows = []
    rows.append(("norm_w", I["norm_w"].rearrange("l i (k p) -> (l i k) p", p=128)))
    stage1 = rows
    rows2 = []
    rows2.append(("gla_lora_b", I["gla_lora_b"].rearrange("o (k p) -> (o k) p", p=128)))
    rows2.append(("gla_norm", I["gla_norm"]))
    rows2.append(("gdn_norm", I["gdn_norm"]))
    rows2.append(("gdn_conv", I["gdn_conv"].rearrange("o j (k p) -> (o j k) p", p=128)))
    rows2.append(("rwkv_mu", I["rwkv_mu"].rearrange("o (k p) -> (o k) p", p=128)))
    for nm in ["rwkv_w0", "rwkv_a0", "rwkv_k_k", "rwkv_k_a", "rwkv_ln_w", "rwkv_ln_b",
               "lru_conv_b", "lru_ba", "lru_bx", "lru_lambda"]:
        rows2.append((nm, I[nm].rearrange("o (k p) -> (o k) p", p=128)))
    rows2.append(("rwkv_r_k", I["rwkv_r_k"].rearrange("o (k a) n -> (o k) (a n)", a=2)))
    rows2.append(("lru_conv_w", I["lru_conv_w"].rearrange("o j (k p) -> (o j k) p", p=128)))
    g.col = {}
    for si, rws in enumerate([stage1, rows2]):
        st = Tl(A, "pst%d" % si, [128, 128], F32)
        P.memset("pool", st.t[:], 0.0, writes=st.b)
        r0 = 0
        for nm, ap in rws:
            r = ap.shape[0]
            P.dma("sp", st.t[r0:r0 + r, :], ap, "c0", writes=st.b)
            g.col[nm] = (si, r0, r)
            r0 += r
        assert r0 <= 128, r0
        ct = Tl(A, "pcol%d" % si, [128, 128], F32)
        P.mm(g.ps[7][:, 0:128], st.t[:], cF("ident"), reads=st.b + g.cf.b, writes=[g.psb[7]])
        P.copy("dve", ct.t[:], g.ps[7][:, 0:128], reads=[g.psb[7]], writes=ct.b)
        if si == 0:
            g.colt0 = ct
        else:
            g.colt1 = ct

    rows64 = [("gla_lora_b", I["gla_lora_b"].rearrange("o (k p) -> (o k) p", p=64))]
    for nm in ["rwkv_w0", "rwkv_a0", "rwkv_k_k", "rwkv_k_a", "rwkv_ln_w", "rwkv_ln_b"]:
        rows64.append((nm, I[nm].rearrange("o (k p) -> (o k) p", p=64)))
    rows64.append(("rwkv_r_k", I["rwkv_r_k"].rearrange("o k n -> (o k) n")))
    rows64.append(("rwkv_mu", I["rwkv_mu"].rearrange("o (k p) -> (o k) p", p=64)))
    st = Tl(A, "pst64", [128, 64], F32)
    P.memset("pool", st.t[:], 0.0, writes=st.b)
    g.col64 = {}
    r0 = 0
    for nm, ap in rows64:
        r = ap.shape[0]
        P.dma("sp", st.t[r0:r0 + r, :], ap, "c0", writes=st.b)
        g.col64[nm] = (r0, r)
        r0 += r
    assert r0 <= 128
    g.colt64 = Tl(A, "pcol64", [64, 128], F32)
    P.mm(g.ps[7][0:64, 0:128], st.t[:], cF("ident"), reads=st.b + g.cf.b, writes=[g.psb[7]])
    P.copy("dve", g.colt64.t[:], g.ps[7][0:64, 0:128], reads=[g.psb[7]], writes=g.colt64.b)

    def c64(name, idx=0):
        r0, r = g.col64[name]
        return g.colt64.t[:, r0 + idx:r0 + idx + 1], g.colt64.b
    g.c64 = c64

    def col(name, idx=0):
        si, r0, r = g.col[name]
        ct = g.colt0 if si == 0 else g.colt1
        return ct.t[:, r0 + idx:r0 + idx + 1], ct.b
    g.colf = col

    g.xT = Tl(A, "xT", [128, NKC, TT], F32, nb=NKC)
    g.hh = Tl(A, "hh", [128, NKC, TT], BF16, nb=NKC)
    ARENA = 62 * 1024
    g.arena = A.sb("arena", [128, ARENA // 2], BF16)
    g.ar_off = 0
    g.ar_bufs = []
    g.ar_tok = None
    g.fscr = Tl(A, "fscr", [128, 2], F32)

    def phase():
        old = g.ar_bufs
        g.ar_tok = P.op("dve", lambda e: e.memset(g.fscr.t[:, 0:1], 0.0), reads=(), writes=old + g.fscr.b)
        g.ar_bufs = []
        g.ar_off = 0

    def carve(name, shape, dt, nb=1):
        nbytes = int(np.prod(shape[1:])) * (4 if dt == F32 else 2)
        nbytes = (nbytes + 63) // 64 * 64
        assert g.ar_off + nbytes <= ARENA, (name, g.ar_off, nbytes)
        v = g.arena[0:shape[0], g.ar_off // 2:(g.ar_off + nbytes) // 2]
        g.ar_off += nbytes
        if dt == F32:
            v = v.bitcast(F32)
        n_el = int(np.prod(shape[1:]))
        v = v[:, 0:n_el]
        if len(shape) == 3:
            v = v.rearrange("p (a b) -> p a b", a=shape[1])
        t = Ctx()
        t.t = v
        t.b = [Buf(name + str(i)) for i in range(nb)]
        for b in t.b:
            b.w = g.ar_tok
        g.ar_bufs.extend(t.b)
        return t
    g.phase, g.carve = phase, carve
    g.f = Tl(A, "f", [128, NKC, TT], F32, nb=NKC)
    g.sq = Tl(A, "sq", [128, NKC, TT], BF16, nb=NKC)
    g.rstd = Tl(A, "rstd", [128, TT], F32)
    g.tmp = Tl(A, "tmp", [128, 2, TT], F32, nb=2)
    g.gsb = Tl(A, "gsb", [128, 2, TT], F32, nb=2)

    if flags.get("even", True) and L >= 1:
        setup_even(g)
    if flags.get("odd", True) and L >= 2:
        setup_odd(g)

    SLOT = 4096
    class _V:
        pass
    g.stg = []
    phase()
    _xs = carve("prep_stg", [128, 4, D], F32)
    for tl in (g.f, _xs):
        v = _V(); v.t = tl.t[:].rearrange("p a c -> p (a c)"); v.b = tl.b
        g.stg.append(v)
    g.cst = []
    for tl in (g.hh, g.sq):
        v = _V(); v.t = tl.t[:].rearrange("p a c -> p (a c)"); v.b = tl.b
        g.cst.append(v)
    g.prep_i = 0
    cast_eng = ["dve", "pool", "act"]

    def prep(src3, dst3, dbuf):
        np_, a, c = src3.shape[0], src3.shape[1], src3.shape[2]
        assert a * c <= SLOT
        i = g.prep_i
        g.prep_i += 1
        s = g.stg[i % 2]
        d = g.cst[i % 2]
        sv = s.t[0:np_, 0:a * c].rearrange("p (a c) -> p a c", a=a)
        dv = d.t[0:np_, 0:a * c].rearrange("p (a c) -> p a c", a=a)
        P.dma("sp", sv, src3, "stg%d" % (i % 2), writes=s.b)
        P.copy(cast_eng[i % 3], dv, sv, reads=s.b, writes=d.b)
        P.dma("act", dst3, dv, "cst%d" % (i % 2), reads=d.b, writes=[dbuf])

    def scratch(name, shape):
        return nc.dram_tensor(name, list(shape), BF16, kind="Internal").ap()

    g.W = {}

    def prep_matrix(name, src2d, pieces, p=128):
        K = src2d.shape[0]
        kc = K // p
        lst = []
        for pi, (c0, c1) in enumerate(pieces):
            w = c1 - c0
            sc = scratch("%s_%d" % (name, pi), [p, kc, w])
            bl = []
            src3 = src2d[:, c0:c1].rearrange("(k p) c -> p k c", p=p)
            kstep = max(1, SLOT // w)
            k0 = 0
            while k0 < kc:
                k1 = min(kc, k0 + kstep)
                b = Buf(name)
                bl.append(b)
                prep(src3[:, k0:k1, :], sc[:, k0:k1, :], b)
                k0 = k1
            lst.append((sc, bl))
        g.W[name] = lst

    g.prep_matrix = prep_matrix

    for l in range(L):
        for f in range(2):
            if flags.get("ffn", True):
                lst = []
                for grp in range(11):
                    sc = scratch("gu%d%d_%d" % (l, f, grp), [128, NKC, 512])
                    bl = []
                    for gi_, nm in enumerate(("ffn_w_gate", "ffn_w_up")):
                        b = Buf("gu")
                        bl.append(b)
                        src3 = I[nm][l, f][:, grp * 256:(grp + 1) * 256].rearrange("(k p) c -> p k c", p=128)
                        prep(src3, sc[:, :, gi_ * 256:(gi_ + 1) * 256], b)
                    lst.append((sc, bl))
                g.W["gu%d%d" % (l, f)] = lst
                prep_matrix("d%d%d" % (l, f), I["ffn_w_down"][l, f], [(i * 128, (i + 1) * 128) for i in range(8)])

    NR = 4
    RS = 4096
    g.ring = [Tl(A, "ring%d" % i, [128, RS], BF16) for i in range(NR)]
    g.ring_i = 0

    def wload(name, pi):
        sc, b = g.W[name][pi]
        np_, kc, w = sc.shape[0], sc.shape[1], sc.shape[2]
        assert kc * w <= RS
        i = g.ring_i
        g.ring_i += 1
        slot = g.ring[i % NR]
        v = slot.t[0:np_, 0:kc * w].rearrange("p (k c) -> p k c", k=kc)
        P.dma("sp", v, sc, "ring%d" % (i % NR), reads=b, writes=slot.b)
        return v, slot.b
    g.wload = wload

    def rmsnorm(src, l, i, mode, coef=1.0):
        for k in range(NKC):
            en = ("act", "dve", "pool", "act", "dve", "act", "pool", "dve")[k]
            if en == "act":
                P.act(g.sq.t[:, k, :], src.t[:, k, :], AF.Square, reads=[src.b[k]], writes=[g.sq.b[k]])
            else:
                P.tt(en, g.sq.t[:, k, :], src.t[:, k, :], src.t[:, k, :], ALU.mult, reads=[src.b[k]], writes=[g.sq.b[k]])
        for k in range(NKC):
            P.mm(g.ps[6][:], g.cb.t[:, CONST_COLS["ones_d"][0]:CONST_COLS["ones_d"][0] + 128], g.sq.t[:, k, :],
                 start=(k == 0), stop=(k == NKC - 1), reads=[g.sq.b[k]] + g.cb.b, writes=[g.psb[6]])
        P.act(g.rstd.t[:], g.ps[6][:], AF.Ln, bias=EPS, reads=[g.psb[6]], writes=g.rstd.b)
        lnc = float(np.log(coef)) if (mode == "post" and coef != 1.0) else 0.0
        P.act(g.rstd.t[:], g.rstd.t[:], AF.Exp, scale=-0.5, bias=lnc, reads=g.rstd.b, writes=g.rstd.b)
        for k in range(NKC):
            wc, wb = col("norm_w", (l * 6 + i) * 8 + k)
            if mode == "pre":
                P.stt("dve", g.hh.t[:, k, :], src.t[:, k, :], wc, g.rstd.t[:], ALU.mult, ALU.mult,
                      reads=[src.b[k]] + wb + g.rstd.b, writes=[g.hh.b[k]])
            else:
                tb = k % 2
                P.stt("dve", g.tmp.t[:, tb, :], src.t[:, k, :], wc, g.rstd.t[:], ALU.mult, ALU.mult,
                      reads=[src.b[k]] + wb + g.rstd.b, writes=[g.tmp.b[tb]])
                P.tt("pool", g.xT.t[:, k, :], g.tmp.t[:, tb, :], g.xT.t[:, k, :], ALU.add,
                     reads=[g.tmp.b[tb], g.xT.b[k]], writes=[g.xT.b[k]])
    g.rmsnorm = rmsnorm

    def ffn(l, f):
        phase()
        g.hid = carve("hid", [128, NM, TT], BF16, nb=NM)
        rmsnorm(g.xT, l, 0 if f == 0 else 4, "pre")
        for grp in range(11):
            wg, wgb = wload("gu%d%d" % (l, f), grp)
            wu, wub = wg[:, :, 256:512], wgb
            for j in range(2):
                m = grp * 2 + j
                pg, pu = g.ps[m % 2], g.ps[2 + m % 2]
                pgb, pub = g.psb[m % 2], g.psb[2 + m % 2]
                for k in range(NKC):
                    P.mm(pg[:], wg[:, k, j * 128:(j + 1) * 128], g.hh.t[:, k, :], start=(k == 0), stop=(k == NKC - 1),
                         reads=wgb + [g.hh.b[k]], writes=[pgb])
                for k in range(NKC):
                    P.mm(pu[:], wu[:, k, j * 128:(j + 1) * 128], g.hh.t[:, k, :], start=(k == 0), stop=(k == NKC - 1),
                         reads=wub + [g.hh.b[k]], writes=[pub])
                P.act(g.gsb.t[:, m % 2, :], pg[:], AF.Silu, reads=[pgb], writes=[g.gsb.b[m % 2]])
                P.tt("dve", g.hid.t[:, m, :], g.gsb.t[:, m % 2, :], pu[:], ALU.mult,
                     reads=[g.gsb.b[m % 2], pub], writes=[g.hid.b[m]])
        for n in range(NKC):
            wd, wdb = wload("d%d%d" % (l, f), n)
            pd, pdb = g.ps[4 + n % 2], g.psb[4 + n % 2]
            for m in range(NM):
                P.mm(pd[:], wd[:, m, :], g.hid.t[:, m, :], start=(m == 0), stop=(m == NM - 1),
                     reads=wdb + [g.hid.b[m]], writes=[pdb])
            P.copy("act", g.f.t[:, n, :], pd[:], reads=[pdb], writes=[g.f.b[n]])
        rmsnorm(g.f, l, 1 if f == 0 else 5, "post", coef=0.5)
    g.ffn = ffn

    def load_x(tt):
        t0 = tt * TT
        phase()
        g.xin = carve("xin", [128, 4, D], F32)
        P.dma("pool", g.xin.t[:], I["x"][t0:t0 + TT, :].rearrange("(g p) d -> p g d", p=128), "xin", writes=g.xin.b)
        for k in range(NKC):
            pb = g.ps[k % 2]
            for gi in range(4):
                P.mm(pb[:, gi * 128:(gi + 1) * 128], g.xin.t[:, gi, k * 128:(k + 1) * 128], cF("ident"),
                     reads=g.xin.b + g.cf.b, writes=[g.psb[k % 2]])
            P.copy("act" if k % 2 else "dve", g.xT.t[:, k, :], pb[:], reads=[g.psb[k % 2]], writes=[g.xT.b[k]])

    def store_x(tt):
        t0 = tt * TT
        phase()
        g.xin = carve("xin", [128, 4, D], F32)
        for gi in range(4):
            for h in range(2):
                pb, pbb = g.ps[(gi * 2 + h) % 2], g.psb[(gi * 2 + h) % 2]
                for kk in range(4):
                    k = h * 4 + kk
                    P.mm(pb[:, kk * 128:(kk + 1) * 128], g.xT.t[:, k, gi * 128:(gi + 1) * 128], cF("ident"),
                         reads=[g.xT.b[k]] + g.cf.b, writes=[pbb])
                P.copy("act" if h else "dve", g.xin.t[:, gi, h * 512:(h + 1) * 512], pb[:], reads=[pbb], writes=g.xin.b)
        return P.dma("pool", g.out[t0:t0 + TT, :].rearrange("(g p) d -> p g d", p=128), g.xin.t[:], "xout", reads=g.xin.b)

    def dump(name, ap_sb, bufs, dram_view=None):
        dv = g.dbg[name] if dram_view is None else dram_view
        g.dbg_toks.append(P.dma("pool", dv, ap_sb, "dbg", reads=bufs))
    g.dump = dump

    if flags.get("even", True) and L >= 1:
        setup_even_w(g)
    if flags.get("odd", True) and L >= 2:
        setup_odd_w(g)

    last = None
    for tt in range(NT):
        load_x(tt)
        for l in range(L):
            if flags.get("ffn", True):
                ffn(l, 0)
            if l == 0 and flags.get("even", True):
                even_mixer(g, tt)
            if l == 1 and flags.get("odd", True):
                odd_mixer(g, tt)
            if flags.get("ffn", True):
                ffn(l, 1)
        last = store_x(tt)
    P.final_wait("pool", [last] + g.dbg_toks)
    P.emit()
    P.close()
    A.close()
    g.n_instr = dict(P.cnt)
    return nc, g


class M:
    pass


M.Tl = Tl
M.Ctx = Ctx
M.CONST_COLS = CONST_COLS
M.EPS = EPS


TT, C, NCH, NKC = 512, 64, 8, 8


def setup_even(g):
    A, P, I = g.A, g.P, g.I
    Tl = M.Tl
    e = M.Ctx()
    g.e = e
    e.w2 = Tl(A, "gla_w2", [16, 256], F32)
    P.dma("sp", e.w2.t[:], I["gla_lora_w2"][0], "c0", writes=e.w2.b)
    e.neglb = Tl(A, "neglb", [64, 4], F32)
    for h in range(4):
        c_, cb_ = g.c64("gla_lora_b", h)
        P.ts("dve", e.neglb.t[:, h:h + 1], c_, -1.0, None, ALU.mult, reads=cb_, writes=e.neglb.b)
    e.S = Tl(A, "gla_S", [64, 4, 128], F32)
    e.Sb = Tl(A, "gla_Sb", [64, 4, 128], BF16)
    P.memset("pool", e.S.t[:], 0.0, writes=e.S.b)
    P.memset("pool", e.Sb.t[:], 0.0, writes=e.Sb.b)
    e.ycat = Tl(A, "ycat", [128, 8, TT], BF16, nb=8)
    if not g.flags.get("gdn", True):
        for k in range(4, 8):
            P.memset("pool", e.ycat.t[:, k, :], 0.0, writes=[e.ycat.b[k]])
    if g.flags.get("gdn", True):
        setup_gdn(g)


def setup_even_w(g):
    I = g.I
    pieces = [(0, 512), (512, 1024), (1024, 1536), (1536, 1552), (1552, 2064), (2064, 2576), (2576, 3088),
              (3088, 3600), (3600, 3608)]
    g.prep_matrix("ein", I["even_w_in"][0], pieces)
    g.prep_matrix("eout", I["even_w_out"][0], [(0, 512), (512, 1024)])


def proj_fm(g, ps, psb, w, wb, c0, ncols):
    P = g.P
    for k in range(NKC):
        P.mm(ps[0:ncols, :], w[:, k, c0:c0 + ncols], g.hh.t[:, k, :], start=(k == 0), stop=(k == NKC - 1),
             reads=wb + [g.hh.b[k]], writes=[psb])


def head_norm_gate(g, ps_o, ps_ob, normcol, normb, gate_ap, gate_b, out_ap, out_b, stat_bank):
    P, e = g.P, g.e
    P.copy("act", e.osb.t[:], ps_o[:], reads=[ps_ob], writes=e.osb.b)
    P.act(g.sq.t[:, 0, :], ps_o[:], AF.Square, reads=[ps_ob], writes=[g.sq.b[0]])
    sp_, spb = g.ps[stat_bank], g.psb[stat_bank]
    o1 = M.CONST_COLS["ones_128m"][0]
    P.mm(sp_[:], g.cb.t[:, o1:o1 + 128], g.sq.t[:, 0, :], reads=g.cb.b + [g.sq.b[0]], writes=[spb])
    P.act(g.rstd.t[:], sp_[:], AF.Ln, bias=M.EPS, reads=[spb], writes=g.rstd.b)
    P.act(g.rstd.t[:], g.rstd.t[:], AF.Exp, scale=-0.5, reads=g.rstd.b, writes=g.rstd.b)
    P.stt("dve", e.osb.t[:], e.osb.t[:], normcol, g.rstd.t[:], ALU.mult, ALU.mult,
          reads=e.osb.b + normb + g.rstd.b, writes=e.osb.b)
    P.tt("dve", out_ap, e.osb.t[:], gate_ap, ALU.mult, reads=e.osb.b + gate_b, writes=out_b)


def zero_y(g):
    for k in range(4):
        g.P.memset("pool", g.e.ycat.t[:, k, :], 0.0, writes=[g.e.ycat.b[k]])


def gla(g, tt):
    P, e = g.P, g.e
    ps, psb = g.ps, g.psb
    cv = g.carve
    e.qe = cv("qe", [64, 4, TT], BF16, nb=4)
    e.ke = cv("ke", [64, 4, TT], BF16, nb=4)
    e.kd = cv("kd", [64, 4, TT], BF16, nb=4)
    e.ksb = cv("ksb", [64, TT], F32)
    e.cs = cv("cs", [64, TT], F32)
    e.ex = cv("ex", [64, 2, TT], F32, nb=2)
    e.alast = cv("alast", [64, 4, NCH], F32, nb=4)
    e.glr = cv("glr", [16, TT], F32)
    e.vtok = cv("vtok", [64, NCH, 512], BF16, nb=NCH)
    e.kdtok = cv("kdtok", [64, NCH, 256], BF16, nb=NCH)
    e.sgg = cv("sgg", [128, 4, TT], BF16, nb=4)
    e.att = cv("att", [64, 4, 512], BF16, nb=4)
    e.osb = cv("osb", [128, TT], F32)
    wl, wlb = g.wload("ein", 3)
    proj_fm(g, ps[0], psb[0], wl, wlb, 0, 16)
    P.copy("act", e.glr.t[:], ps[0][0:16, :], reads=[psb[0]], writes=e.glr.b)
    wqk, wqkb = g.wload("ein", 0)
    for h in range(4):
        b0 = (h % 2) * 2
        P.mm(ps[b0][0:64, :], e.w2.t[0:16, h * 64:(h + 1) * 64], e.glr.t[0:16, :], reads=e.w2.b + e.glr.b, writes=[psb[b0]])
        P.act(e.ex.t[:, 0, :], ps[b0][0:64, :], AF.Exp, scale=-1.0, bias=e.neglb.t[:, h:h + 1], reads=[psb[b0]] + e.neglb.b, writes=[e.ex.b[0]])
        P.act(e.ex.t[:, 0, :], e.ex.t[:, 0, :], AF.Ln, bias=1.0, reads=[e.ex.b[0]], writes=[e.ex.b[0]])
        P.op("dve", lambda e_: e_.tensor_tensor_scan(out=e.cs.t[:], data0=g.cF("reset", 64), data1=e.ex.t[:, 0, :], initial=0.0,
                                                     op0=ALU.mult, op1=ALU.add), reads=[e.ex.b[0]] + g.cf.b, writes=e.cs.b)
        proj_fm(g, ps[b0 + 1], psb[b0 + 1], wqk, wqkb, h * 64, 64)
        P.act(e.ex.t[:, 1, :], e.cs.t[:], AF.Exp, scale=-1.0 / 16, reads=e.cs.b, writes=[e.ex.b[1]])
        P.stt("dve", e.qe.t[:, h, :], ps[b0 + 1][0:64, :], 0.125, e.ex.t[:, 1, :], ALU.mult, ALU.mult,
              reads=[psb[b0 + 1], e.ex.b[1]], writes=[e.qe.b[h]])
        proj_fm(g, ps[b0], psb[b0], wqk, wqkb, 256 + h * 64, 64)
        P.copy("act", e.ksb.t[:], ps[b0][0:64, :], reads=[psb[b0]], writes=e.ksb.b)
        P.act(e.ex.t[:, 1, :], e.cs.t[:], AF.Exp, scale=1.0 / 16, reads=e.cs.b, writes=[e.ex.b[1]])
        P.tt("dve", e.ke.t[:, h, :], e.ksb.t[:], e.ex.t[:, 1, :], ALU.mult, reads=e.ksb.b + [e.ex.b[1]], writes=[e.ke.b[h]])
        cs3 = e.cs.t[:].rearrange("p (n c) -> p n c", c=C)
        ex3 = e.ex.t[:, 0, :].rearrange("p (n c) -> p n c", c=C)
        P.tt("dve", ex3, cs3[:, :, C - 1:C].broadcast_to([64, NCH, C]), cs3, ALU.subtract, reads=e.cs.b, writes=[e.ex.b[0]])
        P.act(e.ex.t[:, 0, :], e.ex.t[:, 0, :], AF.Exp, scale=-1.0 / 16, reads=[e.ex.b[0]], writes=[e.ex.b[0]])
        P.tt("dve", e.kd.t[:, h, :], e.ksb.t[:], e.ex.t[:, 0, :], ALU.mult, reads=e.ksb.b + [e.ex.b[0]], writes=[e.kd.b[h]])
        P.act(e.alast.t[:, h, :], cs3[:, :, C - 1], AF.Exp, scale=-1.0 / 16, reads=e.cs.b, writes=[e.alast.b[h]])
    wv, wvb = g.wload("ein", 1)
    for n in range(NCH):
        pb, pbb = ps[n % 4], psb[n % 4]
        for k in range(NKC):
            P.mm(pb[0:64, :], g.hh.t[:, k, n * 64:(n + 1) * 64], wv[:, k, :], start=(k == 0), stop=(k == NKC - 1),
                 reads=[g.hh.b[k]] + wvb, writes=[pbb])
        P.copy("act" if n % 2 else "dve", e.vtok.t[:, n, :], pb[0:64, :], reads=[pbb], writes=[e.vtok.b[n]])
    wg, wgb = g.wload("ein", 2)
    for h in range(4):
        pb, pbb = ps[4 + h % 2], psb[4 + h % 2]
        proj_fm(g, pb, pbb, wg, wgb, h * 128, 128)
        P.act(e.sgg.t[:, h, :], pb[:], AF.Silu, reads=[pbb], writes=[e.sgg.b[h]])
    ib = M.CONST_COLS["ident"][0]
    for n in range(NCH):
        pb, pbb = ps[n % 4], psb[n % 4]
        for h in range(4):
            P.mm(pb[0:64, h * 64:(h + 1) * 64], e.kd.t[:, h, n * 64:(n + 1) * 64], g.cb.t[0:64, ib:ib + 64],
                 reads=[e.kd.b[h]] + g.cb.b, writes=[pbb])
        P.copy("act" if n % 2 else "dve", e.kdtok.t[:, n, :], pb[0:64, 0:256], reads=[pbb], writes=[e.kdtok.b[n]])
    for h in range(4):
        pb, pbb = ps[h], psb[h]
        for n in range(NCH):
            P.mm(pb[0:64, n * 64:(n + 1) * 64], e.ke.t[:, h, n * 64:(n + 1) * 64], e.qe.t[:, h, n * 64:(n + 1) * 64],
                 reads=[e.ke.b[h], e.qe.b[h]], writes=[pbb])
        P.tt("dve", e.att.t[:, h, :], pb[0:64, :], g.cF("m_incl", 64), ALU.mult, reads=[pbb] + g.cf.b, writes=[e.att.b[h]])
    for n in range(NCH):
        for h in range(4):
            po, pob = ps[4 + h], psb[4 + h]
            P.mm(po[:, n * 64:(n + 1) * 64], e.vtok.t[:, n, h * 128:(h + 1) * 128], e.att.t[:, h, n * 64:(n + 1) * 64],
                 start=True, stop=False, reads=[e.vtok.b[n], e.att.b[h]], writes=[pob])
            P.mm(po[:, n * 64:(n + 1) * 64], e.Sb.t[:, h, :], e.qe.t[:, h, n * 64:(n + 1) * 64],
                 start=False, stop=True, reads=e.Sb.b + [e.qe.b[h]], writes=[pob])
        for h in range(4):
            P.mm(ps[3][0:64, h * 128:(h + 1) * 128], e.kdtok.t[:, n, h * 64:(h + 1) * 64],
                 e.vtok.t[:, n, h * 128:(h + 1) * 128], reads=[e.kdtok.b[n], e.vtok.b[n]], writes=[psb[3]])
        for h in range(4):
            P.stt("dve", e.S.t[:, h, :], e.S.t[:, h, :], e.alast.t[:, h, n:n + 1], ps[3][0:64, h * 128:(h + 1) * 128], ALU.mult, ALU.add,
                  reads=e.S.b + [e.alast.b[h], psb[3]], writes=e.S.b)
        P.copy("act", e.Sb.t[:], e.S.t[:], reads=e.S.b, writes=e.Sb.b)
    nc_, nb_ = g.colf("gla_norm", 0)
    for h in range(4):
        head_norm_gate(g, ps[4 + h], psb[4 + h], nc_, nb_, e.sgg.t[:, h, :], [e.sgg.b[h]], e.ycat.t[:, h, :], [e.ycat.b[h]], 0)


def even_mixer(g, tt):
    P, e = g.P, g.e
    g.phase()
    g.rmsnorm(g.xT, 0, 2, "pre")
    gla(g, tt)
    if g.flags.get("gdn", True):
        gdn(g, tt)
    for half in range(2):
        wo, wob = g.wload("eout", half)
        for nn in range(4):
            n = half * 4 + nn
            pb, pbb = g.ps[n % 2], g.psb[n % 2]
            for k in range(NKC):
                P.mm(pb[:], wo[:, k, nn * 128:(nn + 1) * 128], e.ycat.t[:, k, :], start=(k == 0), stop=(k == NKC - 1),
                     reads=wob + [e.ycat.b[k]], writes=[pbb])
            P.copy("act", g.f.t[:, n, :], pb[:], reads=[pbb], writes=[g.f.b[n]])
    g.rmsnorm(g.f, 0, 3, "post", coef=1.0)


TT, C, NCH, NKC = 512, 64, 8, 8


def setup_gdn(g):
    A, P, I = g.A, g.P, g.I
    Tl = M.Tl
    e = g.e
    e.hist = Tl(A, "gdn_hist", [128, 12, 3], F32, nb=12)
    P.memset("pool", e.hist.t[:], 0.0, writes=e.hist.b)
    e.GS = Tl(A, "gdn_S", [128, 4, 128], F32, nb=4)
    e.GSb = Tl(A, "gdn_Sb", [128, 4, 128], BF16, nb=4)
    P.memset("pool", e.GS.t[:], 0.0, writes=e.GS.b)
    P.memset("pool", e.GSb.t[:], 0.0, writes=e.GSb.b)
    e.negA = Tl(A, "negA", [64, 4], F32)
    e.dtb = Tl(A, "dtb", [64, 4], F32)
    P.dma("sp", e.negA.t[:], I["gdn_a_log"][0].partition_broadcast(64), "c0", writes=e.negA.b)
    P.dma("sp", e.dtb.t[:], I["gdn_dt_bias"][0].partition_broadcast(64), "c0", writes=e.dtb.b)
    P.act(e.negA.t[:], e.negA.t[:], AF.Exp, reads=e.negA.b, writes=e.negA.b)
    P.ts("dve", e.negA.t[:], e.negA.t[:], -1.0, None, ALU.mult, reads=e.negA.b, writes=e.negA.b)


def inv_rsqrt(g, out_ap, out_b, in_ap, in_b, eps):
    P = g.P
    P.act(out_ap, in_ap, AF.Ln, bias=eps, reads=in_b, writes=out_b)
    P.act(out_ap, out_ap, AF.Exp, scale=-0.5, reads=out_b, writes=out_b)


def gdn(g, tt):
    P, e = g.P, g.e
    ps, psb = g.ps, g.psb
    cF, cB = g.cF, g.cB
    g.phase()
    cv = g.carve
    ib = M.CONST_COLS["ident"][0]
    identb = g.cb.t[:, ib:ib + 128]
    identf = g.cf.t[0:64, ib:ib + 64]
    o1 = M.CONST_COLS["ones_1"][0]

    ab = cv("ab", [64, NCH, 8], F32)
    wab, wabb = g.wload("ein", 8)
    for n in range(NCH):
        for k in range(NKC):
            P.mm(ps[0][0:64, n * 8:(n + 1) * 8], g.hh.t[:, k, n * 64:(n + 1) * 64], wab[:, k, 0:8], start=(k == 0), stop=(k == NKC - 1),
                 reads=[g.hh.b[k]] + wabb, writes=[psb[0]])
    P.copy("dve", ab.t[:].rearrange("p n c -> p (n c)"), ps[0][0:64, 0:64], reads=[psb[0]], writes=ab.b)
    beta = cv("beta", [64, NCH, 4], F32)
    gg = cv("gg", [64, NCH, 4], F32)
    P.act(beta.t[:], ab.t[:, :, 4:8], AF.Sigmoid, reads=ab.b, writes=beta.b)
    P.tt("dve", gg.t[:], ab.t[:, :, 0:4], e.dtb.t[:].unsqueeze(1).broadcast_to([64, NCH, 4]), ALU.add, reads=ab.b + e.dtb.b, writes=gg.b)
    P.act(gg.t[:], gg.t[:], AF.Exp, reads=gg.b, writes=gg.b)
    P.act(gg.t[:], gg.t[:], AF.Ln, bias=1.0, reads=gg.b, writes=gg.b)
    P.tt("dve", gg.t[:], gg.t[:], e.negA.t[:].unsqueeze(1).broadcast_to([64, NCH, 4]), ALU.mult, reads=gg.b + e.negA.b, writes=gg.b)
    gflat = gg.t[:].rearrange("p n h -> p (n h)")
    gc = cv("gc", [64, NCH, 4], F32)
    egl = cv("egl", [128, NCH, 4], F32)
    sc_kbe = cv("sc_kbe", [64, NCH, 4], F32)
    sc_kd = cv("sc_kd", [64, NCH, 4], F32)
    P.mm(ps[1][0:64, 0:32], cF("m_incl", 64, 0, 64), gflat, reads=g.cf.b + gg.b, writes=[psb[1]])
    P.mm(ps[1][:, 32:64], cF("ones_1", 64), gflat, reads=g.cf.b + gg.b, writes=[psb[1]])
    P.copy("dve", gc.t[:].rearrange("p n h -> p (n h)"), ps[1][0:64, 0:32], reads=[psb[1]], writes=gc.b)
    P.act(egl.t[:].rearrange("p n h -> p (n h)"), ps[1][:, 32:64], AF.Exp, reads=[psb[1]], writes=egl.b)
    P.tt("dve", sc_kd.t[:].rearrange("p n h -> p (n h)"), ps[1][0:64, 32:64], gc.t[:].rearrange("p n h -> p (n h)"), ALU.subtract,
         reads=[psb[1]] + gc.b, writes=sc_kd.b)
    P.act(sc_kd.t[:], sc_kd.t[:], AF.Exp, reads=sc_kd.b, writes=sc_kd.b)
    P.act(sc_kbe.t[:], gc.t[:], AF.Exp, reads=gc.b, writes=sc_kbe.b)
    P.tt("dve", sc_kbe.t[:], sc_kbe.t[:], beta.t[:], ALU.mult, reads=sc_kbe.b + beta.b, writes=sc_kbe.b)

    st = g.flags.get("gstage", 99)

    def zy(h):
        P.memset("pool", e.ycat.t[:, 4 + h, :], 0.0, writes=[e.ycat.b[4 + h]])
    if st <= 0:
        for h in range(4):
            zy(h)
        return
    raw = cv("raw", [128, TT + 3], F32)
    acc = cv("acc", [128, TT], F32)
    cc = cv("cc", [128, TT], F32)
    qT = cv("qT", [128, TT], BF16)
    kT = cv("kT", [128, TT], BF16)
    qeT = cv("qeT", [128, TT], BF16)
    G = cv("G", [64, NCH, C], F32)
    grep = cv("grep", [64, NCH, 128], F32)
    EB = cv("EB", [128, TT], F32)
    kbe = cv("kbe", [64, NCH, 128], BF16)
    kdc = cv("kdc", [64, NCH, 128], BF16)
    vb = cv("vb", [64, NCH, 128], BF16)
    E = cv("E", [64, TT], F32)
    Ya = [cv("Ya%d" % i, [64, TT], F32) for i in range(2)]
    YTa = [cv("YTa%d" % i, [64, TT], F32) for i in range(2)]
    R = cv("R", [64, TT], F32)
    att = cv("gatt", [64, TT], BF16)
    Tb = cv("Tb", [64, TT], BF16)
    u = cv("u", [64, NCH, 128], F32)
    wT = cv("wT", [128, TT], BF16)
    vnew = cv("vnew", [64, 128], BF16)
    sgz = cv("sgz", [128, TT], BF16)
    osb = cv("gosb", [128, TT], F32)
    e.osb = osb
    wdq = [None] * 3

    def conv_tile(k12, w3, w3b, cbase):
        proj_fm(g, ps[2], psb[2], w3, w3b, cbase, 128)
        P.copy("dve", raw.t[:, 0:3], e.hist.t[:, k12, :], reads=[e.hist.b[k12]], writes=raw.b)
        P.copy("act", raw.t[:, 3:TT + 3], ps[2][:], reads=[psb[2]], writes=raw.b)
        P.copy("dve", e.hist.t[:, k12, :], raw.t[:, TT:TT + 3], reads=raw.b, writes=[e.hist.b[k12]])
        c0, c0b = g.colf("gdn_conv", 0 * 12 + k12)
        P.ts("dve", acc.t[:], raw.t[:, 0:TT], c0, None, ALU.mult, reads=raw.b + c0b, writes=acc.b)
        for j in range(1, 4):
            cj, cjb = g.colf("gdn_conv", j * 12 + k12)
            P.stt("dve", acc.t[:], raw.t[:, j:j + TT], cj, acc.t[:], ALU.mult, ALU.add, reads=raw.b + cjb + acc.b, writes=acc.b)
        P.act(cc.t[:], acc.t[:], AF.Silu, reads=acc.b, writes=cc.b)

    def l2n(dst, scale):
        P.act(g.sq.t[:, 0, :], cc.t[:], AF.Square, reads=cc.b, writes=[g.sq.b[0]])
        P.mm(ps[3][:], g.cb.t[:, o1:o1 + 128], g.sq.t[:, 0, :], reads=g.cb.b + [g.sq.b[0]], writes=[psb[3]])
        inv_rsqrt(g, g.rstd.t[:], g.rstd.b, ps[3][:], [psb[3]], M.EPS)
        P.stt("dve", dst.t[:], cc.t[:], float(scale), g.rstd.t[:], ALU.mult, ALU.mult, reads=cc.b + g.rstd.b, writes=dst.b)

    wq, wqb = g.wload("ein", 4)
    wk, wkb = g.wload("ein", 5)
    wv, wvb = g.wload("ein", 6)
    wz, wzb = g.wload("ein", 7)
    for h in range(4):
        conv_tile(h, wq, wqb, h * 128)
        l2n(qT, 128 ** -0.5)
        conv_tile(4 + h, wk, wkb, h * 128)
        l2n(kT, 1.0)
        conv_tile(8 + h, wv, wvb, h * 128)
        P.copy("act", wT.t[:], cc.t[:], reads=cc.b, writes=wT.b)
        if st <= 1:
            zy(h)
            continue
        for half in range(2):
            pb, pbb = ps[half], psb[half]
            for n4 in range(4):
                n = half * 4 + n4
                P.mm(pb[0:64, n4 * 128:(n4 + 1) * 128], kT.t[:, n * 64:(n + 1) * 64], identb, reads=kT.b + g.cb.b, writes=[pbb])
            pv = pb[0:64, :].rearrange("p (n d) -> p n d", n=4)
            P.tt("dve", kbe.t[:, half * 4:half * 4 + 4, :], pv, sc_kbe.t[:, half * 4:half * 4 + 4, h:h + 1].broadcast_to([64, 4, 128]), ALU.mult,
                 reads=[pbb] + sc_kbe.b, writes=kbe.b)
            P.tt("dve", kdc.t[:, half * 4:half * 4 + 4, :], pv, sc_kd.t[:, half * 4:half * 4 + 4, h:h + 1].broadcast_to([64, 4, 128]), ALU.mult,
                 reads=[pbb] + sc_kd.b, writes=kdc.b)
        for half in range(2):
            pb, pbb = ps[2 + half], psb[2 + half]
            for n4 in range(4):
                n = half * 4 + n4
                P.mm(pb[0:64, n4 * 128:(n4 + 1) * 128], wT.t[:, n * 64:(n + 1) * 64], identb, reads=wT.b + g.cb.b, writes=[pbb])
            pv = pb[0:64, :].rearrange("p (n d) -> p n d", n=4)
            P.tt("dve", vb.t[:, half * 4:half * 4 + 4, :], pv, beta.t[:, half * 4:half * 4 + 4, h:h + 1].broadcast_to([64, 4, 128]), ALU.mult,
                 reads=[pbb] + beta.b, writes=vb.b)
        if st <= 2:
            zy(h)
            continue
        P.tt("dve", G.t[:], cF("m_strict_T", 64).rearrange("p (n c) -> p n c", c=C), gg.t[:, :, h:h + 1].broadcast_to([64, NCH, C]), ALU.mult,
             reads=g.cf.b + gg.b, writes=G.b)
        P.tt("dve", grep.t[:], cF("ones_1", 64).unsqueeze(1).broadcast_to([64, NCH, 128]), gg.t[:, :, h:h + 1].broadcast_to([64, NCH, 128]), ALU.mult,
             reads=g.cf.b + gg.b, writes=grep.b)
        for n in range(NCH):
            P.mm(ps[4][:, n * 64:(n + 1) * 64], grep.t[:, n, :], cF("m_incl", 64, 0, 64), reads=grep.b + g.cf.b, writes=[psb[4]])
        P.act(EB.t[:], ps[4][:], AF.Exp, reads=[psb[4]], writes=EB.b)
        P.tt("dve", qeT.t[:], qT.t[:], EB.t[:], ALU.mult, reads=qT.b + EB.b, writes=qeT.b)
        if st <= 3:
            zy(h)
            continue
        for n in range(NCH):
            sl = slice(n * 64, (n + 1) * 64)
            P.mm(ps[5][0:64, sl], cF("m_incl", 64, 0, 64), G.t[:, n, :], reads=g.cf.b + G.b, writes=[psb[5]])
            P.mm(ps[6][0:64, sl], G.t[:, n, :], cF("m_incl", 64, 0, 64), reads=g.cf.b + G.b, writes=[psb[6]])
            P.mm(ps[0][0:64, sl], kT.t[:, sl], kT.t[:, sl], reads=kT.b, writes=[psb[0]])
            P.mm(ps[1][0:64, sl], kT.t[:, sl], qT.t[:, sl], reads=kT.b + qT.b, writes=[psb[1]])
        P.act(E.t[:], ps[5][0:64, :], AF.Exp, reads=[psb[5]], writes=E.b)
        P.tt("pool", E.t[:], E.t[:], cF("m_strict_T", 64), ALU.mult, reads=E.b + g.cf.b, writes=E.b)
        P.stt("dve", YTa[0].t[:], ps[0][0:64, :], -1.0, E.t[:], ALU.mult, ALU.mult, reads=[psb[0]] + E.b, writes=YTa[0].b)
        yt3 = YTa[0].t[:].rearrange("p (n c) -> p n c", c=C)
        P.tt("dve", yt3, yt3, beta.t[:, :, h:h + 1].broadcast_to([64, NCH, C]), ALU.mult, reads=YTa[0].b + beta.b, writes=YTa[0].b)
        P.act(E.t[:], ps[6][0:64, :], AF.Exp, reads=[psb[6]], writes=E.b)
        P.tt("pool", E.t[:], E.t[:], cF("m_incl", 64), ALU.mult, reads=E.b + g.cf.b, writes=E.b)
        P.tt("dve", att.t[:], ps[1][0:64, :], E.t[:], ALU.mult, reads=[psb[1]] + E.b, writes=att.b)
        if st <= 4:
            zy(h)
            continue
        for n in range(NCH):
            sl = slice(n * 64, (n + 1) * 64)
            P.mm(ps[5][0:64, sl], YTa[0].t[:, sl], identf, reads=YTa[0].b + g.cf.b, writes=[psb[5]])
        nsub = g.flags.get("nsub", 9)
        extra = []
        if g.flags.get("dummy"):
            P.mm(ps[4][0:64, 0:64], YTa[0].t[:, 0:64], identf, reads=YTa[0].b + g.cf.b, writes=[psb[4]])
            extra = [psb[4]]
        if nsub >= 2:
            ydst = {"Ya": Ya[0], "R": R, "E": E}[g.flags.get("ydst", "Ya")]
            if g.flags.get("yeng", "dve") == "actf":
                P.act(ydst.t[:], ps[5][0:64, :], AF.Identity, reads=[psb[5]], writes=ydst.b)
            else:
                P.copy(g.flags.get("yeng", "dve"), ydst.t[:], ps[5][0:64, :], reads=[psb[5]] + extra, writes=ydst.b)
        if nsub >= 3:
            P.tt("dve", R.t[:], ps[5][0:64, :], cF("ident64x", 64), ALU.add, reads=[psb[5]] + g.cf.b, writes=R.b)
        cur = 0
        for lvl in range(1, 1 + g.flags.get("nlev", 5)):
            nxt = 1 - cur
            for n in range(NCH):
                sl = slice(n * 64, (n + 1) * 64)
                P.mm(ps[0][0:64, sl], YTa[cur].t[:, sl], Ya[cur].t[:, sl], reads=YTa[cur].b + Ya[cur].b, writes=[psb[0]])
                P.mm(ps[1][0:64, sl], Ya[cur].t[:, sl], YTa[cur].t[:, sl], reads=YTa[cur].b + Ya[cur].b, writes=[psb[1]])
            lsub = g.flags.get("lsub", 9)
            if lvl < 5 and lsub >= 2:
                P.copy(g.flags.get("yeng", "dve"), Ya[nxt].t[:], ps[0][0:64, :], reads=[psb[0]], writes=Ya[nxt].b)
            if lsub >= 2:
                P.copy("dve", YTa[nxt].t[:], ps[1][0:64, :], reads=[psb[1]], writes=YTa[nxt].b)
            if lsub >= 3:
                for n in range(NCH):
                    sl = slice(n * 64, (n + 1) * 64)
                    P.mm(ps[5][0:64, sl], YTa[nxt].t[:, sl], R.t[:, sl], reads=YTa[nxt].b + R.b, writes=[psb[5]])
            if lsub >= 4:
                P.tt("dve", R.t[:], R.t[:], ps[5][0:64, :], ALU.add, reads=R.b + [psb[5]], writes=R.b)
            cur = nxt
        if nsub >= 4:
            P.copy("act", Tb.t[:], R.t[:], reads=R.b, writes=Tb.b)
        if st <= 5:
            zy(h)
            continue
        for half in range(2):
            pb, pbb = ps[half], psb[half]
            for n4 in range(4):
                n = half * 4 + n4
                P.mm(pb[0:64, n4 * 128:(n4 + 1) * 128], Tb.t[:, n * 64:(n + 1) * 64], vb.t[:, n, :], reads=Tb.b + vb.b, writes=[pbb])
            P.copy("act" if half else "dve", u.t[:, half * 4:half * 4 + 4, :].rearrange("p n d -> p (n d)"), pb[0:64, :], reads=[pbb], writes=u.b)
        for n in range(NCH):
            sl = slice(n * 64, (n + 1) * 64)
            P.mm(ps[2][:, sl], kbe.t[:, n, :], Tb.t[:, sl], reads=kbe.b + Tb.b, writes=[psb[2]])
        P.copy("act", wT.t[:], ps[2][:], reads=[psb[2]], writes=wT.b)
        proj_fm(g, ps[3], psb[3], wz, wzb, h * 128, 128)
        P.act(sgz.t[:], ps[3][:], AF.Silu, reads=[psb[3]], writes=sgz.b)
        if st <= 6:
            zy(h)
            continue
        po, pob = ps[7], psb[7]
        for n in range(NCH):
            sl = slice(n * 64, (n + 1) * 64)
            P.mm(ps[4][0:64, 0:128], wT.t[:, sl], e.GSb.t[:, h, :], reads=wT.b + [e.GSb.b[h]], writes=[psb[4]])
            P.tt("dve", vnew.t[:], u.t[:, n, :], ps[4][0:64, 0:128], ALU.subtract, reads=u.b + [psb[4]], writes=vnew.b)
            P.mm(po[:, sl], e.GSb.t[:, h, :], qeT.t[:, sl], start=True, stop=False, reads=[e.GSb.b[h]] + qeT.b, writes=[pob])
            P.mm(po[:, sl], vnew.t[:], att.t[:, sl], start=False, stop=True, reads=vnew.b + att.b, writes=[pob])
            P.mm(ps[6][:, 0:128], kdc.t[:, n, :], vnew.t[:], reads=kdc.b + vnew.b, writes=[psb[6]])
            P.stt("dve", e.GS.t[:, h, :], e.GS.t[:, h, :], egl.t[:, n, h:h + 1], ps[6][:, 0:128], ALU.mult, ALU.add,
                  reads=[e.GS.b[h], psb[6]] + egl.b, writes=[e.GS.b[h]])
            P.copy("act", e.GSb.t[:, h, :], e.GS.t[:, h, :], reads=[e.GS.b[h]], writes=[e.GSb.b[h]])
        nc_, nb_ = g.colf("gdn_norm", 0)
        head_norm_gate(g, po, pob, nc_, nb_, sgz.t[:], sgz.b, e.ycat.t[:, 4 + h, :], [e.ycat.b[4 + h]], 3)


TT, C, NCH, NKC = 512, 64, 8, 8
RWKV_GN_EPS = 64e-5


def setup_odd(g):
    A, P, I = g.A, g.P, g.I
    Tl = M.Tl
    o = M.Ctx()
    g.o = o
    o.w2 = Tl(A, "rw_w2", [64, 512], F32)
    o.a2 = Tl(A, "rw_a2", [64, 512], F32)
    o.g2 = M.Ctx(); o.g2.t = g.f.t[:, 2, :]; o.g2.b = [g.f.b[2]]
    P.dma("sp", o.w2.t[:], I["rwkv_w2"][0], "c0", writes=o.w2.b)
    P.dma("sp", o.a2.t[:], I["rwkv_a2"][0], "c0", writes=o.a2.b)
    P.dma("sp", o.g2.t[:], I["rwkv_g2"][0], "c0", writes=o.g2.b)
    o.g2b = Tl(A, "rw_g2b", [128, 512], BF16)
    P.copy("dve", o.g2b.t[:], o.g2.t[:], reads=o.g2.b, writes=o.g2b.b)
    o.wa = M.Ctx(); o.wa.t = g.f.t[:, 0, :].rearrange("p (k c) -> p k c", k=4); o.wa.b = [g.f.b[0]]
    o.wx = M.Ctx(); o.wx.t = g.f.t[:, 1, :].rearrange("p (k c) -> p k c", k=4); o.wx.b = [g.f.b[1]]
    P.memset("pool", o.wa.t[:], 0.0, writes=o.wa.b)
    P.memset("pool", o.wx.t[:], 0.0, writes=o.wx.b)
    for k in range(4):
        for a in range(2):
            P.dma("sp", o.wa.t[a * 64:(a + 1) * 64, k, a * 64:(a + 1) * 64], I["lru_wa"][0, 2 * k + a], "c0", writes=o.wa.b)
            P.dma("sp", o.wx.t[a * 64:(a + 1) * 64, k, a * 64:(a + 1) * 64], I["lru_wx"][0, 2 * k + a], "c0", writes=o.wx.b)
    o.wab = Tl(A, "lru_wab", [128, 4, 128], BF16)
    o.wxb = Tl(A, "lru_wxb", [128, 4, 128], BF16)
    P.copy("dve", o.wab.t[:], o.wa.t[:], reads=o.wa.b, writes=o.wab.b)
    P.copy("dve", o.wxb.t[:], o.wx.t[:], reads=o.wx.b, writes=o.wxb.b)
    o.dc = Tl(A, "odd_cols", [128, 16], F32)
    for k in range(4):
        lc, lb = g.colf("lru_lambda", k)
        P.act(o.dc.t[:, k:k + 1], lc, AF.Exp, scale=-1.0, reads=lb, writes=o.dc.b)
    P.act(o.dc.t[:, 0:4], o.dc.t[:, 0:4], AF.Ln, bias=1.0, reads=o.dc.b, writes=o.dc.b)
    P.ts("dve", o.dc.t[:, 0:4], o.dc.t[:, 0:4], -8.0, None, ALU.mult, reads=o.dc.b, writes=o.dc.b)
    for h in range(8):
        wc, wb = g.c64("rwkv_w0", h)
        P.ts("dve", o.dc.t[0:64, 4 + h:5 + h], wc, -1.0, None, ALU.mult, reads=wb, writes=o.dc.b)
    o.A = Tl(A, "rw_A", [64, 8, 64], F32, nb=8)
    o.Ab = Tl(A, "rw_Ab", [64, 8, 64], BF16, nb=8)
    P.memset("pool", o.A.t[:], 0.0, writes=o.A.b)
    P.memset("pool", o.Ab.t[:], 0.0, writes=o.Ab.b)
    o.sh = Tl(A, "rw_sh", [128, 28], F32)
    P.memset("pool", o.sh.t[:], 0.0, writes=o.sh.b)
    o.lh = Tl(A, "lru_hist", [128, 4, 3], F32)
    P.memset("pool", o.lh.t[:], 0.0, writes=o.lh.b)
    o.hs = Tl(A, "lru_state", [128, 4], F32)
    P.memset("pool", o.hs.t[:], 0.0, writes=o.hs.b)
    o.yr = Tl(A, "yr", [64, 8, TT], BF16, nb=8)
    o.yl = g.e.ycat if hasattr(g, "e") and hasattr(g.e, "ycat") else Tl(A, "ycat_o", [128, 8, TT], BF16, nb=8)


def setup_odd_w(g):
    I = g.I
    g.prep_matrix("oin", I["odd_w_in"][0], [(0, 512), (512, 1024), (1024, 1536), (1536, 1792), (1792, 2304), (2304, 2816)])
    g.prep_matrix("oout_r", I["odd_w_out"][0][0:512, :], [(0, 512), (512, 1024)], p=64)
    g.prep_matrix("oout_l", I["odd_w_out"][0][512:1024, :], [(0, 512), (512, 1024)])


def lru(g, tt):
    P, o = g.P, g.o
    ps, psb = g.ps, g.psb
    cv = g.carve
    raw = cv("lraw", [128, TT + 3], F32)
    xb = cv("lxb", [128, TT], F32)
    xbb = cv("lxbb", [128, TT], BF16)
    gr = cv("lgr", [128, TT], F32)
    gi = cv("lgi", [128, TT], F32)
    aa = cv("laa", [128, TT], F32)
    hh_ = cv("lhh", [128, TT], F32)
    gl = cv("lgl", [128, TT], F32)
    wx_, wxb_ = g.wload("oin", 4)
    wy_, wyb_ = g.wload("oin", 5)
    for k in range(4):
        proj_fm(g, ps[0], psb[0], wx_, wxb_, k * 128, 128)
        P.copy("dve", raw.t[:, 0:3], o.lh.t[:, k, :], reads=o.lh.b, writes=raw.b)
        P.copy("act", raw.t[:, 3:TT + 3], ps[0][:], reads=[psb[0]], writes=raw.b)
        P.copy("dve", o.lh.t[:, k, :], raw.t[:, TT:TT + 3], reads=raw.b, writes=o.lh.b)
        c0, c0b = g.colf("lru_conv_w", 0 * 4 + k)
        bc, bcb = g.colf("lru_conv_b", k)
        P.ts("dve", xb.t[:], raw.t[:, 0:TT], c0, bc, ALU.mult, ALU.add, reads=raw.b + c0b + bcb, writes=xb.b)
        for j in range(1, 4):
            cj, cjb = g.colf("lru_conv_w", j * 4 + k)
            P.stt("dve", xb.t[:], raw.t[:, j:j + TT], cj, xb.t[:], ALU.mult, ALU.add, reads=raw.b + cjb + xb.b, writes=xb.b)
        P.copy("act", xbb.t[:], xb.t[:], reads=xb.b, writes=xbb.b)
        P.mm(ps[1][:], o.wab.t[:, k, :], xbb.t[:], reads=o.wab.b + xbb.b, writes=[psb[1]])
        P.mm(ps[2][:], o.wxb.t[:, k, :], xbb.t[:], reads=o.wxb.b + xbb.b, writes=[psb[2]])
        ba, bab = g.colf("lru_ba", k)
        bx, bxb = g.colf("lru_bx", k)
        P.act(gr.t[:], ps[1][:], AF.Sigmoid, bias=ba, reads=[psb[1]] + bab, writes=gr.b)
        P.act(gi.t[:], ps[2][:], AF.Sigmoid, bias=bx, reads=[psb[2]] + bxb, writes=gi.b)
        P.act(aa.t[:], gr.t[:], AF.Exp, scale=o.dc.t[:, k:k + 1], reads=gr.b + o.dc.b, writes=aa.b)
        P.tt("pool", gr.t[:], aa.t[:], aa.t[:], ALU.mult, reads=aa.b, writes=gr.b)
        P.act(gr.t[:], gr.t[:], AF.Ln, scale=-1.0, bias=1.0, reads=gr.b, writes=gr.b)
        P.act(gr.t[:], gr.t[:], AF.Exp, scale=0.5, reads=gr.b, writes=gr.b)
        P.tt("dve", gi.t[:], gi.t[:], gr.t[:], ALU.mult, reads=gi.b + gr.b, writes=gi.b)
        P.tt("dve", gi.t[:], gi.t[:], xb.t[:], ALU.mult, reads=gi.b + xb.b, writes=gi.b)
        P.op("dve", lambda e_, k=k: e_.tensor_tensor_scan(out=hh_.t[:], data0=aa.t[:], data1=gi.t[:], initial=o.hs.t[:, k:k + 1],
                                                          op0=ALU.mult, op1=ALU.add), reads=aa.b + gi.b + o.hs.b, writes=hh_.b)
        P.copy("dve", o.hs.t[:, k:k + 1], hh_.t[:, TT - 1:TT], reads=hh_.b, writes=o.hs.b)
        proj_fm(g, ps[3], psb[3], wy_, wyb_, k * 128, 128)
        P.act(gl.t[:], ps[3][:], AF.Square, reads=[psb[3]], writes=gl.b)
        P.ts("dve", gl.t[:], gl.t[:], 0.044715, 1.0, ALU.mult, ALU.add, reads=gl.b, writes=gl.b)
        P.tt("dve", gl.t[:], gl.t[:], ps[3][:], ALU.mult, reads=gl.b + [psb[3]], writes=gl.b)
        P.act(gl.t[:], gl.t[:], AF.Tanh, scale=0.7978845608028654, reads=gl.b, writes=gl.b)
        P.ts("dve", gl.t[:], gl.t[:], 1.0, 0.5, ALU.add, ALU.mult, reads=gl.b, writes=gl.b)
        P.tt("dve", gl.t[:], gl.t[:], ps[3][:], ALU.mult, reads=gl.b + [psb[3]], writes=gl.b)
        P.tt("dve", o.yl.t[:, 4 + k, :], hh_.t[:], gl.t[:], ALU.mult, reads=hh_.b + gl.b, writes=[o.yl.b[4 + k]])


def rwkv(g, tt):
    P, o = g.P, g.o
    ps, psb = g.ps, g.psb
    cF = g.cF
    cv = g.carve
    ib = M.CONST_COLS["ident"][0]
    identb64 = g.cb.t[0:64, ib:ib + 64]
    identf = g.cf.t[0:64, ib:ib + 64]
    o1 = M.CONST_COLS["ones_1"][0]
    ones64b = g.cb.t[0:64, o1:o1 + 64]
    raw = cv("rraw", [128, TT + 1], F32)
    dd = cv("rdd", [128, TT], F32)
    twl = cv("twl", [64, TT], F32)
    al = cv("ral", [64, TT], F32)
    sgl = cv("sgl", [128, TT], BF16)
    rr = cv("rr", [64, TT], F32)
    kk_ = cv("rk", [64, TT], F32)
    vv = cv("rv", [64, TT], F32)
    vb16 = cv("rvb", [64, TT], BF16)
    ew = cv("rew", [64, TT], F32)
    cum = cv("rcum", [64, TT], F32)
    ex = cv("rex", [64, TT], F32)
    aa = cv("raa", [64, TT], F32)
    kn = cv("rkn", [64, TT], F32)
    km = cv("rkm", [64, TT], F32)
    t1 = cv("rt1", [64, TT], F32)
    bon = cv("rbon", [64, TT], F32)
    gg_ = cv("rgg", [64, TT], F32)
    rt = cv("rrt", [64, TT], BF16)
    bt = cv("rbt", [64, TT], BF16)
    at = cv("rat", [64, TT], BF16)
    kt = cv("rkt", [64, TT], BF16)
    adT = cv("radT", [64, TT], BF16)
    kdT = cv("rkdT", [64, TT], BF16)
    pc = cv("rpc", [64, NCH], F32)
    vtok = cv("rvtok", [64, NCH, 64], BF16)
    adtok = cv("radtok", [64, NCH, 64], BF16)
    kdtok = cv("rkdtok", [64, NCH, 64], BF16)
    Ya = [cv("rYa%d" % i, [64, TT], F32) for i in range(2)]
    YTa = [cv("rYTa%d" % i, [64, TT], F32) for i in range(2)]
    R = cv("rR", [64, TT], F32)
    Tb = cv("rTb", [64, TT], BF16)
    MKT = cv("rMKT", [64, TT], BF16)
    NAT = cv("rNAT", [64, TT], BF16)
    NKT = cv("rNKT", [64, TT], BF16)
    zsb = cv("rz", [64, 64], BF16)
    usb = cv("ru", [64, 64], BF16)
    ysb = cv("rysb", [64, TT], F32)

    def shift_mix(dst, dstb, src_ps, srcb, npart, tile_id, mucol, mub):
        P.copy("dve", raw.t[0:npart, 0:1], o.sh.t[0:npart, tile_id:tile_id + 1], reads=o.sh.b, writes=raw.b)
        P.copy("act", raw.t[0:npart, 1:TT + 1], src_ps, reads=srcb, writes=raw.b)
        P.copy("dve", o.sh.t[0:npart, tile_id:tile_id + 1], raw.t[0:npart, TT:TT + 1], reads=raw.b, writes=o.sh.b)
        P.tt("dve", dd.t[0:npart, :], raw.t[0:npart, 0:TT], raw.t[0:npart, 1:TT + 1], ALU.subtract, reads=raw.b, writes=dd.b)
        P.stt("dve", dst, dd.t[0:npart, :], mucol, raw.t[0:npart, 1:TT + 1], ALU.mult, ALU.add, reads=dd.b + raw.b + mub, writes=dstb)

    wm, wmb = g.wload("oin", 3)
    proj_fm(g, ps[0], psb[0], wm, wmb, 0, 64)
    mc, mb = g.c64("rwkv_mu", 24)
    shift_mix(twl.t[:], twl.b, ps[0][0:64, :], [psb[0]], 64, 24, mc, mb)
    P.act(twl.t[:], twl.t[:], AF.Tanh, reads=twl.b, writes=twl.b)
    proj_fm(g, ps[1], psb[1], wm, wmb, 64, 64)
    mc, mb = g.c64("rwkv_mu", 25)
    shift_mix(al.t[:], al.b, ps[1][0:64, :], [psb[1]], 64, 25, mc, mb)
    proj_fm(g, ps[2], psb[2], wm, wmb, 128, 128)
    mc, mb = g.colf("rwkv_mu", 13)
    shift_mix(dd.t[:], dd.b, ps[2][:], [psb[2]], 128, 26, mc, mb)
    P.act(sgl.t[:], dd.t[:], AF.Sigmoid, reads=dd.b, writes=sgl.b)

    rst = g.flags.get("rstage", 99)

    def zy(h):
        P.memset("pool", o.yr.t[:, h, :], 0.0, writes=[o.yr.b[h]])
    wr, wrb = g.wload("oin", 0)
    wk, wkb = g.wload("oin", 1)
    wv, wvb = g.wload("oin", 2)
    for h in range(8):
        proj_fm(g, ps[0], psb[0], wr, wrb, h * 64, 64)
        mc, mb = g.c64("rwkv_mu", h)
        shift_mix(rr.t[:], rr.b, ps[0][0:64, :], [psb[0]], 64, h, mc, mb)
        proj_fm(g, ps[1], psb[1], wk, wkb, h * 64, 64)
        mc, mb = g.c64("rwkv_mu", 8 + h)
        shift_mix(kk_.t[:], kk_.b, ps[1][0:64, :], [psb[1]], 64, 8 + h, mc, mb)
        proj_fm(g, ps[2], psb[2], wv, wvb, h * 64, 64)
        mc, mb = g.c64("rwkv_mu", 16 + h)
        shift_mix(vv.t[:], vv.b, ps[2][0:64, :], [psb[2]], 64, 16 + h, mc, mb)
        P.copy("act", vb16.t[:], vv.t[:], reads=vv.b, writes=vb16.b)
        P.mm(ps[3][0:64, :], o.w2.t[:, h * 64:(h + 1) * 64], twl.t[:], reads=o.w2.b + twl.b, writes=[psb[3]])
        P.act(ew.t[:], ps[3][0:64, :], AF.Exp, scale=-1.0, bias=o.dc.t[0:64, 4 + h:5 + h], reads=[psb[3]] + o.dc.b, writes=ew.b)
        P.act(ew.t[:], ew.t[:], AF.Ln, bias=1.0, reads=ew.b, writes=ew.b)
        P.act(ew.t[:], ew.t[:], AF.Exp, scale=-1.0, bias=-0.5, reads=ew.b, writes=ew.b)
        P.op("dve", lambda e_: e_.tensor_tensor_scan(out=cum.t[:], data0=cF("reset", 64), data1=ew.t[:], initial=0.0,
                                                     op0=ALU.mult, op1=ALU.add), reads=ew.b + g.cf.b, writes=cum.b)
        P.mm(ps[4][0:64, :], o.a2.t[:, h * 64:(h + 1) * 64], al.t[:], reads=o.a2.b + al.b, writes=[psb[4]])
        a0c, a0b = g.c64("rwkv_a0", h)
        P.act(aa.t[:], ps[4][0:64, :], AF.Sigmoid, bias=a0c, reads=[psb[4]] + a0b, writes=aa.b)
        P.mm(ps[5][0:64, :], o.g2b.t[:, h * 64:(h + 1) * 64], sgl.t[:], reads=o.g2b.b + sgl.b, writes=[psb[5]])
        P.copy("act", gg_.t[:], ps[5][0:64, :], reads=[psb[5]], writes=gg_.b)
        kkc, kkb = g.c64("rwkv_k_k", h)
        P.ts("dve", kn.t[:], kk_.t[:], kkc, None, ALU.mult, reads=kk_.b + kkb, writes=kn.b)
        P.act(g.sq.t[0:64, 0, :], kn.t[:], AF.Square, reads=kn.b, writes=[g.sq.b[0]])
        P.mm(ps[6][0:64, :], ones64b, g.sq.t[0:64, 0, :], reads=g.cb.b + [g.sq.b[0]], writes=[psb[6]])
        inv_rsqrt(g, t1.t[:], t1.b, ps[6][0:64, :], [psb[6]], M.EPS)
        P.tt("dve", kn.t[:], kn.t[:], t1.t[:], ALU.mult, reads=kn.b + t1.b, writes=kn.b)
        kac, kab = g.c64("rwkv_k_a", h)
        P.ts("dve", km.t[:], aa.t[:], -1.0, kac, ALU.add, ALU.mult, reads=aa.b + kab, writes=km.b)
        P.stt("dve", km.t[:], km.t[:], 1.0, kk_.t[:], ALU.add, ALU.mult, reads=km.b + kk_.b, writes=km.b)
        rkc, rkb = g.c64("rwkv_r_k", h)
        P.stt("dve", t1.t[:], rr.t[:], rkc, km.t[:], ALU.mult, ALU.mult, reads=rr.b + rkb + km.b, writes=t1.b)
        P.copy("act", g.sq.t[0:64, 1, :], t1.t[:], reads=t1.b, writes=[g.sq.b[1]])
        P.mm(ps[7][0:64, :], ones64b, g.sq.t[0:64, 1, :], reads=g.cb.b + [g.sq.b[1]], writes=[psb[7]])
        P.tt("dve", bon.t[:], ps[7][0:64, :], vv.t[:], ALU.mult, reads=[psb[7]] + vv.b, writes=bon.b)
        P.stt("dve", t1.t[:], kn.t[:], -1.0, aa.t[:], ALU.mult, ALU.mult, reads=kn.b + aa.b, writes=t1.b)
        P.act(ex.t[:], cum.t[:], AF.Exp, scale=-1.0, reads=cum.b, writes=ex.b)
        P.tt("dve", rt.t[:], rr.t[:], ex.t[:], ALU.mult, reads=rr.b + ex.b, writes=rt.b)
        cum3 = cum.t[:].rearrange("p (n c) -> p n c", c=C)
        P.act(pc.t[:], cum3[:, :, C - 1], AF.Exp, scale=-1.0, reads=cum.b, writes=pc.b)
        P.tt("dve", ex.t[:], cum.t[:], ew.t[:], ALU.subtract, reads=cum.b + ew.b, writes=ex.b)
        P.act(ex.t[:], ex.t[:], AF.Exp, scale=-1.0, reads=ex.b, writes=ex.b)
        P.tt("dve", bt.t[:], kn.t[:], ex.t[:], ALU.mult, reads=kn.b + ex.b, writes=bt.b)
        P.act(ex.t[:], cum.t[:], AF.Exp, reads=cum.b, writes=ex.b)
        P.tt("dve", at.t[:], t1.t[:], ex.t[:], ALU.mult, reads=t1.b + ex.b, writes=at.b)
        P.tt("dve", kt.t[:], km.t[:], ex.t[:], ALU.mult, reads=km.b + ex.b, writes=kt.b)
        ex3 = ex.t[:].rearrange("p (n c) -> p n c", c=C)
        P.tt("dve", ex3, cum3[:, :, C - 1:C].broadcast_to([64, NCH, C]), cum3, ALU.subtract, reads=cum.b, writes=ex.b)
        P.act(ex.t[:], ex.t[:], AF.Exp, scale=-1.0, reads=ex.b, writes=ex.b)
        P.tt("dve", adT.t[:], t1.t[:], ex.t[:], ALU.mult, reads=t1.b + ex.b, writes=adT.b)
        P.tt("dve", kdT.t[:], km.t[:], ex.t[:], ALU.mult, reads=km.b + ex.b, writes=kdT.b)
        if rst <= 1:
            zy(h)
            continue
        for src, dst, bank in ((vb16, vtok, 0), (adT, adtok, 1), (kdT, kdtok, 2)):
            for n in range(NCH):
                sl = slice(n * 64, (n + 1) * 64)
                P.mm(ps[bank][0:64, sl], src.t[:, sl], identb64, reads=src.b + g.cb.b, writes=[psb[bank]])
            P.copy("act" if bank == 1 else "dve", dst.t[:].rearrange("p n c -> p (n c)"), ps[bank][0:64, :], reads=[psb[bank]], writes=dst.b)
        if rst <= 2:
            zy(h)
            continue
        for n in range(NCH):
            sl = slice(n * 64, (n + 1) * 64)
            P.mm(ps[3][0:64, sl], at.t[:, sl], bt.t[:, sl], reads=at.b + bt.b, writes=[psb[3]])
            P.mm(ps[4][0:64, sl], bt.t[:, sl], at.t[:, sl], reads=at.b + bt.b, writes=[psb[4]])
            P.mm(ps[5][0:64, sl], kt.t[:, sl], bt.t[:, sl], reads=kt.b + bt.b, writes=[psb[5]])
            P.mm(ps[6][0:64, sl], at.t[:, sl], rt.t[:, sl], reads=at.b + rt.b, writes=[psb[6]])
            P.mm(ps[7][0:64, sl], kt.t[:, sl], rt.t[:, sl], reads=kt.b + rt.b, writes=[psb[7]])
        P.tt("dve", Ya[0].t[:], ps[3][0:64, :], cF("m_strict", 64), ALU.mult, reads=[psb[3]] + g.cf.b, writes=Ya[0].b)
        P.tt("dve", YTa[0].t[:], ps[4][0:64, :], cF("m_strict_T", 64), ALU.mult, reads=[psb[4]] + g.cf.b, writes=YTa[0].b)
        P.tt("dve", MKT.t[:], ps[5][0:64, :], cF("m_strict", 64), ALU.mult, reads=[psb[5]] + g.cf.b, writes=MKT.b)
        P.tt("dve", NAT.t[:], ps[6][0:64, :], cF("m_incl", 64), ALU.mult, reads=[psb[6]] + g.cf.b, writes=NAT.b)
        P.tt("dve", NKT.t[:], ps[7][0:64, :], cF("m_incl", 64), ALU.mult, reads=[psb[7]] + g.cf.b, writes=NKT.b)
        if rst <= 3:
            zy(h)
            continue
        P.tt("dve", R.t[:], Ya[0].t[:], cF("ident64x", 64), ALU.add, reads=Ya[0].b + g.cf.b, writes=R.b)
        cur = 0
        for lvl in range(1, 6):
            nxt = 1 - cur
            for n in range(NCH):
                sl = slice(n * 64, (n + 1) * 64)
                P.mm(ps[0][0:64, sl], YTa[cur].t[:, sl], Ya[cur].t[:, sl], reads=YTa[cur].b + Ya[cur].b, writes=[psb[0]])
                P.mm(ps[1][0:64, sl], Ya[cur].t[:, sl], YTa[cur].t[:, sl], reads=YTa[cur].b + Ya[cur].b, writes=[psb[1]])
            if lvl < 5:
                P.copy("act", Ya[nxt].t[:], ps[0][0:64, :], reads=[psb[0]], writes=Ya[nxt].b)
            P.copy("dve", YTa[nxt].t[:], ps[1][0:64, :], reads=[psb[1]], writes=YTa[nxt].b)
            for n in range(NCH):
                sl = slice(n * 64, (n + 1) * 64)
                P.mm(ps[2][0:64, sl], YTa[nxt].t[:, sl], R.t[:, sl], reads=YTa[nxt].b + R.b, writes=[psb[2]])
            P.tt("dve", R.t[:], R.t[:], ps[2][0:64, :], ALU.add, reads=R.b + [psb[2]], writes=R.b)
            cur = nxt
        P.copy("act", Tb.t[:], R.t[:], reads=R.b, writes=Tb.b)
        if rst <= 4:
            zy(h)
            continue
        po, pob = ps[7], psb[7]
        for n in range(NCH):
            sl = slice(n * 64, (n + 1) * 64)
            P.mm(ps[3][0:64, 0:64], bt.t[:, sl], o.Ab.t[:, h, :], start=True, stop=False, reads=bt.b + [o.Ab.b[h]], writes=[psb[3]])
            P.mm(ps[3][0:64, 0:64], MKT.t[:, sl], vtok.t[:, n, :], start=False, stop=True, reads=MKT.b + vtok.b, writes=[psb[3]])
            P.copy("act", zsb.t[:], ps[3][0:64, 0:64], reads=[psb[3]], writes=zsb.b)
            P.mm(ps[4][0:64, 0:64], Tb.t[:, sl], zsb.t[:], reads=Tb.b + zsb.b, writes=[psb[4]])
            P.copy("dve", usb.t[:], ps[4][0:64, 0:64], reads=[psb[4]], writes=usb.b)
            P.mm(po[0:64, sl], o.Ab.t[:, h, :], rt.t[:, sl], start=True, stop=False, reads=[o.Ab.b[h]] + rt.b, writes=[pob])
            P.mm(po[0:64, sl], usb.t[:], NAT.t[:, sl], start=False, stop=False, reads=usb.b + NAT.b, writes=[pob])
            P.mm(po[0:64, sl], vtok.t[:, n, :], NKT.t[:, sl], start=False, stop=True, reads=vtok.b + NKT.b, writes=[pob])
            P.mm(ps[5][0:64, 0:64], adtok.t[:, n, :], usb.t[:], start=True, stop=False, reads=adtok.b + usb.b, writes=[psb[5]])
            P.mm(ps[5][0:64, 0:64], kdtok.t[:, n, :], vtok.t[:, n, :], start=False, stop=True, reads=kdtok.b + vtok.b, writes=[psb[5]])
            P.stt("dve", o.A.t[:, h, :], o.A.t[:, h, :], pc.t[:, n:n + 1], ps[5][0:64, 0:64], ALU.mult, ALU.add,
                  reads=[o.A.b[h], psb[5]] + pc.b, writes=[o.A.b[h]])
            P.copy("act", o.Ab.t[:, h, :], o.A.t[:, h, :], reads=[o.A.b[h]], writes=[o.Ab.b[h]])
        if rst <= 5:
            zy(h)
            continue
        om = M.CONST_COLS["bd_64m"][0]
        ones64m = g.cb.t[0:64, om:om + 64]
        P.copy("act", ysb.t[:], po[0:64, :], reads=[pob], writes=ysb.b)
        P.copy("dve", g.sq.t[0:64, 0, :], po[0:64, :], reads=[pob], writes=[g.sq.b[0]])
        P.mm(ps[6][0:64, :], ones64m, g.sq.t[0:64, 0, :], reads=g.cb.b + [g.sq.b[0]], writes=[psb[6]])
        P.tt("dve", ysb.t[:], ysb.t[:], ps[6][0:64, :], ALU.subtract, reads=ysb.b + [psb[6]], writes=ysb.b)
        P.act(g.sq.t[0:64, 1, :], ysb.t[:], AF.Square, reads=ysb.b, writes=[g.sq.b[1]])
        P.mm(ps[6][0:64, :], ones64m, g.sq.t[0:64, 1, :], reads=g.cb.b + [g.sq.b[1]], writes=[psb[6]])
        inv_rsqrt(g, t1.t[:], t1.b, ps[6][0:64, :], [psb[6]], RWKV_GN_EPS)
        lwc, lwb = g.c64("rwkv_ln_w", h)
        lbc, lbb = g.c64("rwkv_ln_b", h)
        P.stt("dve", ysb.t[:], ysb.t[:], lwc, t1.t[:], ALU.mult, ALU.mult, reads=ysb.b + lwb + t1.b, writes=ysb.b)
        P.stt("dve", ysb.t[:], ysb.t[:], lbc, bon.t[:], ALU.add, ALU.add, reads=ysb.b + lbb + bon.b, writes=ysb.b)
        P.tt("dve", o.yr.t[:, h, :], ysb.t[:], gg_.t[:], ALU.mult, reads=ysb.b + gg_.b, writes=[o.yr.b[h]])


def odd_mixer(g, tt):
    P, o = g.P, g.o
    g.phase()
    g.rmsnorm(g.xT, 1, 2, "pre")
    if g.flags.get("lru", True):
        lru(g, tt)
    else:
        for k in range(4):
            P.memset("pool", o.yl.t[:, 4 + k, :], 0.0, writes=[o.yl.b[4 + k]])
    g.phase()
    if g.flags.get("rwkv", True):
        rwkv(g, tt)
    else:
        for h in range(8):
            P.memset("pool", o.yr.t[:, h, :], 0.0, writes=[o.yr.b[h]])
    for half in range(2):
        wr, wrb = g.wload("oout_r", half)
        wl, wlb = g.wload("oout_l", half)
        for nn in range(4):
            n = half * 4 + nn
            pb, pbb = g.ps[n % 2], g.psb[n % 2]
            for h in range(8):
                P.mm(pb[:], wr[:, h, nn * 128:(nn + 1) * 128], o.yr.t[:, h, :], start=(h == 0), stop=False,
                     reads=wrb + [o.yr.b[h]], writes=[pbb])
            for k in range(4):
                P.mm(pb[:], wl[:, k, nn * 128:(nn + 1) * 128], o.yl.t[:, 4 + k, :], start=False, stop=(k == 3),
                     reads=wlb + [o.yl.b[4 + k]], writes=[pbb])
            P.copy("act", g.f.t[:, n, :], pb[:], reads=[pbb], writes=[g.f.b[n]])
    g.rmsnorm(g.f, 1, 3, "post", coef=1.0)


_PARAM_NAMES = ["norm_w", "ffn_w_gate", "ffn_w_up", "ffn_w_down", "even_w_in", "even_w_out", "gla_lora_w2", "gla_lora_b",
                "gla_norm", "gdn_conv", "gdn_a_log", "gdn_dt_bias", "gdn_norm", "odd_w_in", "odd_w_out", "rwkv_mu", "rwkv_w0",
                "rwkv_w2", "rwkv_a0", "rwkv_a2", "rwkv_g2", "rwkv_k_k", "rwkv_k_a", "rwkv_r_k", "rwkv_ln_w", "rwkv_ln_b",
                "lru_conv_w", "lru_conv_b", "lru_wa", "lru_ba", "lru_wx", "lru_bx", "lru_lambda"]


def kernel(**inputs):
    x = np.ascontiguousarray(np.asarray(inputs["x"], dtype=np.float32))
    B, T, _ = x.shape
    nc, g = build(T, 2, {})
    consts = make_consts()
    params = {k: np.ascontiguousarray(np.asarray(inputs[k], dtype=np.float32)) for k in _PARAM_NAMES}
    n_cores = 8
    in_maps = []
    for c in range(n_cores):
        m = dict(params)
        m["x"] = np.ascontiguousarray(x[c % B])
        m["consts"] = consts
        in_maps.append(m)
    res = run_bass_kernel_spmd(nc, in_maps, core_ids=list(range(n_cores)))
    out = np.stack([np.asarray(res.results[b]["out"], dtype=np.float32) for b in range(B)], 0)
    return out
```

```python
import numpy as np
import concourse.bass as bass
import concourse.mybir as mybir
from concourse.bass_utils import run_bass_kernel_spmd

F32 = mybir.dt.float32
BF16 = mybir.dt.bfloat16
AF = mybir.ActivationFunctionType
ALU = mybir.AluOpType
AX = mybir.AxisListType

USE_DRAIN = False
EPOCH = 20000
N_EPOCH = 12


class Buf:
    __slots__ = ("name", "w", "r", "excl")

    def __init__(self, name="", excl=False):
        self.name = name
        self.excl = excl
        self.w = None
        self.r = []


class Prog:
    ENGS = ("pe", "dve", "act", "pool", "sp")

    def __init__(self, nc, same_engine_sync=True):
        self.nc = nc
        self.ops = {e: [] for e in self.ENGS}
        self.cnt = {e: 0 for e in self.ENGS}
        self.waited = {}
        self.same_engine_sync = same_engine_sync
        self.dma_cnt = {}
        self.sems = {}
        self._ctx = []
        self.n_wait = 0
        self.barrier_streams = set(["c0"])
        self.slow_map = {}
        self.slow_pe = set()
        self.last_drain = None

    def _sem(self, key):
        if key not in self.sems:
            cm = self.nc.semaphore("s_%s_%s" % key if isinstance(key, tuple) else str(key))
            h = cm.__enter__()
            self._ctx.append(cm)
            self.sems[key] = h
        return self.sems[key]

    def _tok(self, eng):
        i = self.cnt[eng]
        self.cnt[eng] = i + 1
        return ((eng, i // EPOCH), (i % EPOCH) + 1)

    def _need(self, eng, tok):
        if tok is None:
            return None
        key, val = tok
        if key[0] == "pe" and eng != "pe":
            tv = (key[1], val)
            if tv in self.slow_map:
                key, val = self.slow_map[tv]
            elif tv in self.slow_pe:
                dtok = self.op("pe", lambda e: e.drain(), (), ())
                for t in list(self.slow_pe):
                    if t <= tv:
                        self.slow_map[t] = dtok
                        self.slow_pe.discard(t)
                key, val = dtok
        if key[0] == "dma":
            val = self.dma_cnt[key]
        if key[0] == eng and (eng == "pe" or not self.same_engine_sync):
            return None
        if self.waited.get((eng, key), 0) >= val:
            return None
        self.waited[(eng, key)] = val
        return (key, val)

    def op(self, eng, fn, reads=(), writes=()):
        waits = []
        for b in reads:
            w = self._need(eng, b.w)
            if w:
                waits.append(w)
            if b.excl:
                for t in b.r:
                    if t[0][0] != eng:
                        w = self._need(eng, t)
                        if w:
                            waits.append(w)
        for b in writes:
            w = self._need(eng, b.w)
            if w:
                waits.append(w)
            for t in b.r:
                w = self._need(eng, t)
                if w:
                    waits.append(w)
        mx = {}
        for k, v in waits:
            mx[k] = max(mx.get(k, 0), v)
        tok = self._tok(eng)
        for b in reads:
            b.r.append(tok)
        for b in writes:
            b.w = tok
            b.r = []
        self._sem(tok[0])
        for k in mx:
            self._sem(k)
        self.n_wait += len(mx)
        self.ops[eng].append((list(mx.items()), fn, tok[0], 1))
        return tok

    def dma(self, eng, out_ap, in_ap, stream, reads=(), writes=(), **kw):
        waits = []
        for b in reads:
            w = self._need(eng, b.w)
            if w:
                waits.append(w)
        for b in writes:
            w = self._need(eng, b.w)
            if w:
                waits.append(w)
            for t in b.r:
                w = self._need(eng, t)
                if w:
                    waits.append(w)
        mx = {}
        for k, v in waits:
            if k == ("dma", stream) and stream in self.barrier_streams:
                continue
            mx[k] = max(mx.get(k, 0), v)
        key = ("dma", stream)
        c = self.dma_cnt.get(key, 0) + 16
        self.dma_cnt[key] = c
        tok = (key, c)
        for b in reads:
            b.r.append(tok)
        for b in writes:
            b.w = tok
            b.r = []
        self._sem(key)
        for k in mx:
            self._sem(k)

        def fn(e, out_ap=out_ap, in_ap=in_ap, kw=kw):
            return e.dma_start(out=out_ap, in_=in_ap, **kw)
        self.ops[eng].append((list(mx.items()), fn, key, 16))
        return tok

    def pe_fence(self):
        self.op("pe", lambda e: e.drain(), (), ())

    def final_wait(self, eng, toks):
        for tok in toks:
            w = self._need(eng, tok)
            if w:
                self._sem(w[0])
                self.ops[eng].append(([w], None, None, 0))

    def mm(self, out, lhsT, rhs, start=True, stop=True, reads=(), writes=()):
        tok = self.op("pe", lambda e: e.matmul(out, lhsT, rhs, start=start, stop=stop), reads, writes)
        if USE_DRAIN and lhsT.dtype == F32:
            self.slow_pe.add((tok[0][1], tok[1]))
        return tok

    def transpose(self, out, in_, ident, reads=(), writes=()):
        return self.op("pe", lambda e: e.transpose(out, in_, ident), reads, writes)

    def act(self, out, in_, func, reads=(), writes=(), eng="act", **kw):
        return self.op(eng, lambda e: e.activation(out=out, in_=in_, func=func, **kw), reads, writes)

    def tt(self, eng, out, in0, in1, op, reads=(), writes=()):
        return self.op(eng, lambda e: e.tensor_tensor(out=out, in0=in0, in1=in1, op=op), reads, writes)

    def ts(self, eng, out, in0, s1, s2, op0, op1=None, reads=(), writes=(), **kw):
        if op1 is None:
            return self.op(eng, lambda e: e.tensor_scalar(out=out, in0=in0, scalar1=s1, scalar2=s2, op0=op0, **kw), reads, writes)
        return self.op(eng, lambda e: e.tensor_scalar(out=out, in0=in0, scalar1=s1, scalar2=s2, op0=op0, op1=op1, **kw), reads, writes)

    def stt(self, eng, out, in0, scalar, in1, op0, op1, reads=(), writes=()):
        return self.op(eng, lambda e: e.scalar_tensor_tensor(out=out, in0=in0, scalar=scalar, in1=in1, op0=op0, op1=op1), reads, writes)

    def copy(self, eng, out, in_, reads=(), writes=()):
        if eng == "act":
            return self.op(eng, lambda e: e.copy(out=out, in_=in_), reads, writes)
        return self.op(eng, lambda e: e.tensor_copy(out=out, in_=in_), reads, writes)

    def memset(self, eng, ap, val, writes=()):
        return self.op(eng, lambda e: e.memset(ap, val), (), writes)

    def emit(self):
        nc = self.nc
        sems = self.sems
        ops = self.ops
        with nc.Block() as block:
            def run(e, lst):
                for waits, fn, inckey, incv in lst:
                    for k, v in waits:
                        if k[0] == "dma" and k[1] in self.barrier_streams:
                            v = self.dma_cnt[k]
                        e.wait_ge(sems[k], v)
                    if fn is not None:
                        ins = fn(e)
                        ins.then_inc(sems[inckey], incv)

            @block.tensor
            def _(e):
                run(e, ops["pe"])

            @block.vector
            def _(e):
                run(e, ops["dve"])

            @block.scalar
            def _(e):
                run(e, ops["act"])

            @block.gpsimd
            def _(e):
                run(e, ops["pool"])

            @block.sync
            def _(e):
                run(e, ops["sp"])

    def close(self):
        for cm in reversed(self._ctx):
            cm.__exit__(None, None, None)
        self._ctx = []


class Alloc:
    def __init__(self, nc):
        self.nc = nc
        self._ctx = []

    def sb(self, name, shape, dt):
        cm = self.nc.sbuf_tensor(name, list(shape), dt)
        t = cm.__enter__()
        self._ctx.append(cm)
        return t

    def ps(self, name, shape, dt=F32):
        cm = self.nc.psum_tensor(name, list(shape), dt)
        t = cm.__enter__()
        self._ctx.append(cm)
        return t

    def close(self):
        for cm in reversed(self._ctx):
            cm.__exit__(None, None, None)
        self._ctx = []


D = 1024
DFF = 2816
NKC = 8
NM = 22
TT = 512
C = 64
NCH = TT // C
EPS = 1e-6
EVEN_IN = 3608
ODD_IN = 2816


class Tl:
    def __init__(self, A, name, shape, dt, nb=1):
        self.t = A.sb(name, shape, dt)
        self.b = [Buf(name + str(i)) for i in range(nb)]


CONST_COLS = {}


def make_consts():
    cols = []
    off = 0

    def add(name, arr):
        nonlocal off
        arr = np.asarray(arr, np.float32)
        assert arr.shape[0] == 128
        CONST_COLS[name] = (off, arr.shape[1])
        cols.append(arr)
        off += arr.shape[1]

    add("ident", np.eye(128))
    add("ones_d", np.full((128, 128), 1.0 / D))
    add("ones_128m", np.full((128, 128), 1.0 / 128))
    add("ones_1", np.ones((128, 128)))
    bd = np.zeros((128, 128)); bd[:64, :64] = 1; bd[64:, 64:] = 1
    add("bd_1", bd)
    add("bd_64m", bd / 64.0)
    s = np.arange(64)[:, None]; c = np.arange(64)[None, :]
    incl = (s <= c).astype(np.float32)
    strict = (s < c).astype(np.float32)
    add("m_incl", np.tile(np.concatenate([incl, incl], 0), (1, NCH)))
    add("m_strict", np.tile(np.concatenate([strict, strict], 0), (1, NCH)))
    add("m_incl_T", np.tile(np.concatenate([incl.T, incl.T], 0), (1, NCH)))
    add("m_strict_T", np.tile(np.concatenate([strict.T, strict.T], 0), (1, NCH)))
    rst = np.ones((128, TT)); rst[:, ::C] = 0.0
    add("reset", rst)
    add("ident64x", np.tile(np.concatenate([np.eye(64), np.eye(64)], 0), (1, NCH)))
    return np.concatenate(cols, 1)


class Ctx:
    pass


def build(T, L=2, flags=None, dbg=()):
    flags = flags or {}
    NT = T // TT
    nc = bass.Bass("TRN2", target_bir_lowering=False)
    A = Alloc(nc)
    P = Prog(nc, same_engine_sync=True)
    g = Ctx()
    g.nc, g.A, g.P, g.T, g.NT, g.L, g.flags = nc, A, P, T, NT, L, flags

    def din(name, shape):
        return nc.dram_tensor(name, list(shape), F32, kind="ExternalInput").ap()

    consts_np = make_consts()
    NCC = consts_np.shape[1]
    I = {}
    I["x"] = din("x", [T, D])
    I["consts"] = din("consts", [128, NCC])
    I["norm_w"] = din("norm_w", [2, 6, D])
    I["ffn_w_gate"] = din("ffn_w_gate", [2, 2, D, DFF])
    I["ffn_w_up"] = din("ffn_w_up", [2, 2, D, DFF])
    I["ffn_w_down"] = din("ffn_w_down", [2, 2, DFF, D])
    I["even_w_in"] = din("even_w_in", [1, D, EVEN_IN])
    I["even_w_out"] = din("even_w_out", [1, D, D])
    I["gla_lora_w2"] = din("gla_lora_w2", [1, 16, 256])
    I["gla_lora_b"] = din("gla_lora_b", [1, 256])
    I["gla_norm"] = din("gla_norm", [1, 128])
    I["gdn_conv"] = din("gdn_conv", [1, 4, 1536])
    I["gdn_a_log"] = din("gdn_a_log", [1, 4])
    I["gdn_dt_bias"] = din("gdn_dt_bias", [1, 4])
    I["gdn_norm"] = din("gdn_norm", [1, 128])
    I["odd_w_in"] = din("odd_w_in", [1, D, ODD_IN])
    I["odd_w_out"] = din("odd_w_out", [1, D, D])
    I["rwkv_mu"] = din("rwkv_mu", [1, 1792])
    for nm in ["rwkv_w0", "rwkv_a0", "rwkv_k_k", "rwkv_k_a", "rwkv_ln_w", "rwkv_ln_b",
               "lru_conv_b", "lru_ba", "lru_bx", "lru_lambda"]:
        I[nm] = din(nm, [1, 512])
    I["rwkv_w2"] = din("rwkv_w2", [1, 64, 512])
    I["rwkv_a2"] = din("rwkv_a2", [1, 64, 512])
    I["rwkv_g2"] = din("rwkv_g2", [1, 128, 512])
    I["rwkv_r_k"] = din("rwkv_r_k", [1, 8, 64])
    I["lru_conv_w"] = din("lru_conv_w", [1, 4, 512])
    I["lru_wa"] = din("lru_wa", [1, 8, 64, 64])
    I["lru_wx"] = din("lru_wx", [1, 8, 64, 64])
    g.I = I
    g.out = nc.dram_tensor("out", [T, D], F32, kind="ExternalOutput").ap()
    g.dbg = {}
    for nm, shp in dbg:
        g.dbg[nm] = nc.dram_tensor("dbg_" + nm, list(shp), F32, kind="ExternalOutput").ap()
    g.dbg_toks = []

    g.cf = Tl(A, "cf", [128, NCC], F32)
    P.dma("sp", g.cf.t[:], I["consts"], "c0", writes=g.cf.b)
    g.cb = Tl(A, "cb", [128, 768], BF16)
    P.copy("dve", g.cb.t[:], g.cf.t[:, 0:768], reads=g.cf.b, writes=g.cb.b)

    def cF(name, rows=128, c0=0, n=None):
        o, w = CONST_COLS[name]
        n = w if n is None else n
        return g.cf.t[0:rows, o + c0:o + c0 + n]

    def cB(name, rows=128, c0=0, n=None):
        o, w = CONST_COLS[name]
        n = w if n is None else n
        return g.cb.t[0:rows, o + c0:o + c0 + n]
    g.cF, g.cB = cF, cB

    g.ps = [A.ps("ps%d" % i, [128, 512]) for i in range(8)]
    g.psb = [Buf("ps%d" % i, excl=True) for i in range(8)]

    rows = []
    rows.append(("norm_w", I["norm_w"].rearrange("l i (k p) -> (l i k) p", p=128)))
    stage1 = rows
    rows2 = []
    rows2.append(("gla_lora_b", I["gla_lora_b"].rearrange("o (k p) -> (o k) p", p=128)))
    rows2.append(("gla_norm", I["gla_norm"]))
    rows2.append(("gdn_norm", I["gdn_norm"]))
    rows2.append(("gdn_conv", I["gdn_conv"].rearrange("o j (k p) -> (o j k) p", p=128)))
    rows2.append(("rwkv_mu", I["rwkv_mu"].rearrange("o (k p) -> (o k) p", p=128)))
    for nm in ["rwkv_w0", "rwkv_a0", "rwkv_k_k", "rwkv_k_a", "rwkv_ln_w", "rwkv_ln_b",
               "lru_conv_b", "lru_ba", "lru_bx", "lru_lambda"]:
        rows2.append((nm, I[nm].rearrange("o (k p) -> (o k) p", p=128)))
    rows2.append(("rwkv_r_k", I["rwkv_r_k"].rearrange("o (k a) n -> (o k) (a n)", a=2)))
    rows2.append(("lru_conv_w", I["lru_conv_w"].rearrange("o j (k p) -> (o j k) p", p=128)))
    g.col = {}
    for si, rws in enumerate([stage1, rows2]):
        st = Tl(A, "pst%d" % si, [128, 128], F32)
        P.memset("pool", st.t[:], 0.0, writes=st.b)
        r0 = 0
        for nm, ap in rws:
            r = ap.shape[0]
            P.dma("sp", st.t[r0:r0 + r, :], ap, "c0", writes=st.b)
            g.col[nm] = (si, r0, r)
            r0 += r
        assert r0 <= 128, r0
        ct = Tl(A, "pcol%d" % si, [128, 128], F32)
        P.mm(g.ps[7][:, 0:128], st.t[:], cF("ident"), reads=st.b + g.cf.b, writes=[g.psb[7]])
        P.copy("dve", ct.t[:], g.ps[7][:, 0:128], reads=[g.psb[7]], writes=ct.b)
        if si == 0:
            g.colt0 = ct
        else:
            g.colt1 = ct

    rows64 = [("gla_lora_b", I["gla_lora_b"].rearrange("o (k p) -> (o k) p", p=64))]
    for nm in ["rwkv_w0", "rwkv_a0", "rwkv_k_k", "rwkv_k_a", "rwkv_ln_w", "rwkv_ln_b"]:
        rows64.append((nm, I[nm].rearrange("o (k p) -> (o k) p", p=64)))
    rows64.append(("rwkv_r_k", I["rwkv_r_k"].rearrange("o k n -> (o k) n")))
    rows64.append(("rwkv_mu", I["rwkv_mu"].rearrange("o (k p) -> (o k) p", p=64)))
    st = Tl(A, "pst64", [128, 64], F32)
    P.memset("pool", st.t[:], 0.0, writes=st.b)
    g.col64 = {}
    r0 = 0
    for nm, ap in rows64:
        r = ap.shape[0]
        P.dma("sp", st.t[r0:r0 + r, :], ap, "c0", writes=st.b)
        g.col64[nm] = (r0, r)
        r0 += r
    assert r0 <= 128
    g.colt64 = Tl(A, "pcol64", [64, 128], F32)
    P.mm(g.ps[7][0:64, 0:128], st.t[:], cF("ident"), reads=st.b + g.cf.b, writes=[g.psb[7]])
    P.copy("dve", g.colt64.t[:], g.ps[7][0:64, 0:128], reads=[g.psb[7]], writes=g.colt64.b)

    def c64(name, idx=0):
        r0, r = g.col64[name]
        return g.colt64.t[:, r0 + idx:r0 + idx + 1], g.colt64.b
    g.c64 = c64

    def col(name, idx=0):
        si, r0, r = g.col[name]
        ct = g.colt0 if si == 0 else g.colt1
        return ct.t[:, r0 + idx:r0 + idx + 1], ct.b
    g.colf = col

    g.xT = Tl(A, "xT", [128, NKC, TT], F32, nb=NKC)
    g.hh = Tl(A, "hh", [128, NKC, TT], BF16, nb=NKC)
    ARENA = 62 * 1024
    g.arena = A.sb("arena", [128, ARENA // 2], BF16)
    g.ar_off = 0
    g.ar_bufs = []
    g.ar_tok = None
    g.fscr = Tl(A, "fscr", [128, 2], F32)

    def phase():
        old = g.ar_bufs
        g.ar_tok = P.op("dve", lambda e: e.memset(g.fscr.t[:, 0:1], 0.0), reads=(), writes=old + g.fscr.b)
        g.ar_bufs = []
        g.ar_off = 0

    def carve(name, shape, dt, nb=1):
        nbytes = int(np.prod(shape[1:])) * (4 if dt == F32 else 2)
        nbytes = (nbytes + 63) // 64 * 64
        assert g.ar_off + nbytes <= ARENA, (name, g.ar_off, nbytes)
        v = g.arena[0:shape[0], g.ar_off // 2:(g.ar_off + nbytes) // 2]
        g.ar_off += nbytes
        if dt == F32:
            v = v.bitcast(F32)
        n_el = int(np.prod(shape[1:]))
        v = v[:, 0:n_el]
        if len(shape) == 3:
            v = v.rearrange("p (a b) -> p a b", a=shape[1])
        t = Ctx()
        t.t = v
        t.b = [Buf(name + str(i)) for i in range(nb)]
        for b in t.b:
            b.w = g.ar_tok
        g.ar_bufs.extend(t.b)
        return t
    g.phase, g.carve = phase, carve
    g.f = Tl(A, "f", [128, NKC, TT], F32, nb=NKC)
    g.sq = Tl(A, "sq", [128, NKC, TT], BF16, nb=NKC)
    g.rstd = Tl(A, "rstd", [128, TT], F32)
    g.tmp = Tl(A, "tmp", [128, 2, TT], F32, nb=2)
    g.gsb = Tl(A, "gsb", [128, 2, TT], F32, nb=2)

    if flags.get("even", True) and L >= 1:
        setup_even(g)
    if flags.get("odd", True) and L >= 2:
        setup_odd(g)

    SLOT = 4096
    class _V:
        pass
    g.stg = []
    phase()
    _xs = carve("prep_stg", [128, 4, D], F32)
    for tl in (g.f, _xs):
        v = _V(); v.t = tl.t[:].rearrange("p a c -> p (a c)"); v.b = tl.b
        g.stg.append(v)
    g.cst = []
    for tl in (g.hh, g.sq):
        v = _V(); v.t = tl.t[:].rearrange("p a c -> p (a c)"); v.b = tl.b
        g.cst.append(v)
    g.prep_i = 0
    cast_eng = ["dve", "pool", "act"]

    def prep(src3, dst3, dbuf):
        np_, a, c = src3.shape[0], src3.shape[1], src3.shape[2]
        assert a * c <= SLOT
        i = g.prep_i
        g.prep_i += 1
        s = g.stg[i % 2]
        d = g.cst[i % 2]
        sv = s.t[0:np_, 0:a * c].rearrange("p (a c) -> p a c", a=a)
        dv = d.t[0:np_, 0:a * c].rearrange("p (a c) -> p a c", a=a)
        P.dma("sp", sv, src3, "stg%d" % (i % 2), writes=s.b)
        P.copy(cast_eng[i % 3], dv, sv, reads=s.b, writes=d.b)
        P.dma("act", dst3, dv, "cst%d" % (i % 2), reads=d.b, writes=[dbuf])

    def scratch(name, shape):
        return nc.dram_tensor(name, list(shape), BF16, kind="Internal").ap()

    g.W = {}

    def prep_matrix(name, src2d, pieces, p=128):
        K = src2d.shape[0]
        kc = K // p
        lst = []
        for pi, (c0, c1) in enumerate(pieces):
            w = c1 - c0
            sc = scratch("%s_%d" % (name, pi), [p, kc, w])
            bl = []
            src3 = src2d[:, c0:c1].rearrange("(k p) c -> p k c", p=p)
            kstep = max(1, SLOT // w)
            k0 = 0
            while k0 < kc:
                k1 = min(kc, k0 + kstep)
                b = Buf(name)
                bl.append(b)
                prep(src3[:, k0:k1, :], sc[:, k0:k1, :], b)
                k0 = k1
            lst.append((sc, bl))
        g.W[name] = lst

    g.prep_matrix = prep_matrix

    for l in range(L):
        for f in range(2):
            if flags.get("ffn", True):
                lst = []
                for grp in range(11):
                    sc = scratch("gu%d%d_%d" % (l, f, grp), [128, NKC, 512])
                    bl = []
                    for gi_, nm in enumerate(("ffn_w_gate", "ffn_w_up")):
                        b = Buf("gu")
                        bl.append(b)
                        src3 = I[nm][l, f][:, grp * 256:(grp + 1) * 256].rearrange("(k p) c -> p k c", p=128)
                        prep(src3, sc[:, :, gi_ * 256:(gi_ + 1) * 256], b)
                    lst.append((sc, bl))
                g.W["gu%d%d" % (l, f)] = lst
                prep_matrix("d%d%d" % (l, f), I["ffn_w_down"][l, f], [(i * 128, (i + 1) * 128) for i in range(8)])

    NR = 4
    RS = 4096
    g.ring = [Tl(A, "ring%d" % i, [128, RS], BF16) for i in range(NR)]
    g.ring_i = 0

    def wload(name, pi):
        sc, b = g.W[name][pi]
        np_, kc, w = sc.shape[0], sc.shape[1], sc.shape[2]
        assert kc * w <= RS
        i = g.ring_i
        g.ring_i += 1
        slot = g.ring[i % NR]
        v = slot.t[0:np_, 0:kc * w].rearrange("p (k c) -> p k c", k=kc)
        P.dma("sp", v, sc, "ring%d" % (i % NR), reads=b, writes=slot.b)
        return v, slot.b
    g.wload = wload

    def rmsnorm(src, l, i, mode, coef=1.0):
        for k in range(NKC):
            en = ("act", "dve", "pool", "act", "dve", "act", "pool", "dve")[k]
            if en == "act":
                P.act(g.sq.t[:, k, :], src.t[:, k, :], AF.Square, reads=[src.b[k]], writes=[g.sq.b[k]])
            else:
                P.tt(en, g.sq.t[:, k, :], src.t[:, k, :], src.t[:, k, :], ALU.mult, reads=[src.b[k]], writes=[g.sq.b[k]])
        for k in range(NKC):
            P.mm(g.ps[6][:], g.cb.t[:, CONST_COLS["ones_d"][0]:CONST_COLS["ones_d"][0] + 128], g.sq.t[:, k, :],
                 start=(k == 0), stop=(k == NKC - 1), reads=[g.sq.b[k]] + g.cb.b, writes=[g.psb[6]])
        P.act(g.rstd.t[:], g.ps[6][:], AF.Ln, bias=EPS, reads=[g.psb[6]], writes=g.rstd.b)
        lnc = float(np.log(coef)) if (mode == "post" and coef != 1.0) else 0.0
        P.act(g.rstd.t[:], g.rstd.t[:], AF.Exp, scale=-0.5, bias=lnc, reads=g.rstd.b, writes=g.rstd.b)
        for k in range(NKC):
            wc, wb = col("norm_w", (l * 6 + i) * 8 + k)
            if mode == "pre":
                P.stt("dve", g.hh.t[:, k, :], src.t[:, k, :], wc, g.rstd.t[:], ALU.mult, ALU.mult,
                      reads=[src.b[k]] + wb + g.rstd.b, writes=[g.hh.b[k]])
            else:
                tb = k % 2
                P.stt("dve", g.tmp.t[:, tb, :], src.t[:, k, :], wc, g.rstd.t[:], ALU.mult, ALU.mult,
                      reads=[src.b[k]] + wb + g.rstd.b, writes=[g.tmp.b[tb]])
                P.tt("pool", g.xT.t[:, k, :], g.tmp.t[:, tb, :], g.xT.t[:, k, :], ALU.add,
                     reads=[g.tmp.b[tb], g.xT.b[k]], writes=[g.xT.b[k]])
    g.rmsnorm = rmsnorm

    def ffn(l, f):
        phase()
        g.hid = carve("hid", [128, NM, TT], BF16, nb=NM)
        rmsnorm(g.xT, l, 0 if f == 0 else 4, "pre")
        for grp in range(11):
            wg, wgb = wload("gu%d%d" % (l, f), grp)
            wu, wub = wg[:, :, 256:512], wgb
            for j in range(2):
                m = grp * 2 + j
                pg, pu = g.ps[m % 2], g.ps[2 + m % 2]
                pgb, pub = g.psb[m % 2], g.psb[2 + m % 2]
                for k in range(NKC):
                    P.mm(pg[:], wg[:, k, j * 128:(j + 1) * 128], g.hh.t[:, k, :], start=(k == 0), stop=(k == NKC - 1),
                         reads=wgb + [g.hh.b[k]], writes=[pgb])
                for k in range(NKC):
                    P.mm(pu[:], wu[:, k, j * 128:(j + 1) * 128], g.hh.t[:, k, :], start=(k == 0), stop=(k == NKC - 1),
                         reads=wub + [g.hh.b[k]], writes=[pub])
                P.act(g.gsb.t[:, m % 2, :], pg[:], AF.Silu, reads=[pgb], writes=[g.gsb.b[m % 2]])
                P.tt("dve", g.hid.t[:, m, :], g.gsb.t[:, m % 2, :], pu[:], ALU.mult,
                     reads=[g.gsb.b[m % 2], pub], writes=[g.hid.b[m]])
        for n in range(NKC):
            wd, wdb = wload("d%d%d" % (l, f), n)
            pd, pdb = g.ps[4 + n % 2], g.psb[4 + n % 2]
            for m in range(NM):
                P.mm(pd[:], wd[:, m, :], g.hid.t[:, m, :], start=(m == 0), stop=(m == NM - 1),
                     reads=wdb + [g.hid.b[m]], writes=[pdb])
            P.copy("act", g.f.t[:, n, :], pd[:], reads=[pdb], writes=[g.f.b[n]])
        rmsnorm(g.f, l, 1 if f == 0 else 5, "post", coef=0.5)
    g.ffn = ffn

    def load_x(tt):
        t0 = tt * TT
        phase()
        g.xin = carve("xin", [128, 4, D], F32)
        P.dma("pool", g.xin.t[:], I["x"][t0:t0 + TT, :].rearrange("(g p) d -> p g d", p=128), "xin", writes=g.xin.b)
        for k in range(NKC):
            pb = g.ps[k % 2]
            for gi in range(4):
                P.mm(pb[:, gi * 128:(gi + 1) * 128], g.xin.t[:, gi, k * 128:(k + 1) * 128], cF("ident"),
                     reads=g.xin.b + g.cf.b, writes=[g.psb[k % 2]])
            P.copy("act" if k % 2 else "dve", g.xT.t[:, k, :], pb[:], reads=[g.psb[k % 2]], writes=[g.xT.b[k]])

    def store_x(tt):
        t0 = tt * TT
        phase()
        g.xin = carve("xin", [128, 4, D], F32)
        for gi in range(4):
            for h in range(2):
                pb, pbb = g.ps[(gi * 2 + h) % 2], g.psb[(gi * 2 + h) % 2]
                for kk in range(4):
                    k = h * 4 + kk
                    P.mm(pb[:, kk * 128:(kk + 1) * 128], g.xT.t[:, k, gi * 128:(gi + 1) * 128], cF("ident"),
                         reads=[g.xT.b[k]] + g.cf.b, writes=[pbb])
                P.copy("act" if h else "dve", g.xin.t[:, gi, h * 512:(h + 1) * 512], pb[:], reads=[pbb], writes=g.xin.b)
        return P.dma("pool", g.out[t0:t0 + TT, :].rearrange("(g p) d -> p g d", p=128), g.xin.t[:], "xout", reads=g.xin.b)

    def dump(name, ap_sb, bufs, dram_view=None):
        dv = g.dbg[name] if dram_view is None else dram_view
        g.dbg_toks.append(P.dma("pool", dv, ap_sb, "dbg", reads=bufs))
    g.dump = dump

    if flags.get("even", True) and L >= 1:
        setup_even_w(g)
    if flags.get("odd", True) and L >= 2:
        setup_odd_w(g)

    last = None
    for tt in range(NT):
        load_x(tt)
        for l in range(L):
            if flags.get("ffn", True):
                ffn(l, 0)
            if l == 0 and flags.get("even", True):
                even_mixer(g, tt)
            if l == 1 and flags.get("odd", True):
                odd_mixer(g, tt)
            if flags.get("ffn", True):
                ffn(l, 1)
        last = store_x(tt)
    P.final_wait("pool", [last] + g.dbg_toks)
    P.emit()
    P.close()
    A.close()
    g.n_instr = dict(P.cnt)
    return nc, g


class M:
    pass


M.Tl = Tl
M.Ctx = Ctx
M.CONST_COLS = CONST_COLS
M.EPS = EPS


TT, C, NCH, NKC = 512, 64, 8, 8


def setup_even(g):
    A, P, I = g.A, g.P, g.I
    Tl = M.Tl
    e = M.Ctx()
    g.e = e
    e.w2 = Tl(A, "gla_w2", [16, 256], F32)
    P.dma("sp", e.w2.t[:], I["gla_lora_w2"][0], "c0", writes=e.w2.b)
    e.neglb = Tl(A, "neglb", [64, 4], F32)
    for h in range(4):
        c_, cb_ = g.c64("gla_lora_b", h)
        P.ts("dve", e.neglb.t[:, h:h + 1], c_, -1.0, None, ALU.mult, reads=cb_, writes=e.neglb.b)
    e.S = Tl(A, "gla_S", [64, 4, 128], F32)
    e.Sb = Tl(A, "gla_Sb", [64, 4, 128], BF16)
    P.memset("pool", e.S.t[:], 0.0, writes=e.S.b)
    P.memset("pool", e.Sb.t[:], 0.0, writes=e.Sb.b)
    e.ycat = Tl(A, "ycat", [128, 8, TT], BF16, nb=8)
    if not g.flags.get("gdn", True):
        for k in range(4, 8):
            P.memset("pool", e.ycat.t[:, k, :], 0.0, writes=[e.ycat.b[k]])
    if g.flags.get("gdn", True):
        setup_gdn(g)


def setup_even_w(g):
    I = g.I
    pieces = [(0, 512), (512, 1024), (1024, 1536), (1536, 1552), (1552, 2064), (2064, 2576), (2576, 3088),
              (3088, 3600), (3600, 3608)]
    g.prep_matrix("ein", I["even_w_in"][0], pieces)
    g.prep_matrix("eout", I["even_w_out"][0], [(0, 512), (512, 1024)])


def proj_fm(g, ps, psb, w, wb, c0, ncols):
    P = g.P
    for k in range(NKC):
        P.mm(ps[0:ncols, :], w[:, k, c0:c0 + ncols], g.hh.t[:, k, :], start=(k == 0), stop=(k == NKC - 1),
             reads=wb + [g.hh.b[k]], writes=[psb])


def head_norm_gate(g, ps_o, ps_ob, normcol, normb, gate_ap, gate_b, out_ap, out_b, stat_bank):
    P, e = g.P, g.e
    P.copy("act", e.osb.t[:], ps_o[:], reads=[ps_ob], writes=e.osb.b)
    P.act(g.sq.t[:, 0, :], ps_o[:], AF.Square, reads=[ps_ob], writes=[g.sq.b[0]])
    sp_, spb = g.ps[stat_bank], g.psb[stat_bank]
    o1 = M.CONST_COLS["ones_128m"][0]
    P.mm(sp_[:], g.cb.t[:, o1:o1 + 128], g.sq.t[:, 0, :], reads=g.cb.b + [g.sq.b[0]], writes=[spb])
    P.act(g.rstd.t[:], sp_[:], AF.Ln, bias=M.EPS, reads=[spb], writes=g.rstd.b)
    P.act(g.rstd.t[:], g.rstd.t[:], AF.Exp, scale=-0.5, reads=g.rstd.b, writes=g.rstd.b)
    P.stt("dve", e.osb.t[:], e.osb.t[:], normcol, g.rstd.t[:], ALU.mult, ALU.mult,
          reads=e.osb.b + normb + g.rstd.b, writes=e.osb.b)
    P.tt("dve", out_ap, e.osb.t[:], gate_ap, ALU.mult, reads=e.osb.b + gate_b, writes=out_b)


def zero_y(g):
    for k in range(4):
        g.P.memset("pool", g.e.ycat.t[:, k, :], 0.0, writes=[g.e.ycat.b[k]])


def gla(g, tt):
    P, e = g.P, g.e
    ps, psb = g.ps, g.psb
    cv = g.carve
    e.qe = cv("qe", [64, 4, TT], BF16, nb=4)
    e.ke = cv("ke", [64, 4, TT], BF16, nb=4)
    e.kd = cv("kd", [64, 4, TT], BF16, nb=4)
    e.ksb = cv("ksb", [64, TT], F32)
    e.cs = cv("cs", [64, TT], F32)
    e.ex = cv("ex", [64, 2, TT], F32, nb=2)
    e.alast = cv("alast", [64, 4, NCH], F32, nb=4)
    e.glr = cv("glr", [16, TT], F32)
    e.vtok = cv("vtok", [64, NCH, 512], BF16, nb=NCH)
    e.kdtok = cv("kdtok", [64, NCH, 256], BF16, nb=NCH)
    e.sgg = cv("sgg", [128, 4, TT], BF16, nb=4)
    e.att = cv("att", [64, 4, 512], BF16, nb=4)
    e.osb = cv("osb", [128, TT], F32)
    wl, wlb = g.wload("ein", 3)
    proj_fm(g, ps[0], psb[0], wl, wlb, 0, 16)
    P.copy("act", e.glr.t[:], ps[0][0:16, :], reads=[psb[0]], writes=e.glr.b)
    wqk, wqkb = g.wload("ein", 0)
    for h in range(4):
        b0 = (h % 2) * 2
        P.mm(ps[b0][0:64, :], e.w2.t[0:16, h * 64:(h + 1) * 64], e.glr.t[0:16, :], reads=e.w2.b + e.glr.b, writes=[psb[b0]])
        P.act(e.ex.t[:, 0, :], ps[b0][0:64, :], AF.Exp, scale=-1.0, bias=e.neglb.t[:, h:h + 1], reads=[psb[b0]] + e.neglb.b, writes=[e.ex.b[0]])
        P.act(e.ex.t[:, 0, :], e.ex.t[:, 0, :], AF.Ln, bias=1.0, reads=[e.ex.b[0]], writes=[e.ex.b[0]])
        P.op("dve", lambda e_: e_.tensor_tensor_scan(out=e.cs.t[:], data0=g.cF("reset", 64), data1=e.ex.t[:, 0, :], initial=0.0,
                                                     op0=ALU.mult, op1=ALU.add), reads=[e.ex.b[0]] + g.cf.b, writes=e.cs.b)
        proj_fm(g, ps[b0 + 1], psb[b0 + 1], wqk, wqkb, h * 64, 64)
        P.act(e.ex.t[:, 1, :], e.cs.t[:], AF.Exp, scale=-1.0 / 16, reads=e.cs.b, writes=[e.ex.b[1]])
        P.stt("dve", e.qe.t[:, h, :], ps[b0 + 1][0:64, :], 0.125, e.ex.t[:, 1, :], ALU.mult, ALU.mult,
              reads=[psb[b0 + 1], e.ex.b[1]], writes=[e.qe.b[h]])
        proj_fm(g, ps[b0], psb[b0], wqk, wqkb, 256 + h * 64, 64)
        P.copy("act", e.ksb.t[:], ps[b0][0:64, :], reads=[psb[b0]], writes=e.ksb.b)
        P.act(e.ex.t[:, 1, :], e.cs.t[:], AF.Exp, scale=1.0 / 16, reads=e.cs.b, writes=[e.ex.b[1]])
        P.tt("dve", e.ke.t[:, h, :], e.ksb.t[:], e.ex.t[:, 1, :], ALU.mult, reads=e.ksb.b + [e.ex.b[1]], writes=[e.ke.b[h]])
        cs3 = e.cs.t[:].rearrange("p (n c) -> p n c", c=C)
        ex3 = e.ex.t[:, 0, :].rearrange("p (n c) -> p n c", c=C)
        P.tt("dve", ex3, cs3[:, :, C - 1:C].broadcast_to([64, NCH, C]), cs3, ALU.subtract, reads=e.cs.b, writes=[e.ex.b[0]])
        P.act(e.ex.t[:, 0, :], e.ex.t[:, 0, :], AF.Exp, scale=-1.0 / 16, reads=[e.ex.b[0]], writes=[e.ex.b[0]])
        P.tt("dve", e.kd.t[:, h, :], e.ksb.t[:], e.ex.t[:, 0, :], ALU.mult, reads=e.ksb.b + [e.ex.b[0]], writes=[e.kd.b[h]])
        P.act(e.alast.t[:, h, :], cs3[:, :, C - 1], AF.Exp, scale=-1.0 / 16, reads=e.cs.b, writes=[e.alast.b[h]])
    wv, wvb = g.wload("ein", 1)
    for n in range(NCH):
        pb, pbb = ps[n % 4], psb[n % 4]
        for k in range(NKC):
            P.mm(pb[0:64, :], g.hh.t[:, k, n * 64:(n + 1) * 64], wv[:, k, :], start=(k == 0), stop=(k == NKC - 1),
                 reads=[g.hh.b[k]] + wvb, writes=[pbb])
        P.copy("act" if n % 2 else "dve", e.vtok.t[:, n, :], pb[0:64, :], reads=[pbb], writes=[e.vtok.b[n]])
    wg, wgb = g.wload("ein", 2)
    for h in range(4):
        pb, pbb = ps[4 + h % 2], psb[4 + h % 2]
        proj_fm(g, pb, pbb, wg, wgb, h * 128, 128)
        P.act(e.sgg.t[:, h, :], pb[:], AF.Silu, reads=[pbb], writes=[e.sgg.b[h]])
    ib = M.CONST_COLS["ident"][0]
    for n in range(NCH):
        pb, pbb = ps[n % 4], psb[n % 4]
        for h in range(4):
            P.mm(pb[0:64, h * 64:(h + 1) * 64], e.kd.t[:, h, n * 64:(n + 1) * 64], g.cb.t[0:64, ib:ib + 64],
                 reads=[e.kd.b[h]] + g.cb.b, writes=[pbb])
        P.copy("act" if n % 2 else "dve", e.kdtok.t[:, n, :], pb[0:64, 0:256], reads=[pbb], writes=[e.kdtok.b[n]])
    for h in range(4):
        pb, pbb = ps[h], psb[h]
        for n in range(NCH):
            P.mm(pb[0:64, n * 64:(n + 1) * 64], e.ke.t[:, h, n * 64:(n + 1) * 64], e.qe.t[:, h, n * 64:(n + 1) * 64],
                 reads=[e.ke.b[h], e.qe.b[h]], writes=[pbb])
        P.tt("dve", e.att.t[:, h, :], pb[0:64, :], g.cF("m_incl", 64), ALU.mult, reads=[pbb] + g.cf.b, writes=[e.att.b[h]])
    for n in range(NCH):
        for h in range(4):
            po, pob = ps[4 + h], psb[4 + h]
            P.mm(po[:, n * 64:(n + 1) * 64], e.vtok.t[:, n, h * 128:(h + 1) * 128], e.att.t[:, h, n * 64:(n + 1) * 64],
                 start=True, stop=False, reads=[e.vtok.b[n], e.att.b[h]], writes=[pob])
            P.mm(po[:, n * 64:(n + 1) * 64], e.Sb.t[:, h, :], e.qe.t[:, h, n * 64:(n + 1) * 64],
                 start=False, stop=True, reads=e.Sb.b + [e.qe.b[h]], writes=[pob])
        for h in range(4):
            P.mm(ps[3][0:64, h * 128:(h + 1) * 128], e.kdtok.t[:, n, h * 64:(h + 1) * 64],
                 e.vtok.t[:, n, h * 128:(h + 1) * 128], reads=[e.kdtok.b[n], e.vtok.b[n]], writes=[psb[3]])
        for h in range(4):
            P.stt("dve", e.S.t[:, h, :], e.S.t[:, h, :], e.alast.t[:, h, n:n + 1], ps[3][0:64, h * 128:(h + 1) * 128], ALU.mult, ALU.add,
                  reads=e.S.b + [e.alast.b[h], psb[3]], writes=e.S.b)
        P.copy("act", e.Sb.t[:], e.S.t[:], reads=e.S.b, writes=e.Sb.b)
    nc_, nb_ = g.colf("gla_norm", 0)
    for h in range(4):
        head_norm_gate(g, ps[4 + h], psb[4 + h], nc_, nb_, e.sgg.t[:, h, :], [e.sgg.b[h]], e.ycat.t[:, h, :], [e.ycat.b[h]], 0)


def even_mixer(g, tt):
    P, e = g.P, g.e
    g.phase()
    g.rmsnorm(g.xT, 0, 2, "pre")
    gla(g, tt)
    if g.flags.get("gdn", True):
        gdn(g, tt)
    for half in range(2):
        wo, wob = g.wload("eout", half)
        for nn in range(4):
            n = half * 4 + nn
            pb, pbb = g.ps[n % 2], g.psb[n % 2]
            for k in range(NKC):
                P.mm(pb[:], wo[:, k, nn * 128:(nn + 1) * 128], e.ycat.t[:, k, :], start=(k == 0), stop=(k == NKC - 1),
                     reads=wob + [e.ycat.b[k]], writes=[pbb])
            P.copy("act", g.f.t[:, n, :], pb[:], reads=[pbb], writes=[g.f.b[n]])
    g.rmsnorm(g.f, 0, 3, "post", coef=1.0)


TT, C, NCH, NKC = 512, 64, 8, 8


def setup_gdn(g):
    A, P, I = g.A, g.P, g.I
    Tl = M.Tl
    e = g.e
    e.hist = Tl(A, "gdn_hist", [128, 12, 3], F32, nb=12)
    P.memset("pool", e.hist.t[:], 0.0, writes=e.hist.b)
    e.GS = Tl(A, "gdn_S", [128, 4, 128], F32, nb=4)
    e.GSb = Tl(A, "gdn_Sb", [128, 4, 128], BF16, nb=4)
    P.memset("pool", e.GS.t[:], 0.0, writes=e.GS.b)
    P.memset("pool", e.GSb.t[:], 0.0, writes=e.GSb.b)
    e.negA = Tl(A, "negA", [64, 4], F32)
    e.dtb = Tl(A, "dtb", [64, 4], F32)
    P.dma("sp", e.negA.t[:], I["gdn_a_log"][0].partition_broadcast(64), "c0", writes=e.negA.b)
    P.dma("sp", e.dtb.t[:], I["gdn_dt_bias"][0].partition_broadcast(64), "c0", writes=e.dtb.b)
    P.act(e.negA.t[:], e.negA.t[:], AF.Exp, reads=e.negA.b, writes=e.negA.b)
    P.ts("dve", e.negA.t[:], e.negA.t[:], -1.0, None, ALU.mult, reads=e.negA.b, writes=e.negA.b)


def inv_rsqrt(g, out_ap, out_b, in_ap, in_b, eps):
    P = g.P
    P.act(out_ap, in_ap, AF.Ln, bias=eps, reads=in_b, writes=out_b)
    P.act(out_ap, out_ap, AF.Exp, scale=-0.5, reads=out_b, writes=out_b)


def gdn(g, tt):
    P, e = g.P, g.e
    ps, psb = g.ps, g.psb
    cF, cB = g.cF, g.cB
    g.phase()
    cv = g.carve
    ib = M.CONST_COLS["ident"][0]
    identb = g.cb.t[:, ib:ib + 128]
    identf = g.cf.t[0:64, ib:ib + 64]
    o1 = M.CONST_COLS["ones_1"][0]

    ab = cv("ab", [64, NCH, 8], F32)
    wab, wabb = g.wload("ein", 8)
    for n in range(NCH):
        for k in range(NKC):
            P.mm(ps[0][0:64, n * 8:(n + 1) * 8], g.hh.t[:, k, n * 64:(n + 1) * 64], wab[:, k, 0:8], start=(k == 0), stop=(k == NKC - 1),
                 reads=[g.hh.b[k]] + wabb, writes=[psb[0]])
    P.copy("dve", ab.t[:].rearrange("p n c -> p (n c)"), ps[0][0:64, 0:64], reads=[psb[0]], writes=ab.b)
    beta = cv("beta", [64, NCH, 4], F32)
    gg = cv("gg", [64, NCH, 4], F32)
    P.act(beta.t[:], ab.t[:, :, 4:8], AF.Sigmoid, reads=ab.b, writes=beta.b)
    P.tt("dve", gg.t[:], ab.t[:, :, 0:4], e.dtb.t[:].unsqueeze(1).broadcast_to([64, NCH, 4]), ALU.add, reads=ab.b + e.dtb.b, writes=gg.b)
    P.act(gg.t[:], gg.t[:], AF.Exp, reads=gg.b, writes=gg.b)
    P.act(gg.t[:], gg.t[:], AF.Ln, bias=1.0, reads=gg.b, writes=gg.b)
    P.tt("dve", gg.t[:], gg.t[:], e.negA.t[:].unsqueeze(1).broadcast_to([64, NCH, 4]), ALU.mult, reads=gg.b + e.negA.b, writes=gg.b)
    gflat = gg.t[:].rearrange("p n h -> p (n h)")
    gc = cv("gc", [64, NCH, 4], F32)
    egl = cv("egl", [128, NCH, 4], F32)
    sc_kbe = cv("sc_kbe", [64, NCH, 4], F32)
    sc_kd = cv("sc_kd", [64, NCH, 4], F32)
    P.mm(ps[1][0:64, 0:32], cF("m_incl", 64, 0, 64), gflat, reads=g.cf.b + gg.b, writes=[psb[1]])
    P.mm(ps[1][:, 32:64], cF("ones_1", 64), gflat, reads=g.cf.b + gg.b, writes=[psb[1]])
    P.copy("dve", gc.t[:].rearrange("p n h -> p (n h)"), ps[1][0:64, 0:32], reads=[psb[1]], writes=gc.b)
    P.act(egl.t[:].rearrange("p n h -> p (n h)"), ps[1][:, 32:64], AF.Exp, reads=[psb[1]], writes=egl.b)
    P.tt("dve", sc_kd.t[:].rearrange("p n h -> p (n h)"), ps[1][0:64, 32:64], gc.t[:].rearrange("p n h -> p (n h)"), ALU.subtract,
         reads=[psb[1]] + gc.b, writes=sc_kd.b)
    P.act(sc_kd.t[:], sc_kd.t[:], AF.Exp, reads=sc_kd.b, writes=sc_kd.b)
    P.act(sc_kbe.t[:], gc.t[:], AF.Exp, reads=gc.b, writes=sc_kbe.b)
    P.tt("dve", sc_kbe.t[:], sc_kbe.t[:], beta.t[:], ALU.mult, reads=sc_kbe.b + beta.b, writes=sc_kbe.b)

    st = g.flags.get("gstage", 99)

    def zy(h):
        P.memset("pool", e.ycat.t[:, 4 + h, :], 0.0, writes=[e.ycat.b[4 + h]])
    if st <= 0:
        for h in range(4):
            zy(h)
        return
    raw = cv("raw", [128, TT + 3], F32)
    acc = cv("acc", [128, TT], F32)
    cc = cv("cc", [128, TT], F32)
    qT = cv("qT", [128, TT], BF16)
    kT = cv("kT", [128, TT], BF16)
    qeT = cv("qeT", [128, TT], BF16)
    G = cv("G", [64, NCH, C], F32)
    grep = cv("grep", [64, NCH, 128], F32)
    EB = cv("EB", [128, TT], F32)
    kbe = cv("kbe", [64, NCH, 128], BF16)
    kdc = cv("kdc", [64, NCH, 128], BF16)
    vb = cv("vb", [64, NCH, 128], BF16)
    E = cv("E", [64, TT], F32)
    Ya = [cv("Ya%d" % i, [64, TT], F32) for i in range(2)]
    YTa = [cv("YTa%d" % i, [64, TT], F32) for i in range(2)]
    R = cv("R", [64, TT], F32)
    att = cv("gatt", [64, TT], BF16)
    Tb = cv("Tb", [64, TT], BF16)
    u = cv("u", [64, NCH, 128], F32)
    wT = cv("wT", [128, TT], BF16)
    vnew = cv("vnew", [64, 128], BF16)
    sgz = cv("sgz", [128, TT], BF16)
    osb = cv("gosb", [128, TT], F32)
    e.osb = osb
    wdq = [None] * 3

    def conv_tile(k12, w3, w3b, cbase):
        proj_fm(g, ps[2], psb[2], w3, w3b, cbase, 128)
        P.copy("dve", raw.t[:, 0:3], e.hist.t[:, k12, :], reads=[e.hist.b[k12]], writes=raw.b)
        P.copy("act", raw.t[:, 3:TT + 3], ps[2][:], reads=[psb[2]], writes=raw.b)
        P.copy("dve", e.hist.t[:, k12, :], raw.t[:, TT:TT + 3], reads=raw.b, writes=[e.hist.b[k12]])
        c0, c0b = g.colf("gdn_conv", 0 * 12 + k12)
        P.ts("dve", acc.t[:], raw.t[:, 0:TT], c0, None, ALU.mult, reads=raw.b + c0b, writes=acc.b)
        for j in range(1, 4):
            cj, cjb = g.colf("gdn_conv", j * 12 + k12)
            P.stt("dve", acc.t[:], raw.t[:, j:j + TT], cj, acc.t[:], ALU.mult, ALU.add, reads=raw.b + cjb + acc.b, writes=acc.b)
        P.act(cc.t[:], acc.t[:], AF.Silu, reads=acc.b, writes=cc.b)

    def l2n(dst, scale):
        P.act(g.sq.t[:, 0, :], cc.t[:], AF.Square, reads=cc.b, writes=[g.sq.b[0]])
        P.mm(ps[3][:], g.cb.t[:, o1:o1 + 128], g.sq.t[:, 0, :], reads=g.cb.b + [g.sq.b[0]], writes=[psb[3]])
        inv_rsqrt(g, g.rstd.t[:], g.rstd.b, ps[3][:], [psb[3]], M.EPS)
        P.stt("dve", dst.t[:], cc.t[:], float(scale), g.rstd.t[:], ALU.mult, ALU.mult, reads=cc.b + g.rstd.b, writes=dst.b)

    wq, wqb = g.wload("ein", 4)
    wk, wkb = g.wload("ein", 5)
    wv, wvb = g.wload("ein", 6)
    wz, wzb = g.wload("ein", 7)
    for h in range(4):
        conv_tile(h, wq, wqb, h * 128)
        l2n(qT, 128 ** -0.5)
        conv_tile(4 + h, wk, wkb, h * 128)
        l2n(kT, 1.0)
        conv_tile(8 + h, wv, wvb, h * 128)
        P.copy("act", wT.t[:], cc.t[:], reads=cc.b, writes=wT.b)
        if st <= 1:
            zy(h)
            continue
        for half in range(2):
            pb, pbb = ps[half], psb[half]
            for n4 in range(4):
                n = half * 4 + n4
                P.mm(pb[0:64, n4 * 128:(n4 + 1) * 128], kT.t[:, n * 64:(n + 1) * 64], identb, reads=kT.b + g.cb.b, writes=[pbb])
            pv = pb[0:64, :].rearrange("p (n d) -> p n d", n=4)
            P.tt("dve", kbe.t[:, half * 4:half * 4 + 4, :], pv, sc_kbe.t[:, half * 4:half * 4 + 4, h:h + 1].broadcast_to([64, 4, 128]), ALU.mult,
                 reads=[pbb] + sc_kbe.b, writes=kbe.b)
            P.tt("dve", kdc.t[:, half * 4:half * 4 + 4, :], pv, sc_kd.t[:, half * 4:half * 4 + 4, h:h + 1].broadcast_to([64, 4, 128]), ALU.mult,
                 reads=[pbb] + sc_kd.b, writes=kdc.b)
        for half in range(2):
            pb, pbb = ps[2 + half], psb[2 + half]
            for n4 in range(4):
                n = half * 4 + n4
                P.mm(pb[0:64, n4 * 128:(n4 + 1) * 128], wT.t[:, n * 64:(n + 1) * 64], identb, reads=wT.b + g.cb.b, writes=[pbb])
            pv = pb[0:64, :].rearrange("p (n d) -> p n d", n=4)
            P.tt("dve", vb.t[:, half * 4:half * 4 + 4, :], pv, beta.t[:, half * 4:half * 4 + 4, h:h + 1].broadcast_to([64, 4, 128]), ALU.mult,
                 reads=[pbb] + beta.b, writes=vb.b)
        if st <= 2:
            zy(h)
            continue
        P.tt("dve", G.t[:], cF("m_strict_T", 64).rearrange("p (n c) -> p n c", c=C), gg.t[:, :, h:h + 1].broadcast_to([64, NCH, C]), ALU.mult,
             reads=g.cf.b + gg.b, writes=G.b)
        P.tt("dve", grep.t[:], cF("ones_1", 64).unsqueeze(1).broadcast_to([64, NCH, 128]), gg.t[:, :, h:h + 1].broadcast_to([64, NCH, 128]), ALU.mult,
             reads=g.cf.b + gg.b, writes=grep.b)
        for n in range(NCH):
            P.mm(ps[4][:, n * 64:(n + 1) * 64], grep.t[:, n, :], cF("m_incl", 64, 0, 64), reads=grep.b + g.cf.b, writes=[psb[4]])
        P.act(EB.t[:], ps[4][:], AF.Exp, reads=[psb[4]], writes=EB.b)
        P.tt("dve", qeT.t[:], qT.t[:], EB.t[:], ALU.mult, reads=qT.b + EB.b, writes=qeT.b)
        if st <= 3:
            zy(h)
            continue
        for n in range(NCH):
            sl = slice(n * 64, (n + 1) * 64)
            P.mm(ps[5][0:64, sl], cF("m_incl", 64, 0, 64), G.t[:, n, :], reads=g.cf.b + G.b, writes=[psb[5]])
            P.mm(ps[6][0:64, sl], G.t[:, n, :], cF("m_incl", 64, 0, 64), reads=g.cf.b + G.b, writes=[psb[6]])
            P.mm(ps[0][0:64, sl], kT.t[:, sl], kT.t[:, sl], reads=kT.b, writes=[psb[0]])
            P.mm(ps[1][0:64, sl], kT.t[:, sl], qT.t[:, sl], reads=kT.b + qT.b, writes=[psb[1]])
        P.act(E.t[:], ps[5][0:64, :], AF.Exp, reads=[psb[5]], writes=E.b)
        P.tt("pool", E.t[:], E.t[:], cF("m_strict_T", 64), ALU.mult, reads=E.b + g.cf.b, writes=E.b)
        P.stt("dve", YTa[0].t[:], ps[0][0:64, :], -1.0, E.t[:], ALU.mult, ALU.mult, reads=[psb[0]] + E.b, writes=YTa[0].b)
        yt3 = YTa[0].t[:].rearrange("p (n c) -> p n c", c=C)
        P.tt("dve", yt3, yt3, beta.t[:, :, h:h + 1].broadcast_to([64, NCH, C]), ALU.mult, reads=YTa[0].b + beta.b, writes=YTa[0].b)
        P.act(E.t[:], ps[6][0:64, :], AF.Exp, reads=[psb[6]], writes=E.b)
        P.tt("pool", E.t[:], E.t[:], cF("m_incl", 64), ALU.mult, reads=E.b + g.cf.b, writes=E.b)
        P.tt("dve", att.t[:], ps[1][0:64, :], E.t[:], ALU.mult, reads=[psb[1]] + E.b, writes=att.b)
        if st <= 4:
            zy(h)
            continue
        for n in range(NCH):
            sl = slice(n * 64, (n + 1) * 64)
            P.mm(ps[5][0:64, sl], YTa[0].t[:, sl], identf, reads=YTa[0].b + g.cf.b, writes=[psb[5]])
        nsub = g.flags.get("nsub", 9)
        extra = []
        if g.flags.get("dummy"):
            P.mm(ps[4][0:64, 0:64], YTa[0].t[:, 0:64], identf, reads=YTa[0].b + g.cf.b, writes=[psb[4]])
            extra = [psb[4]]
        if nsub >= 2:
            ydst = {"Ya": Ya[0], "R": R, "E": E}[g.flags.get("ydst", "Ya")]
            if g.flags.get("yeng", "dve") == "actf":
                P.act(ydst.t[:], ps[5][0:64, :], AF.Identity, reads=[psb[5]], writes=ydst.b)
            else:
                P.copy(g.flags.get("yeng", "dve"), ydst.t[:], ps[5][0:64, :], reads=[psb[5]] + extra, writes=ydst.b)
        if nsub >= 3:
            P.tt("dve", R.t[:], ps[5][0:64, :], cF("ident64x", 64), ALU.add, reads=[psb[5]] + g.cf.b, writes=R.b)
        cur = 0
        for lvl in range(1, 1 + g.flags.get("nlev", 5)):
            nxt = 1 - cur
            for n in range(NCH):
                sl = slice(n * 64, (n + 1) * 64)
                P.mm(ps[0][0:64, sl], YTa[cur].t[:, sl], Ya[cur].t[:, sl], reads=YTa[cur].b + Ya[cur].b, writes=[psb[0]])
                P.mm(ps[1][0:64, sl], Ya[cur].t[:, sl], YTa[cur].t[:, sl], reads=YTa[cur].b + Ya[cur].b, writes=[psb[1]])
            lsub = g.flags.get("lsub", 9)
            if lvl < 5 and lsub >= 2:
                P.copy(g.flags.get("yeng", "dve"), Ya[nxt].t[:], ps[0][0:64, :], reads=[psb[0]], writes=Ya[nxt].b)
            if lsub >= 2:
                P.copy("dve", YTa[nxt].t[:], ps[1][0:64, :], reads=[psb[1]], writes=YTa[nxt].b)
            if lsub >= 3:
                for n in range(NCH):
                    sl = slice(n * 64, (n + 1) * 64)
                    P.mm(ps[5][0:64, sl], YTa[nxt].t[:, sl], R.t[:, sl], reads=YTa[nxt].b + R.b, writes=[psb[5]])
            if lsub >= 4:
                P.tt("dve", R.t[:], R.t[:], ps[5][0:64, :], ALU.add, reads=R.b + [psb[5]], writes=R.b)
            cur = nxt
        if nsub >= 4:
            P.copy("act", Tb.t[:], R.t[:], reads=R.b, writes=Tb.b)
        if st <= 5:
            zy(h)
            continue
        for half in range(2):
            pb, pbb = ps[half], psb[half]
            for n4 in range(4):
                n = half * 4 + n4
                P.mm(pb[0:64, n4 * 128:(n4 + 1) * 128], Tb.t[:, n * 64:(n + 1) * 64], vb.t[:, n, :], reads=Tb.b + vb.b, writes=[pbb])
            P.copy("act" if half else "dve", u.t[:, half * 4:half * 4 + 4, :].rearrange("p n d -> p (n d)"), pb[0:64, :], reads=[pbb], writes=u.b)
        for n in range(NCH):
            sl = slice(n * 64, (n + 1) * 64)
            P.mm(ps[2][:, sl], kbe.t[:, n, :], Tb.t[:, sl], reads=kbe.b + Tb.b, writes=[psb[2]])
        P.copy("act", wT.t[:], ps[2][:], reads=[psb[2]], writes=wT.b)
        proj_fm(g, ps[3], psb[3], wz, wzb, h * 128, 128)
        P.act(sgz.t[:], ps[3][:], AF.Silu, reads=[psb[3]], writes=sgz.b)
        if st <= 6:
            zy(h)
            continue
        po, pob = ps[7], psb[7]
        for n in range(NCH):
            sl = slice(n * 64, (n + 1) * 64)
            P.mm(ps[4][0:64, 0:128], wT.t[:, sl], e.GSb.t[:, h, :], reads=wT.b + [e.GSb.b[h]], writes=[psb[4]])
            P.tt("dve", vnew.t[:], u.t[:, n, :], ps[4][0:64, 0:128], ALU.subtract, reads=u.b + [psb[4]], writes=vnew.b)
            P.mm(po[:, sl], e.GSb.t[:, h, :], qeT.t[:, sl], start=True, stop=False, reads=[e.GSb.b[h]] + qeT.b, writes=[pob])
            P.mm(po[:, sl], vnew.t[:], att.t[:, sl], start=False, stop=True, reads=vnew.b + att.b, writes=[pob])
            P.mm(ps[6][:, 0:128], kdc.t[:, n, :], vnew.t[:], reads=kdc.b + vnew.b, writes=[psb[6]])
            P.stt("dve", e.GS.t[:, h, :], e.GS.t[:, h, :], egl.t[:, n, h:h + 1], ps[6][:, 0:128], ALU.mult, ALU.add,
                  reads=[e.GS.b[h], psb[6]] + egl.b, writes=[e.GS.b[h]])
            P.copy("act", e.GSb.t[:, h, :], e.GS.t[:, h, :], reads=[e.GS.b[h]], writes=[e.GSb.b[h]])
        nc_, nb_ = g.colf("gdn_norm", 0)
        head_norm_gate(g, po, pob, nc_, nb_, sgz.t[:], sgz.b, e.ycat.t[:, 4 + h, :], [e.ycat.b[4 + h]], 3)


TT, C, NCH, NKC = 512, 64, 8, 8
RWKV_GN_EPS = 64e-5


def setup_odd(g):
    A, P, I = g.A, g.P, g.I
    Tl = M.Tl
    o = M.Ctx()
    g.o = o
    o.w2 = Tl(A, "rw_w2", [64, 512], F32)
    o.a2 = Tl(A, "rw_a2", [64, 512], F32)
    o.g2 = M.Ctx(); o.g2.t = g.f.t[:, 2, :]; o.g2.b = [g.f.b[2]]
    P.dma("sp", o.w2.t[:], I["rwkv_w2"][0], "c0", writes=o.w2.b)
    P.dma("sp", o.a2.t[:], I["rwkv_a2"][0], "c0", writes=o.a2.b)
    P.dma("sp", o.g2.t[:], I["rwkv_g2"][0], "c0", writes=o.g2.b)
    o.g2b = Tl(A, "rw_g2b", [128, 512], BF16)
    P.copy("dve", o.g2b.t[:], o.g2.t[:], reads=o.g2.b, writes=o.g2b.b)
    o.wa = M.Ctx(); o.wa.t = g.f.t[:, 0, :].rearrange("p (k c) -> p k c", k=4); o.wa.b = [g.f.b[0]]
    o.wx = M.Ctx(); o.wx.t = g.f.t[:, 1, :].rearrange("p (k c) -> p k c", k=4); o.wx.b = [g.f.b[1]]
    P.memset("pool", o.wa.t[:], 0.0, writes=o.wa.b)
    P.memset("pool", o.wx.t[:], 0.0, writes=o.wx.b)
    for k in range(4):
        for a in range(2):
            P.dma("sp", o.wa.t[a * 64:(a + 1) * 64, k, a * 64:(a + 1) * 64], I["lru_wa"][0, 2 * k + a], "c0", writes=o.wa.b)
            P.dma("sp", o.wx.t[a * 64:(a + 1) * 64, k, a * 64:(a + 1) * 64], I["lru_wx"][0, 2 * k + a], "c0", writes=o.wx.b)
    o.wab = Tl(A, "lru_wab", [128, 4, 128], BF16)
    o.wxb = Tl(A, "lru_wxb", [128, 4, 128], BF16)
    P.copy("dve", o.wab.t[:], o.wa.t[:], reads=o.wa.b, writes=o.wab.b)
    P.copy("dve", o.wxb.t[:], o.wx.t[:], reads=o.wx.b, writes=o.wxb.b)
    o.dc = Tl(A, "odd_cols", [128, 16], F32)
    for k in range(4):
        lc, lb = g.colf("lru_lambda", k)
        P.act(o.dc.t[:, k:k + 1], lc, AF.Exp, scale=-1.0, reads=lb, writes=o.dc.b)
    P.act(o.dc.t[:, 0:4], o.dc.t[:, 0:4], AF.Ln, bias=1.0, reads=o.dc.b, writes=o.dc.b)
    P.ts("dve", o.dc.t[:, 0:4], o.dc.t[:, 0:4], -8.0, None, ALU.mult, reads=o.dc.b, writes=o.dc.b)
    for hp in range(4):
        wc, wb = g.colf("rwkv_w0", hp)
        P.ts("dve", o.dc.t[:, 4 + hp:5 + hp], wc, -1.0, None, ALU.mult, reads=wb, writes=o.dc.b)
    o.A = Tl(A, "rw_A", [128, 4, 64], F32, nb=4)
    o.Ab = Tl(A, "rw_Ab", [128, 4, 64], BF16, nb=4)
    P.memset("pool", o.A.t[:], 0.0, writes=o.A.b)
    P.memset("pool", o.Ab.t[:], 0.0, writes=o.Ab.b)
    o.sh = Tl(A, "rw_sh", [128, 28], F32)
    P.memset("pool", o.sh.t[:], 0.0, writes=o.sh.b)
    o.lh = Tl(A, "lru_hist", [128, 4, 3], F32)
    P.memset("pool", o.lh.t[:], 0.0, writes=o.lh.b)
    o.hs = Tl(A, "lru_state", [128, 4], F32)
    P.memset("pool", o.hs.t[:], 0.0, writes=o.hs.b)
    o.yr = Tl(A, "yr", [128, 4, TT], BF16, nb=4)
    o.yl = g.e.ycat if hasattr(g, "e") and hasattr(g.e, "ycat") else Tl(A, "ycat_o", [128, 8, TT], BF16, nb=8)


def setup_odd_w(g):
    I = g.I
    g.prep_matrix("oin", I["odd_w_in"][0], [(0, 512), (512, 1024), (1024, 1536), (1536, 1792), (1792, 2304), (2304, 2816)])
    g.prep_matrix("oout_r", I["odd_w_out"][0][0:512, :], [(0, 512), (512, 1024)])
    g.prep_matrix("oout_l", I["odd_w_out"][0][512:1024, :], [(0, 512), (512, 1024)])


def lru(g, tt):
    P, o = g.P, g.o
    ps, psb = g.ps, g.psb
    cv = g.carve
    raw = cv("lraw", [128, TT + 3], F32)
    xb = cv("lxb", [128, TT], F32)
    xbb = cv("lxbb", [128, TT], BF16)
    gr = cv("lgr", [128, TT], F32)
    gi = cv("lgi", [128, TT], F32)
    aa = cv("laa", [128, TT], F32)
    hh_ = cv("lhh", [128, TT], F32)
    gl = cv("lgl", [128, TT], F32)
    wx_, wxb_ = g.wload("oin", 4)
    wy_, wyb_ = g.wload("oin", 5)
    for k in range(4):
        proj_fm(g, ps[0], psb[0], wx_, wxb_, k * 128, 128)
        P.copy("dve", raw.t[:, 0:3], o.lh.t[:, k, :], reads=o.lh.b, writes=raw.b)
        P.copy("act", raw.t[:, 3:TT + 3], ps[0][:], reads=[psb[0]], writes=raw.b)
        P.copy("dve", o.lh.t[:, k, :], raw.t[:, TT:TT + 3], reads=raw.b, writes=o.lh.b)
        c0, c0b = g.colf("lru_conv_w", 0 * 4 + k)
        bc, bcb = g.colf("lru_conv_b", k)
        P.ts("dve", xb.t[:], raw.t[:, 0:TT], c0, bc, ALU.mult, ALU.add, reads=raw.b + c0b + bcb, writes=xb.b)
        for j in range(1, 4):
            cj, cjb = g.colf("lru_conv_w", j * 4 + k)
            P.stt("dve", xb.t[:], raw.t[:, j:j + TT], cj, xb.t[:], ALU.mult, ALU.add, reads=raw.b + cjb + xb.b, writes=xb.b)
        P.copy("act", xbb.t[:], xb.t[:], reads=xb.b, writes=xbb.b)
        P.mm(ps[1][:], o.wab.t[:, k, :], xbb.t[:], reads=o.wab.b + xbb.b, writes=[psb[1]])
        P.mm(ps[2][:], o.wxb.t[:, k, :], xbb.t[:], reads=o.wxb.b + xbb.b, writes=[psb[2]])
        ba, bab = g.colf("lru_ba", k)
        bx, bxb = g.colf("lru_bx", k)
        P.act(gr.t[:], ps[1][:], AF.Sigmoid, bias=ba, reads=[psb[1]] + bab, writes=gr.b)
        P.act(gi.t[:], ps[2][:], AF.Sigmoid, bias=bx, reads=[psb[2]] + bxb, writes=gi.b)
        P.act(aa.t[:], gr.t[:], AF.Exp, scale=o.dc.t[:, k:k + 1], reads=gr.b + o.dc.b, writes=aa.b)
        P.tt("pool", gr.t[:], aa.t[:], aa.t[:], ALU.mult, reads=aa.b, writes=gr.b)
        P.act(gr.t[:], gr.t[:], AF.Ln, scale=-1.0, bias=1.0, reads=gr.b, writes=gr.b)
        P.act(gr.t[:], gr.t[:], AF.Exp, scale=0.5, reads=gr.b, writes=gr.b)
        P.tt("dve", gi.t[:], gi.t[:], gr.t[:], ALU.mult, reads=gi.b + gr.b, writes=gi.b)
        P.tt("dve", gi.t[:], gi.t[:], xb.t[:], ALU.mult, reads=gi.b + xb.b, writes=gi.b)
        P.op("dve", lambda e_, k=k: e_.tensor_tensor_scan(out=hh_.t[:], data0=aa.t[:], data1=gi.t[:], initial=o.hs.t[:, k:k + 1],
                                                          op0=ALU.mult, op1=ALU.add), reads=aa.b + gi.b + o.hs.b, writes=hh_.b)
        P.copy("dve", o.hs.t[:, k:k + 1], hh_.t[:, TT - 1:TT], reads=hh_.b, writes=o.hs.b)
        proj_fm(g, ps[3], psb[3], wy_, wyb_, k * 128, 128)
        P.act(gl.t[:], ps[3][:], AF.Square, reads=[psb[3]], writes=gl.b)
        P.ts("dve", gl.t[:], gl.t[:], 0.044715, 1.0, ALU.mult, ALU.add, reads=gl.b, writes=gl.b)
        P.tt("dve", gl.t[:], gl.t[:], ps[3][:], ALU.mult, reads=gl.b + [psb[3]], writes=gl.b)
        P.act(gl.t[:], gl.t[:], AF.Tanh, scale=0.7978845608028654, reads=gl.b, writes=gl.b)
        P.ts("dve", gl.t[:], gl.t[:], 1.0, 0.5, ALU.add, ALU.mult, reads=gl.b, writes=gl.b)
        P.tt("dve", gl.t[:], gl.t[:], ps[3][:], ALU.mult, reads=gl.b + [psb[3]], writes=gl.b)
        P.tt("dve", o.yl.t[:, 4 + k, :], hh_.t[:], gl.t[:], ALU.mult, reads=hh_.b + gl.b, writes=[o.yl.b[4 + k]])


def rwkv(g, tt):
    P, o = g.P, g.o
    ps, psb = g.ps, g.psb
    cF = g.cF
    cv = g.carve
    ib = M.CONST_COLS["ident"][0]
    o1 = M.CONST_COLS["bd_1"][0]
    om = M.CONST_COLS["bd_64m"][0]
    bd1b = g.cb.t[:, o1:o1 + 128]
    bd64m = g.cb.t[:, om:om + 128]
    F = lambda nm: cv(nm, [128, TT], F32)
    H = lambda nm: cv(nm, [128, TT], BF16)
    raw = cv("rraw", [128, TT + 1], F32)
    dd = F("rdd"); twl = cv("twl", [64, TT], F32); al = cv("ral", [64, TT], F32); sgl = H("sgl")
    rr = F("rr"); kk_ = F("rk"); vv = F("rv"); vb16 = H("rvb")
    ew = F("rew"); cum = F("rcum"); ex = F("rex"); aa = F("raa"); kn = F("rkn"); km = F("rkm"); t1 = F("rt1")
    bon = F("rbon"); gg_ = F("rgg")
    rt = H("rrt"); bt = H("rbt"); at = H("rat"); kt = H("rkt"); adT = H("radT"); kdT = H("rkdT")
    pc = cv("rpc", [128, NCH], F32)
    vtok = cv("rvtok", [128, NCH, 64], BF16)
    adtok = cv("radtok", [128, NCH, 64], BF16)
    kdtok = cv("rkdtok", [128, NCH, 64], BF16)
    Ya = [F("rYa%d" % i) for i in range(2)]
    YTa = [F("rYTa%d" % i) for i in range(2)]
    R = F("rR")
    Tb = H("rTb"); MKT = H("rMKT"); NAT = H("rNAT"); NKT = H("rNKT")
    zsb = cv("rz", [128, 64], BF16)
    usb = cv("ru", [128, 64], BF16)
    ysb = F("rysb")
    HB = (slice(0, 64), slice(64, 128))

    def shift_mix(dst, dstb, src_ps, srcb, npart, tile_id, mucol, mub):
        P.copy("dve", raw.t[0:npart, 0:1], o.sh.t[0:npart, tile_id:tile_id + 1], reads=o.sh.b, writes=raw.b)
        P.copy("act", raw.t[0:npart, 1:TT + 1], src_ps, reads=srcb, writes=raw.b)
        P.copy("dve", o.sh.t[0:npart, tile_id:tile_id + 1], raw.t[0:npart, TT:TT + 1], reads=raw.b, writes=o.sh.b)
        P.tt("dve", dd.t[0:npart, :], raw.t[0:npart, 0:TT], raw.t[0:npart, 1:TT + 1], ALU.subtract, reads=raw.b, writes=dd.b)
        P.stt("dve", dst, dd.t[0:npart, :], mucol, raw.t[0:npart, 1:TT + 1], ALU.mult, ALU.add, reads=dd.b + raw.b + mub, writes=dstb)

    def both(fn):
        fn(HB[0])
        P.pe_fence()
        fn(HB[1])
        P.pe_fence()

    wm, wmb = g.wload("oin", 3)
    proj_fm(g, ps[0], psb[0], wm, wmb, 0, 64)
    mc, mb = g.c64("rwkv_mu", 24)
    shift_mix(twl.t[:], twl.b, ps[0][0:64, :], [psb[0]], 64, 24, mc, mb)
    P.act(twl.t[:], twl.t[:], AF.Tanh, reads=twl.b, writes=twl.b)
    proj_fm(g, ps[1], psb[1], wm, wmb, 64, 64)
    mc, mb = g.c64("rwkv_mu", 25)
    shift_mix(al.t[:], al.b, ps[1][0:64, :], [psb[1]], 64, 25, mc, mb)
    proj_fm(g, ps[2], psb[2], wm, wmb, 128, 128)
    mc, mb = g.colf("rwkv_mu", 13)
    shift_mix(dd.t[:], dd.b, ps[2][:], [psb[2]], 128, 26, mc, mb)
    P.act(sgl.t[:], dd.t[:], AF.Sigmoid, reads=dd.b, writes=sgl.b)

    wr, wrb = g.wload("oin", 0)
    wk, wkb = g.wload("oin", 1)
    wv, wvb = g.wload("oin", 2)
    for hp in range(4):
        proj_fm(g, ps[0], psb[0], wr, wrb, hp * 128, 128)
        mc, mb = g.colf("rwkv_mu", hp)
        shift_mix(rr.t[:], rr.b, ps[0][:], [psb[0]], 128, hp, mc, mb)
        proj_fm(g, ps[1], psb[1], wk, wkb, hp * 128, 128)
        mc, mb = g.colf("rwkv_mu", 4 + hp)
        shift_mix(kk_.t[:], kk_.b, ps[1][:], [psb[1]], 128, 4 + hp, mc, mb)
        proj_fm(g, ps[2], psb[2], wv, wvb, hp * 128, 128)
        mc, mb = g.colf("rwkv_mu", 8 + hp)
        shift_mix(vv.t[:], vv.b, ps[2][:], [psb[2]], 128, 8 + hp, mc, mb)
        P.copy("act", vb16.t[:], vv.t[:], reads=vv.b, writes=vb16.b)
        P.mm(ps[3][:], o.w2.t[:, hp * 128:(hp + 1) * 128], twl.t[:], reads=o.w2.b + twl.b, writes=[psb[3]])
        P.act(ew.t[:], ps[3][:], AF.Exp, scale=-1.0, bias=o.dc.t[:, 4 + hp:5 + hp], reads=[psb[3]] + o.dc.b, writes=ew.b)
        P.act(ew.t[:], ew.t[:], AF.Ln, bias=1.0, reads=ew.b, writes=ew.b)
        P.act(ew.t[:], ew.t[:], AF.Exp, scale=-1.0, bias=-0.5, reads=ew.b, writes=ew.b)
        P.op("dve", lambda e_: e_.tensor_tensor_scan(out=cum.t[:], data0=cF("reset"), data1=ew.t[:], initial=0.0,
                                                     op0=ALU.mult, op1=ALU.add), reads=ew.b + g.cf.b, writes=cum.b)
        P.mm(ps[4][:], o.a2.t[:, hp * 128:(hp + 1) * 128], al.t[:], reads=o.a2.b + al.b, writes=[psb[4]])
        a0c, a0b = g.colf("rwkv_a0", hp)
        P.act(aa.t[:], ps[4][:], AF.Sigmoid, bias=a0c, reads=[psb[4]] + a0b, writes=aa.b)
        P.mm(ps[5][:], o.g2b.t[:, hp * 128:(hp + 1) * 128], sgl.t[:], reads=o.g2b.b + sgl.b, writes=[psb[5]])
        P.copy("act", gg_.t[:], ps[5][:], reads=[psb[5]], writes=gg_.b)
        kkc, kkb = g.colf("rwkv_k_k", hp)
        P.ts("pool", kn.t[:], kk_.t[:], kkc, None, ALU.mult, reads=kk_.b + kkb, writes=kn.b)
        P.act(g.sq.t[:, 0, :], kn.t[:], AF.Square, reads=kn.b, writes=[g.sq.b[0]])
        P.mm(ps[6][:], bd1b, g.sq.t[:, 0, :], reads=g.cb.b + [g.sq.b[0]], writes=[psb[6]])
        inv_rsqrt(g, t1.t[:], t1.b, ps[6][:], [psb[6]], M.EPS)
        P.tt("dve", kn.t[:], kn.t[:], t1.t[:], ALU.mult, reads=kn.b + t1.b, writes=kn.b)
        kac, kab = g.colf("rwkv_k_a", hp)
        P.ts("dve", km.t[:], aa.t[:], -1.0, kac, ALU.add, ALU.mult, reads=aa.b + kab, writes=km.b)
        P.stt("dve", km.t[:], km.t[:], 1.0, kk_.t[:], ALU.add, ALU.mult, reads=km.b + kk_.b, writes=km.b)
        rkc, rkb = g.colf("rwkv_r_k", hp)
        P.stt("dve", t1.t[:], rr.t[:], rkc, km.t[:], ALU.mult, ALU.mult, reads=rr.b + rkb + km.b, writes=t1.b)
        P.copy("act", g.sq.t[:, 1, :], t1.t[:], reads=t1.b, writes=[g.sq.b[1]])
        P.mm(ps[7][:], bd1b, g.sq.t[:, 1, :], reads=g.cb.b + [g.sq.b[1]], writes=[psb[7]])
        P.tt("dve", bon.t[:], ps[7][:], vv.t[:], ALU.mult, reads=[psb[7]] + vv.b, writes=bon.b)
        P.stt("dve", t1.t[:], kn.t[:], -1.0, aa.t[:], ALU.mult, ALU.mult, reads=kn.b + aa.b, writes=t1.b)
        P.act(ex.t[:], cum.t[:], AF.Exp, scale=-1.0, reads=cum.b, writes=ex.b)
        P.tt("pool", rt.t[:], rr.t[:], ex.t[:], ALU.mult, reads=rr.b + ex.b, writes=rt.b)
        cum3 = cum.t[:].rearrange("p (n c) -> p n c", c=C)
        P.act(pc.t[:], cum3[:, :, C - 1], AF.Exp, scale=-1.0, reads=cum.b, writes=pc.b)
        P.tt("dve", ex.t[:], cum.t[:], ew.t[:], ALU.subtract, reads=cum.b + ew.b + rt.b, writes=ex.b)
        P.act(ex.t[:], ex.t[:], AF.Exp, scale=-1.0, reads=ex.b, writes=ex.b)
        P.tt("dve", bt.t[:], kn.t[:], ex.t[:], ALU.mult, reads=kn.b + ex.b, writes=bt.b)
        P.act(ex.t[:], cum.t[:], AF.Exp, reads=cum.b, writes=ex.b)
        P.tt("dve", at.t[:], t1.t[:], ex.t[:], ALU.mult, reads=t1.b + ex.b, writes=at.b)
        P.tt("pool", kt.t[:], km.t[:], ex.t[:], ALU.mult, reads=km.b + ex.b, writes=kt.b)
        ex3 = ex.t[:].rearrange("p (n c) -> p n c", c=C)
        P.tt("dve", ex3, cum3[:, :, C - 1:C].broadcast_to([128, NCH, C]), cum3, ALU.subtract, reads=cum.b + kt.b, writes=ex.b)
        P.act(ex.t[:], ex.t[:], AF.Exp, scale=-1.0, reads=ex.b, writes=ex.b)
        P.tt("dve", adT.t[:], t1.t[:], ex.t[:], ALU.mult, reads=t1.b + ex.b, writes=adT.b)
        P.tt("pool", kdT.t[:], km.t[:], ex.t[:], ALU.mult, reads=km.b + ex.b, writes=kdT.b)
        for src, dst, bank in ((vb16, vtok, 0), (adT, adtok, 1), (kdT, kdtok, 2)):
            def f_(p, src=src, bank=bank):
                idb = g.cb.t[p, ib + p.start:ib + p.start + 64]
                for n in range(NCH):
                    sl = slice(n * 64, (n + 1) * 64)
                    P.mm(ps[bank][p, sl], src.t[p, sl], idb, reads=src.b + g.cb.b, writes=[psb[bank]])
            both(f_)
            P.copy("act" if bank == 1 else "dve", dst.t[:].rearrange("p n c -> p (n c)"), ps[bank][:], reads=[psb[bank]], writes=dst.b)
        def f_(p):
            for n in range(NCH):
                sl = slice(n * 64, (n + 1) * 64)
                P.mm(ps[3][p, sl], at.t[p, sl], bt.t[p, sl], reads=at.b + bt.b, writes=[psb[3]])
                P.mm(ps[4][p, sl], bt.t[p, sl], at.t[p, sl], reads=at.b + bt.b, writes=[psb[4]])
                P.mm(ps[5][p, sl], kt.t[p, sl], bt.t[p, sl], reads=kt.b + bt.b, writes=[psb[5]])
                P.mm(ps[6][p, sl], at.t[p, sl], rt.t[p, sl], reads=at.b + rt.b, writes=[psb[6]])
                P.mm(ps[7][p, sl], kt.t[p, sl], rt.t[p, sl], reads=kt.b + rt.b, writes=[psb[7]])
        both(f_)
        P.tt("dve", Ya[0].t[:], ps[3][:], cF("m_strict"), ALU.mult, reads=[psb[3]] + g.cf.b, writes=Ya[0].b)
        P.tt("dve", YTa[0].t[:], ps[4][:], cF("m_strict_T"), ALU.mult, reads=[psb[4]] + g.cf.b, writes=YTa[0].b)
        P.tt("dve", MKT.t[:], ps[5][:], cF("m_strict"), ALU.mult, reads=[psb[5]] + g.cf.b, writes=MKT.b)
        P.tt("dve", NAT.t[:], ps[6][:], cF("m_incl"), ALU.mult, reads=[psb[6]] + g.cf.b, writes=NAT.b)
        P.tt("dve", NKT.t[:], ps[7][:], cF("m_incl"), ALU.mult, reads=[psb[7]] + g.cf.b, writes=NKT.b)
        P.tt("pool", R.t[:], Ya[0].t[:], cF("ident64x"), ALU.add, reads=Ya[0].b + g.cf.b, writes=R.b)
        cur = 0
        for lvl in range(1, 6):
            nxt = 1 - cur

            def f_(p, cur=cur):
                for n in range(NCH):
                    sl = slice(n * 64, (n + 1) * 64)
                    P.mm(ps[0][p, sl], YTa[cur].t[p, sl], Ya[cur].t[p, sl], reads=YTa[cur].b + Ya[cur].b, writes=[psb[0]])
                    P.mm(ps[1][p, sl], Ya[cur].t[p, sl], YTa[cur].t[p, sl], reads=YTa[cur].b + Ya[cur].b, writes=[psb[1]])
            both(f_)
            if lvl < 5:
                P.copy("act", Ya[nxt].t[:], ps[0][:], reads=[psb[0]], writes=Ya[nxt].b)
            P.copy("dve", YTa[nxt].t[:], ps[1][:], reads=[psb[1]], writes=YTa[nxt].b)

            def f_(p, nxt=nxt):
                for n in range(NCH):
                    sl = slice(n * 64, (n + 1) * 64)
                    P.mm(ps[2][p, sl], YTa[nxt].t[p, sl], R.t[p, sl], reads=YTa[nxt].b + R.b, writes=[psb[2]])
            both(f_)
            P.tt("dve", R.t[:], R.t[:], ps[2][:], ALU.add, reads=R.b + [psb[2]], writes=R.b)
            cur = nxt
        P.copy("act", Tb.t[:], R.t[:], reads=R.b, writes=Tb.b)
        po, pob = ps[7], psb[7]
        Ab_b = [o.Ab.b[hp]]
        for n in range(NCH):
            sl = slice(n * 64, (n + 1) * 64)

            def f_(p):
                P.mm(ps[3][p, 0:64], bt.t[p, sl], o.Ab.t[p, hp, :], start=True, stop=False, reads=bt.b + Ab_b, writes=[psb[3]])
                P.mm(ps[3][p, 0:64], MKT.t[p, sl], vtok.t[p, n, :], start=False, stop=True, reads=MKT.b + vtok.b, writes=[psb[3]])
            both(f_)
            P.copy("act", zsb.t[:], ps[3][:, 0:64], reads=[psb[3]], writes=zsb.b)

            def f_(p):
                P.mm(ps[4][p, 0:64], Tb.t[p, sl], zsb.t[p, :], reads=Tb.b + zsb.b, writes=[psb[4]])
            both(f_)
            P.copy("dve", usb.t[:], ps[4][:, 0:64], reads=[psb[4]], writes=usb.b)

            def f_(p):
                P.mm(po[p, sl], o.Ab.t[p, hp, :], rt.t[p, sl], start=True, stop=False, reads=Ab_b + rt.b, writes=[pob])
                P.mm(po[p, sl], usb.t[p, :], NAT.t[p, sl], start=False, stop=False, reads=usb.b + NAT.b, writes=[pob])
                P.mm(po[p, sl], vtok.t[p, n, :], NKT.t[p, sl], start=False, stop=True, reads=vtok.b + NKT.b, writes=[pob])
                P.mm(ps[5][p, 0:64], adtok.t[p, n, :], usb.t[p, :], start=True, stop=False, reads=adtok.b + usb.b, writes=[psb[5]])
                P.mm(ps[5][p, 0:64], kdtok.t[p, n, :], vtok.t[p, n, :], start=False, stop=True, reads=kdtok.b + vtok.b, writes=[psb[5]])
            both(f_)
            P.stt("dve", o.A.t[:, hp, :], o.A.t[:, hp, :], pc.t[:, n:n + 1], ps[5][:, 0:64], ALU.mult, ALU.add,
                  reads=[o.A.b[hp], psb[5]] + pc.b, writes=[o.A.b[hp]])
            P.copy("act", o.Ab.t[:, hp, :], o.A.t[:, hp, :], reads=[o.A.b[hp]], writes=Ab_b)
        P.copy("act", ysb.t[:], po[:], reads=[pob], writes=ysb.b)
        P.copy("act", g.sq.t[:, 0, :], ysb.t[:], reads=ysb.b, writes=[g.sq.b[0]])
        P.mm(ps[6][:], bd64m, g.sq.t[:, 0, :], reads=g.cb.b + [g.sq.b[0]], writes=[psb[6]])
        P.tt("dve", ysb.t[:], ysb.t[:], ps[6][:], ALU.subtract, reads=ysb.b + [psb[6]], writes=ysb.b)
        P.act(g.sq.t[:, 1, :], ysb.t[:], AF.Square, reads=ysb.b, writes=[g.sq.b[1]])
        P.mm(ps[6][:], bd64m, g.sq.t[:, 1, :], reads=g.cb.b + [g.sq.b[1]], writes=[psb[6]])
        inv_rsqrt(g, t1.t[:], t1.b, ps[6][:], [psb[6]], RWKV_GN_EPS)
        lwc, lwb = g.colf("rwkv_ln_w", hp)
        lbc, lbb = g.colf("rwkv_ln_b", hp)
        P.stt("dve", ysb.t[:], ysb.t[:], lwc, t1.t[:], ALU.mult, ALU.mult, reads=ysb.b + lwb + t1.b, writes=ysb.b)
        P.stt("dve", ysb.t[:], ysb.t[:], lbc, bon.t[:], ALU.add, ALU.add, reads=ysb.b + lbb + bon.b, writes=ysb.b)
        P.tt("dve", o.yr.t[:, hp, :], ysb.t[:], gg_.t[:], ALU.mult, reads=ysb.b + gg_.b, writes=[o.yr.b[hp]])


def odd_mixer(g, tt):
    P, o = g.P, g.o
    g.phase()
    g.rmsnorm(g.xT, 1, 2, "pre")
    if g.flags.get("lru", True):
        lru(g, tt)
    else:
        for k in range(4):
            P.memset("pool", o.yl.t[:, 4 + k, :], 0.0, writes=[o.yl.b[4 + k]])
    g.phase()
    if g.flags.get("rwkv", True):
        rwkv(g, tt)
    else:
        for hp in range(4):
            P.memset("pool", o.yr.t[:, hp, :], 0.0, writes=[o.yr.b[hp]])
    for half in range(2):
        wr, wrb = g.wload("oout_r", half)
        wl, wlb = g.wload("oout_l", half)
        for nn in range(4):
            n = half * 4 + nn
            pb, pbb = g.ps[n % 2], g.psb[n % 2]
            for hp in range(4):
                P.mm(pb[:], wr[:, hp, nn * 128:(nn + 1) * 128], o.yr.t[:, hp, :], start=(hp == 0), stop=False,
                     reads=wrb + [o.yr.b[hp]], writes=[pbb])
            for k in range(4):
                P.mm(pb[:], wl[:, k, nn * 128:(nn + 1) * 128], o.yl.t[:, 4 + k, :], start=False, stop=(k == 3),
                     reads=wlb + [o.yl.b[4 + k]], writes=[pbb])
            P.copy("act", g.f.t[:, n, :], pb[:], reads=[pbb], writes=[g.f.b[n]])
    g.rmsnorm(g.f, 1, 3, "post", coef=1.0)


_PARAM_NAMES = ["norm_w", "ffn_w_gate", "ffn_w_up", "ffn_w_down", "even_w_in", "even_w_out", "gla_lora_w2", "gla_lora_b",
                "gla_norm", "gdn_conv", "gdn_a_log", "gdn_dt_bias", "gdn_norm", "odd_w_in", "odd_w_out", "rwkv_mu", "rwkv_w0",
                "rwkv_w2", "rwkv_a0", "rwkv_a2", "rwkv_g2", "rwkv_k_k", "rwkv_k_a", "rwkv_r_k", "rwkv_ln_w", "rwkv_ln_b",
                "lru_conv_w", "lru_conv_b", "lru_wa", "lru_ba", "lru_wx", "lru_bx", "lru_lambda"]


def kernel(**inputs):
    x = np.ascontiguousarray(np.asarray(inputs["x"], dtype=np.float32))
    B, T, _ = x.shape
    nc, g = build(T, 2, {})
    consts = make_consts()
    params = {k: np.ascontiguousarray(np.asarray(inputs[k], dtype=np.float32)) for k in _PARAM_NAMES}
    n_cores = 8
    in_maps = []
    for c in range(n_cores):
        m = dict(params)
        m["x"] = np.ascontiguousarray(x[c % B])
        m["consts"] = consts
        in_maps.append(m)
    res = run_bass_kernel_spmd(nc, in_maps, core_ids=list(range(n_cores)))
    out = np.stack([np.asarray(res.results[b]["out"], dtype=np.float32) for b in range(B)], 0)
    return out
```
